# Optimizing a Trainium2 kernel written in Bass

```python
import math
import jax, jax.numpy as jnp
from jax import lax
import numpy as np

D_MODEL = 1024
BATCH = 8
SEQ = 2048
DEPTH = 4
DEC_BATCH = 32
DEC_SEQ = 8
PAST_LEN = 8192
PAGE_SIZE = 128

N_MIXERS = 4
D_FF = 4 * D_MODEL
EPS = 1e-6

N_MEM = 256
XA_HEADS = 4
XA_DIM = 64
XA_WIDTH = XA_HEADS * XA_DIM

GDN_DK = 128
GDN_DV = 128
GDN_HEADS = D_MODEL // GDN_DV
GDN_KEY = GDN_HEADS * GDN_DK
GDN_VAL = GDN_HEADS * GDN_DV
GDN_CONV_CH = 2 * GDN_KEY + GDN_VAL
CONV_W = 4
GDN_CHUNK = 64
GDN_IN = GDN_CONV_CH + GDN_VAL + 2 * GDN_HEADS

HG_DK = 128
HG_DV = 128
HG_HEADS = D_MODEL // HG_DK
HG_WIDTH = HG_HEADS * HG_DK
HG_CHUNK = 32
HG_IN = 4 * HG_WIDTH

FOX_DIM = 64
FOX_HEADS = D_MODEL // FOX_DIM
FOX_WIDTH = FOX_HEADS * FOX_DIM
Q_BLOCK = 128
FOX_IN = 4 * FOX_WIDTH + FOX_HEADS
FOX_FBIAS_LO = 2.0
FOX_FBIAS_HI = 9.0

CM_CHUNK = 128
CM_GROUPS = 8
CM_WIDTH = D_MODEL
CM_GDIM = CM_WIDTH // CM_GROUPS
CM_IN = 2 * CM_WIDTH

MIX_OUT = D_MODEL + XA_WIDTH
N_A = (DEPTH + N_MIXERS - 1) // N_MIXERS
N_B = (DEPTH - 1 + N_MIXERS - 1) // N_MIXERS
N_C = (DEPTH - 2 + N_MIXERS - 1) // N_MIXERS
N_D = (DEPTH - 3 + N_MIXERS - 1) // N_MIXERS

kernel_name = 'hybrid_gdn_hgrn2_fox_chunkmlp_decode_step'


def rmsnorm(x, g):
    xf = x.astype(jnp.float32)
    y = xf * lax.rsqrt(jnp.mean(xf * xf, -1, keepdims=True) + EPS)
    return (y * g.astype(jnp.float32)).astype(x.dtype)


def layernorm(x, g, b):
    xf = x.astype(jnp.float32)
    mu = jnp.mean(xf, -1, keepdims=True)
    xc = xf - mu
    y = xc * lax.rsqrt(jnp.mean(xc * xc, -1, keepdims=True) + EPS)
    return (y * g.astype(jnp.float32) + b.astype(jnp.float32)).astype(x.dtype)


def l2norm(x):
    xf = x.astype(jnp.float32)
    return xf * lax.rsqrt(jnp.sum(xf * xf, -1, keepdims=True) + EPS)


def _to_chunks(a, c):
    b, t = a.shape[:2]
    pad = (-t) % c
    a = jnp.pad(a, [(0, 0), (0, pad)] + [(0, 0)] * (a.ndim - 2))
    a = a.reshape((b, (t + pad) // c, c) + a.shape[2:])
    return jnp.moveaxis(jnp.moveaxis(a, 3, 2), 1, 0)


def _from_chunks(o, t):
    n, b, h, c, d = o.shape
    return jnp.moveaxis(o, 0, 1).transpose(0, 1, 3, 2, 4).reshape(b, n * c, h, d)[:, :t]


def gated_delta_rule(q, k, v, g, beta, s0):
    f32 = jnp.float32
    t = q.shape[1]
    c = GDN_CHUNK
    dv = v.shape[-1]
    q = q.astype(f32) * (q.shape[-1] ** -0.5)
    xs = tuple(_to_chunks(a.astype(f32), c) for a in (q, k, v, g, beta))
    incl = jnp.tril(jnp.ones((c, c), dtype=bool))
    strict = jnp.tril(jnp.ones((c, c), dtype=bool), -1)
    eye = jnp.eye(c, dtype=f32)

    def step(S, inp):
        qc, kc, vc, gc, bc = inp
        G = jnp.cumsum(gc, -1)
        decay = jnp.exp(jnp.where(incl, G[..., :, None] - G[..., None, :], -jnp.inf))
        a = jnp.where(strict, bc[..., :, None] * jnp.einsum('bhtd,bhsd->bhts', kc, kc) * decay, 0.0)
        rhs = jnp.concatenate([vc * bc[..., None], kc * (bc * jnp.exp(G))[..., None]], -1)
        sol = lax.linalg.triangular_solve(eye + a, rhs, left_side=True, lower=True)
        u = sol[..., :dv] - jnp.einsum('bhtk,bhkv->bhtv', sol[..., dv:], S)
        att = jnp.einsum('bhtd,bhsd->bhts', qc, kc) * decay
        o = jnp.einsum('bhtk,bhkv->bhtv', qc * jnp.exp(G)[..., None], S) + jnp.einsum('bhts,bhsv->bhtv', att, u)
        gl = G[..., -1:]
        S = S * jnp.exp(gl)[..., None] + jnp.einsum('bhtk,bhtv->bhkv', kc * jnp.exp(gl - G)[..., None], u)
        return S, o

    S, o = lax.scan(step, s0.astype(f32), xs)
    return _from_chunks(o, t), S


def gla_recurrence(q, k, v, logf, s0):
    f32 = jnp.float32
    t = q.shape[1]
    c = HG_CHUNK
    xs = tuple(_to_chunks(a.astype(f32), c) for a in (q, k, v, logf))
    incl = jnp.tril(jnp.ones((c, c), dtype=bool))

    def step(S, inp):
        qc, kc, vc, lc = inp
        Bc = jnp.cumsum(lc, 2)
        w = jnp.exp(jnp.where(incl[:, :, None], Bc[:, :, :, None, :] - Bc[:, :, None, :, :], -jnp.inf))
        att = jnp.einsum('bhtd,bhsd,bhtsd->bhts', qc, kc, w)
        o = jnp.einsum('bhtd,bhdv->bhtv', qc * jnp.exp(Bc), S) + jnp.einsum('bhts,bhsv->bhtv', att, vc)
        bl = Bc[:, :, -1:, :]
        S = S * jnp.exp(bl[:, :, 0, :, None]) + jnp.einsum('bhsd,bhsv->bhdv', kc * jnp.exp(bl - Bc), vc)
        return S, o

    S, o = lax.scan(step, s0.astype(f32), xs)
    return _from_chunks(o, t), S


def gdn_mixer(p, conv_buf, s0, conv_w, a_log, dt_bias, norm_w):
    f32 = jnp.float32
    b, t, _ = p.shape
    qkv, z, beta_raw, a_raw = jnp.split(p, [GDN_CONV_CH, GDN_CONV_CH + GDN_VAL, GDN_CONV_CH + GDN_VAL + GDN_HEADS], -1)
    xx = jnp.concatenate([conv_buf.astype(qkv.dtype), qkv], 1)
    conv = jax.nn.silu(sum(xx[:, i:i + t] * conv_w[i] for i in range(CONV_W)))
    q, k, v = jnp.split(conv, [GDN_KEY, 2 * GDN_KEY], -1)
    q = l2norm(q.reshape(b, t, GDN_HEADS, GDN_DK))
    k = l2norm(k.reshape(b, t, GDN_HEADS, GDN_DK))
    v = v.reshape(b, t, GDN_HEADS, GDN_DV)
    beta = jax.nn.sigmoid(beta_raw.astype(f32))
    g = -jnp.exp(a_log.astype(f32)) * jax.nn.softplus(a_raw.astype(f32) + dt_bias)
    o, s = gated_delta_rule(q, k, v, g, beta, s0)
    o = rmsnorm(o, norm_w) * jax.nn.silu(z.astype(f32).reshape(b, t, GDN_HEADS, GDN_DV))
    return o.reshape(b, t, GDN_VAL).astype(p.dtype), xx[:, t:], s


def hgrn2_mixer(p, s0, lb, norm_w):
    f32 = jnp.float32
    b, t, _ = p.shape
    q, f, i, g = jnp.split(p, 4, axis=-1)
    shp = (b, t, HG_HEADS, HG_DK)
    ff = f.astype(f32).reshape(shp)
    lbh = lb.reshape(HG_HEADS, HG_DK)
    logf = jnp.logaddexp(jnp.log(lbh), jnp.log1p(-lbh) + jax.nn.log_sigmoid(ff))
    k = (1.0 - lbh) * jax.nn.sigmoid(-ff)
    qf = jax.nn.silu(q.astype(f32)).reshape(shp) * HG_DK ** -0.5
    o, s = gla_recurrence(qf, k, i.reshape(b, t, HG_HEADS, HG_DV), logf, s0)
    o = rmsnorm(o, norm_w) * jax.nn.silu(g.astype(f32).reshape(b, t, HG_HEADS, HG_DV))
    return o.reshape(b, t, HG_WIDTH).astype(p.dtype), s


def fox_project(p, fbias, qn, kn):
    b, t, _ = p.shape
    q, k, v, og, fl = jnp.split(p, [FOX_WIDTH, 2 * FOX_WIDTH, 3 * FOX_WIDTH, 4 * FOX_WIDTH], -1)
    shp = (b, t, FOX_HEADS, FOX_DIM)
    q = rmsnorm(q.reshape(shp), qn)
    k = rmsnorm(k.reshape(shp), kn)
    logf = jax.nn.log_sigmoid(fl.astype(jnp.float32) + fbias)
    return q, k, v.reshape(shp), logf, jax.nn.sigmoid(og)


def fox_prompt(q, k, v, logf):
    f32 = jnp.float32
    b, t, h, d = q.shape
    nb = t // Q_BLOCK
    F = jnp.cumsum(logf, 1)
    Fk = F.transpose(0, 2, 1)
    pos = jnp.arange(t)
    qb = q.reshape(b, nb, Q_BLOCK, h, d).swapaxes(0, 1)
    fb = F.reshape(b, nb, Q_BLOCK, h).swapaxes(0, 1)
    pb = pos.reshape(nb, Q_BLOCK)

    def one(args):
        qi, fi, pi = args
        s = jnp.einsum('bqhd,bkhd->bhqk', qi, k).astype(f32) * FOX_DIM ** -0.5
        s = s + fi.transpose(0, 2, 1)[..., :, None] - Fk[:, :, None, :]
        s = jnp.where(pos[None, :] <= pi[:, None], s, -jnp.inf)
        pr = jax.nn.softmax(s, -1).astype(v.dtype)
        return jnp.einsum('bhqk,bkhd->bqhd', pr, v)

    o = lax.map(one, (qb, fb, pb))
    return o.swapaxes(0, 1).reshape(b, t, h, d)


def fox_sample(q, k, v, logf, k_past, v_past, logf_past):
    f32 = jnp.float32
    p = k_past.shape[1]
    t = q.shape[1]
    scale = FOX_DIM ** -0.5
    fn = jnp.cumsum(logf, 1).transpose(0, 2, 1)
    rc = lax.cumsum(logf_past.astype(f32), axis=1, reverse=True)
    rest = jnp.concatenate([rc[:, 1:], jnp.zeros_like(rc[:, :1])], 1).transpose(0, 2, 1)
    s_past = jnp.einsum('bqhd,bkhd->bhqk', q, k_past).astype(f32) * scale + fn[..., :, None] + rest[..., None, :]
    s_new = jnp.einsum('bqhd,bkhd->bhqk', q, k).astype(f32) * scale + fn[..., :, None] - fn[..., None, :]
    s_new = jnp.where(jnp.tril(jnp.ones((t, t), dtype=bool)), s_new, -jnp.inf)
    pr = jax.nn.softmax(jnp.concatenate([s_past, s_new], -1), -1).astype(v.dtype)
    return jnp.einsum('bhqk,bkhd->bqhd', pr[..., :p], v_past) + jnp.einsum('bhqk,bkhd->bqhd', pr[..., p:], v)


def chunk_mlp_mixer(p, ln_g, ln_b, ws, bs):
    b, t, _ = p.shape
    z = jax.nn.gelu(p)
    u, v = jnp.split(z, 2, -1)
    v = layernorm(v, ln_g, ln_b)
    pad = (-t) % CM_CHUNK
    n = (t + pad) // CM_CHUNK
    vc = jnp.pad(v, ((0, 0), (0, pad), (0, 0))).reshape(b, n, CM_CHUNK, CM_GROUPS, CM_GDIM)
    wm = jnp.where(jnp.tril(jnp.ones((CM_CHUNK, CM_CHUNK), dtype=bool)), ws, 0.0).astype(v.dtype)
    mixed = jnp.einsum('grc,bncgd->bnrgd', wm, vc) + bs.T[:, :, None].astype(v.dtype)
    mixed = mixed.reshape(b, n * CM_CHUNK, CM_WIDTH)[:, :t]
    return u * mixed, v


def memory_kv(mem, g, w_kv, kn):
    b, n, _ = mem.shape
    k, v = jnp.split(rmsnorm(mem, g) @ w_kv, 2, -1)
    shp = (b, n, XA_HEADS, XA_DIM)
    return rmsnorm(k.reshape(shp), kn), v.reshape(shp)


def memory_attend(q, mk, mv, qn):
    b, t, _ = q.shape
    q = rmsnorm(q.reshape(b, t, XA_HEADS, XA_DIM), qn)
    s = jnp.einsum('bthd,bnhd->bhtn', q, mk).astype(jnp.float32) * XA_DIM ** -0.5
    pr = jax.nn.softmax(s, -1).astype(mv.dtype)
    return jnp.einsum('bhtn,bnhd->bthd', pr, mv).reshape(b, t, XA_WIDTH)


def sq_relu_mlp(x, w_up, w_down):
    return jnp.square(jax.nn.relu(x @ w_up)) @ w_down


def setup_inputs(seed: int = 0) -> dict:
    key = jax.random.key(seed)
    ks = iter(jax.random.split(key, 48))

    def nrm(shape, scale=1.0):
        return jax.random.normal(next(ks), shape, jnp.float32) * scale

    def gain(shape):
        return 1.0 + nrm(shape, 0.05)

    D = D_MODEL
    n_pages = PAST_LEN // PAGE_SIZE
    n_used = DEC_BATCH * n_pages
    n_pool = n_used + max(1, n_used // 4)
    cache_fbias = jax.random.uniform(next(ks), (N_C, 1, 1, FOX_HEADS), jnp.float32, FOX_FBIAS_LO, FOX_FBIAS_HI)
    cache_logf = jax.nn.log_sigmoid(nrm((N_C, n_pool, PAGE_SIZE, FOX_HEADS), 0.5) + cache_fbias)
    return {
        'x_prompt': nrm((BATCH, SEQ, D)),
        'x_sample': nrm((DEC_BATCH, DEC_SEQ, D)),
        'mem_prompt': nrm((BATCH, N_MEM, D)),
        'state_a_conv': nrm((N_A, DEC_BATCH, CONV_W - 1, GDN_CONV_CH)),
        'state_a_ssm': nrm((N_A, DEC_BATCH, GDN_HEADS, GDN_DK, GDN_DV), 0.1),
        'state_b_ssm': nrm((N_B, DEC_BATCH, HG_HEADS, HG_DK, HG_DV), 0.3),
        'cache_c_k': nrm((N_C, n_pool, PAGE_SIZE, FOX_HEADS, FOX_DIM)),
        'cache_c_v': nrm((N_C, n_pool, PAGE_SIZE, FOX_HEADS, FOX_DIM)),
        'cache_c_logf': cache_logf,
        'cache_mem_k': nrm((DEPTH, DEC_BATCH, N_MEM, XA_HEADS, XA_DIM)),
        'cache_mem_v': nrm((DEPTH, DEC_BATCH, N_MEM, XA_HEADS, XA_DIM)),
        'page_table': jax.random.permutation(next(ks), n_pool)[:n_used].reshape(DEC_BATCH, n_pages).astype(jnp.int32),
        'norm_mix': gain((DEPTH, D)),
        'w_out': nrm((DEPTH, MIX_OUT, D), MIX_OUT ** -0.5),
        'norm_mlp': gain((DEPTH, D)),
        'w_up': nrm((DEPTH, D, D_FF), D ** -0.5),
        'w_down': nrm((DEPTH, D_FF, D), D_FF ** -0.5),
        'mem_norm': gain((DEPTH, D)),
        'w_mem_kv': nrm((DEPTH, D, 2 * XA_WIDTH), D ** -0.5),
        'xa_qnorm': gain((DEPTH, XA_DIM)),
        'xa_knorm': gain((DEPTH, XA_DIM)),
        'w_in_a': nrm((N_A, D, GDN_IN + XA_WIDTH), D ** -0.5),
        'a_conv_w': nrm((N_A, CONV_W, GDN_CONV_CH), CONV_W ** -0.5),
        'a_log': jnp.log(jax.random.uniform(next(ks), (N_A, GDN_HEADS), jnp.float32, 1.0, 16.0)),
        'a_dt_bias': (lambda dt: dt + jnp.log(-jnp.expm1(-dt)))(jnp.exp(jax.random.uniform(next(ks), (N_A, GDN_HEADS), jnp.float32, math.log(1e-3), math.log(1e-1)))),
        'a_norm_w': gain((N_A, GDN_DV)),
        'w_in_b': nrm((N_B, D, HG_IN + XA_WIDTH), D ** -0.5),
        'hg_lb': nrm((DEPTH, HG_WIDTH), 0.5),
        'b_norm_w': gain((N_B, HG_DV)),
        'w_in_c': nrm((N_C, D, FOX_IN + XA_WIDTH), D ** -0.5),
        'c_fbias': jax.random.uniform(next(ks), (N_C, FOX_HEADS), jnp.float32, FOX_FBIAS_LO, FOX_FBIAS_HI),
        'c_qnorm': gain((N_C, FOX_DIM)),
        'c_knorm': gain((N_C, FOX_DIM)),
        'w_in_d': nrm((N_D, D, CM_IN + XA_WIDTH), D ** -0.5),
        'd_ln_g': gain((N_D, CM_WIDTH)),
        'd_ln_b': nrm((N_D, CM_WIDTH), 0.02),
        'd_ws': nrm((N_D, CM_GROUPS, CM_CHUNK, CM_CHUNK), CM_CHUNK ** -0.5),
        'd_bs': 1.0 + nrm((N_D, CM_GROUPS, CM_CHUNK), 0.05),
    }


def reference(x_prompt, x_sample, mem_prompt, state_a_conv, state_a_ssm, state_b_ssm,
              cache_c_k, cache_c_v, cache_c_logf, cache_mem_k, cache_mem_v, page_table,
              norm_mix, w_out, norm_mlp, w_up, w_down, mem_norm, w_mem_kv, xa_qnorm, xa_knorm,
              w_in_a, a_conv_w, a_log, a_dt_bias, a_norm_w,
              w_in_b, hg_lb, b_norm_w,
              w_in_c, c_fbias, c_qnorm, c_knorm,
              w_in_d, d_ln_g, d_ln_b, d_ws, d_bs):
    f32 = jnp.float32
    bp = x_prompt.shape[0]
    ds = x_sample.shape[0]
    past = page_table.shape[1] * PAGE_SIZE
    lb_w = jax.nn.softmax(hg_lb.astype(f32), axis=0)
    lower_bounds = jnp.cumsum(lb_w, axis=0) - lb_w[0]
    a_conv_p, a_conv_s, a_ssm_p, a_ssm_s = [], [], [], []
    b_ssm_p, b_ssm_s = [], []
    c_k_p, c_v_p, c_lf_p, c_k_s, c_v_s, c_lf_s = [], [], [], [], [], []
    d_v_s, mem_k_p, mem_v_p = [], [], []
    hp, hs = x_prompt, x_sample
    for l in range(DEPTH):
        kind, j = l % N_MIXERS, l // N_MIXERS
        w_in = (w_in_a, w_in_b, w_in_c, w_in_d)[kind][j]
        pp = rmsnorm(hp, norm_mix[l]) @ w_in
        ps = rmsnorm(hs, norm_mix[l]) @ w_in
        mix_p, xq_p = pp[..., :-XA_WIDTH], pp[..., -XA_WIDTH:]
        mix_s, xq_s = ps[..., :-XA_WIDTH], ps[..., -XA_WIDTH:]
        mk, mv = memory_kv(mem_prompt, mem_norm[l], w_mem_kv[l], xa_knorm[l])
        mem_k_p.append(mk)
        mem_v_p.append(mv)
        xo_p = memory_attend(xq_p, mk, mv, xa_qnorm[l])
        xo_s = memory_attend(xq_s, cache_mem_k[l], cache_mem_v[l], xa_qnorm[l])
        if kind == 0:
            mo_p, buf, st = gdn_mixer(mix_p, jnp.zeros((bp, CONV_W - 1, GDN_CONV_CH), mix_p.dtype),
                                      jnp.zeros((bp, GDN_HEADS, GDN_DK, GDN_DV), f32),
                                      a_conv_w[j], a_log[j], a_dt_bias[j], a_norm_w[j])
            a_conv_p.append(buf)
            a_ssm_p.append(st)
            mo_s, buf, st = gdn_mixer(mix_s, state_a_conv[j], state_a_ssm[j],
                                      a_conv_w[j], a_log[j], a_dt_bias[j], a_norm_w[j])
            a_conv_s.append(buf)
            a_ssm_s.append(st)
        elif kind == 1:
            mo_p, st = hgrn2_mixer(mix_p, jnp.zeros((bp, HG_HEADS, HG_DK, HG_DV), f32), lower_bounds[l], b_norm_w[j])
            b_ssm_p.append(st)
            mo_s, st = hgrn2_mixer(mix_s, state_b_ssm[j], lower_bounds[l], b_norm_w[j])
            b_ssm_s.append(st)
        elif kind == 2:
            q, k, v, lf, og = fox_project(mix_p, c_fbias[j], c_qnorm[j], c_knorm[j])
            mo_p = fox_prompt(q, k, v, lf).reshape(og.shape) * og
            c_k_p.append(k)
            c_v_p.append(v)
            c_lf_p.append(lf)
            q, k, v, lf, og = fox_project(mix_s, c_fbias[j], c_qnorm[j], c_knorm[j])
            k_past = cache_c_k[j][page_table].reshape(ds, past, FOX_HEADS, FOX_DIM)
            v_past = cache_c_v[j][page_table].reshape(ds, past, FOX_HEADS, FOX_DIM)
            lf_past = cache_c_logf[j][page_table].reshape(ds, past, FOX_HEADS)
            mo_s = fox_sample(q, k, v, lf, k_past, v_past, lf_past).reshape(og.shape) * og
            c_k_s.append(k)
            c_v_s.append(v)
            c_lf_s.append(lf)
        else:
            mo_p, _ = chunk_mlp_mixer(mix_p, d_ln_g[j], d_ln_b[j], d_ws[j], d_bs[j])
            mo_s, vrows = chunk_mlp_mixer(mix_s, d_ln_g[j], d_ln_b[j], d_ws[j], d_bs[j])
            d_v_s.append(vrows)
        hp = hp + jnp.concatenate([mo_p.astype(hp.dtype), xo_p.astype(hp.dtype)], -1) @ w_out[l]
        hs = hs + jnp.concatenate([mo_s.astype(hs.dtype), xo_s.astype(hs.dtype)], -1) @ w_out[l]
        hp = hp + sq_relu_mlp(rmsnorm(hp, norm_mlp[l]), w_up[l], w_down[l])
        hs = hs + sq_relu_mlp(rmsnorm(hs, norm_mlp[l]), w_up[l], w_down[l])
    return (hp, hs,
            jnp.stack(a_conv_p), jnp.stack(a_conv_s), jnp.stack(a_ssm_p), jnp.stack(a_ssm_s),
            jnp.stack(b_ssm_p), jnp.stack(b_ssm_s),
            jnp.stack(c_k_p), jnp.stack(c_v_p), jnp.stack(c_lf_p),
            jnp.stack(c_k_s), jnp.stack(c_v_s), jnp.stack(c_lf_s),
            jnp.stack(d_v_s), jnp.stack(mem_k_p), jnp.stack(mem_v_p))
```

```python
import numpy as np
import concourse.bass as bass
import concourse.mybir as mybir
from concourse.bass_utils import run_bass_kernel_spmd

F32 = mybir.dt.float32
BF16 = mybir.dt.bfloat16
I32 = mybir.dt.int32
AF = mybir.ActivationFunctionType
ALU = mybir.AluOpType
AX = mybir.AxisListType

NCORES = 8
D = 1024
KC = 8
TP = 2048
TS = 32
TT = TP + TS
EPS = 1e-6
SAME_ENG_SYNC = True


class Tok:
    __slots__ = ("name", "w", "r", "sem", "cnt")

    def __init__(self, name):
        self.name = name
        self.w = None
        self.r = []
        self.sem = None
        self.cnt = 0


class Prog:
    ENGS = ("sp", "act", "dve", "pool", "pe")

    def __init__(self, nc):
        self.nc = nc
        self.ops = []
        self.toks = []

    def tok(self, name):
        t = Tok(name)
        self.toks.append(t)
        return t

    def add(self, eng, fn, rd=(), wr=(), dma_tok=None):
        oid = len(self.ops)
        deps = set()
        for t in rd:
            if t.w is not None:
                deps.add(t.w)
        for t in wr:
            if t.w is not None:
                deps.add(t.w)
            deps.update(t.r)
        for t in rd:
            t.r.append(oid)
        for t in wr:
            t.w = oid
            t.r = []
        op = {"id": oid, "eng": eng, "fn": fn, "deps": deps, "dma_tok": dma_tok,
              "dma_val": None, "sig": False, "sigidx": None}
        if dma_tok is not None:
            dma_tok.cnt += 16
            op["dma_val"] = dma_tok.cnt
        self.ops.append(op)
        return oid

    def finalize(self):
        nc = self.nc
        ops = self.ops
        for op in ops:
            need = []
            for d in op["deps"]:
                a = ops[d]
                if a["dma_tok"] is not None:
                    need.append(d)
                elif a["eng"] == op["eng"]:
                    if a["eng"] == "pe" or not SAME_ENG_SYNC:
                        continue
                    if a["fn"] is None:
                        continue
                    need.append(d)
                    a["sig"] = True
                else:
                    if a["fn"] is None:
                        a["sig"] = True
                        need.append(d)
                    else:
                        a["sig"] = True
                        need.append(d)
            op["need"] = need
        cnt = {e: 0 for e in self.ENGS}
        for op in ops:
            if op["sig"] and op["dma_tok"] is None:
                cnt[op["eng"]] += 1
                op["sigidx"] = cnt[op["eng"]]
        self.esem = {e: nc.alloc_semaphore(name="sem_" + e) for e in self.ENGS}
        for t in self.toks:
            if t.cnt > 0:
                t.sem = nc.alloc_semaphore(name="dsem_" + t.name)
        self.nsig = cnt

    def emit(self):
        nc = self.nc
        ops = self.ops
        per = {e: [op for op in ops if op["eng"] == e] for e in self.ENGS}
        esem = self.esem

        def run(e, eng):
            waited = {}
            for op in per[e]:
                for d in op["need"]:
                    a = ops[d]
                    if a["dma_tok"] is not None:
                        sem, val = a["dma_tok"].sem, a["dma_val"]
                    else:
                        sem, val = esem[a["eng"]], a["sigidx"]
                    key = sem.num
                    if waited.get(key, 0) >= val:
                        continue
                    eng.wait_ge(sem, val)
                    waited[key] = val
                if op["fn"] is None:
                    ins = eng.nop() if op["sig"] else None
                else:
                    ins = op["fn"](eng)
                if ins is not None:
                    if op["dma_tok"] is not None:
                        ins.then_inc(op["dma_tok"].sem, 16)
                    elif op["sig"]:
                        ins.then_inc(esem[e], 1)
            if e == "sp":
                for t in self.toks:
                    if t.sem is not None and waited.get(t.sem.num, 0) < t.cnt:
                        eng.wait_ge(t.sem, t.cnt)

        with nc.Block() as block:
            @block.sync
            def _(eng):
                run("sp", eng)

            @block.scalar
            def _(eng):
                run("act", eng)

            @block.vector
            def _(eng):
                run("dve", eng)

            @block.gpsimd
            def _(eng):
                run("pool", eng)

            @block.tensor
            def _(eng):
                run("pe", eng)


class T:
    def __init__(self, P, handle, name):
        self.h = handle
        self.t = P.tok(name)

    def __getitem__(self, idx):
        return self.h[idx]


from contextlib import ExitStack

BLKS = [(0, 512), (512, 512), (1024, 512), (1536, 512), (2048, 32)]
SUPER = [[0], [1], [2], [3, 4]]
NW = 3
WSLOT = 4096

OUT_SPECS = [
    ("y_p", [TP, D]), ("y_s", [TS, D]),
    ("a_conv_p", [3, 3072]), ("a_conv_s", [4, 3, 3072]),
    ("a_ssm_p", [8, 128, 128]), ("a_ssm_s", [4, 8, 128, 128]),
    ("b_ssm_p", [8, 128, 128]), ("b_ssm_s", [4, 8, 128, 128]),
    ("c_k_p", [TP, D]), ("c_v_p", [TP, D]), ("c_lf_p", [TP, 16]),
    ("c_k_s", [TS, D]), ("c_v_s", [TS, D]), ("c_lf_s", [TS, 16]),
    ("d_v_s", [TS, D]), ("mem_k_p", [4, 256, 256]), ("mem_v_p", [4, 256, 256]),
]


def build(npool=2560, layers=(0, 1, 2, 3), mixers=(0, 1, 2, 3)):
    nc = bass.Bass("TRN2", target_bir_lowering=False)
    P = Prog(nc)
    es = ExitStack()

    def din(name, shape, dt=F32):
        return nc.dram_tensor(name, list(shape), dt, kind="ExternalInput").ap()

    def dout(name, shape, dt=F32):
        return nc.dram_tensor(name, list(shape), dt, kind="ExternalOutput").ap()

    def sb(name, shape, dt=F32):
        return T(P, es.enter_context(nc.sbuf_tensor(name, list(shape), dt)), name)

    I = {}
    I["xp"] = din("xp", [TP, D])
    I["xs"] = din("xs", [TS, D])
    I["mem"] = din("mem", [256, D])
    I["a_conv"] = din("a_conv", [4, 3, 3072])
    I["a_ssm"] = din("a_ssm", [4, 8, 128, 128])
    I["b_ssm"] = din("b_ssm", [4, 8, 128, 128])
    I["ck"] = din("ck", [npool * 128, 1024])
    I["cv"] = din("cv", [npool * 128, 1024])
    I["clf"] = din("clf", [npool * 128, 16])
    I["cmk"] = din("cmk", [4, 4, 256, 256])
    I["cmv"] = din("cmv", [4, 4, 256, 256])
    I["pt"] = din("pt", [4, 64], I32)
    I["c_iota"] = din("c_iota", [128, 1], I32)
    for nm, shp in [("norm_mix", [4, D]), ("w_out", [4, 1280, D]), ("norm_mlp", [4, D]),
                    ("w_up", [4, D, 4096]), ("w_down", [4, 4096, D]), ("mem_norm", [4, D]),
                    ("w_mem_kv", [4, D, 512]), ("xa_qnorm", [4, 64]), ("xa_knorm", [4, 64]),
                    ("w_in_a", [D, 4368]), ("a_conv_w", [4, 3072]), ("a_log", [1, 8]),
                    ("a_dt_bias", [1, 8]), ("a_norm_w", [1, 128]), ("w_in_b", [D, 4352]),
                    ("hg_lb", [4, D]), ("b_norm_w", [1, 128]), ("w_in_c", [D, 4368]),
                    ("c_fbias", [1, 16]), ("c_qnorm", [1, 64]), ("c_knorm", [1, 64]),
                    ("w_in_d", [D, 2304]), ("d_ln_g", [1, D]), ("d_ln_b", [1, D]),
                    ("d_ws", [8, 128, 128]), ("d_bs", [8, 128]),
                    ("c_ident", [128, 128]), ("c_ones", [128, 128]), ("c_blk64", [128, 128]),
                    ("c_triu", [128, 128]), ("c_striu", [128, 128]),
                    ("c_sel", [8, 1024]), ("c_sl64", [64, 64]), ("c_sl8", [8, 8]),
                    ("c_stril", [128, 128]), ("c_msel", [32, 32]), ("c_bd32", [32, 32]), ("c_smask", [4, 32])]:
        I[nm] = din(nm, shp)
    O = {nm: dout(nm, shp) for nm, shp in OUT_SPECS}
    DEBUG = globals().get("KDBG", False)
    if DEBUG:
        O["dbg1"] = dout("dbg1", [128, 1024])
        O["dbg2"] = dout("dbg2", [128, 1024])
        dbgtok = P.tok("dbgtok")
    otok = {nm: P.tok("o_" + nm) for nm, _ in OUT_SPECS}

    def mm(out, lhsT, rhs, start, stop, rd, wr):
        P.add("pe", lambda e: e.matmul(out, lhsT, rhs, start=start, stop=stop), rd, wr)

    def tr(out, in_, ident, rd, wr):
        P.add("pe", lambda e: e.transpose(out, in_, ident), rd, wr)

    def act(out, in_, func, rd, wr, bias=None, scale=1.0):
        if bias is None:
            P.add("act", lambda e: e.activation(out, in_, func, scale=scale), rd, wr)
        else:
            P.add("act", lambda e: e.activation(out, in_, func, bias=bias, scale=scale), rd, wr)

    def tt(eng, out, in0, in1, op, rd, wr):
        P.add(eng, lambda e: e.tensor_tensor(out, in0, in1, op), rd, wr)

    def ts(eng, out, in0, s1, s2, op0, op1, rd, wr):
        if s2 is None:
            P.add(eng, lambda e: e.tensor_scalar(out, in0, s1, None, op0), rd, wr)
        else:
            P.add(eng, lambda e: e.tensor_scalar(out, in0, s1, s2, op0, op1), rd, wr)

    def stt(eng, out, in0, scalar, in1, op0, op1, rd, wr):
        P.add(eng, lambda e: e.scalar_tensor_tensor(out, in0, scalar, in1, op0, op1), rd, wr)

    def cp(eng, out, in_, rd, wr):
        if eng == "act":
            P.add("act", lambda e: e.copy(out, in_), rd, wr)
        else:
            P.add(eng, lambda e: e.tensor_copy(out, in_), rd, wr)

    def memset(eng, ap, val, wr):
        P.add(eng, lambda e: e.memset(ap, val), (), wr)

    def dma(q, out, in_, rd, wr, tok, slow=False):
        if slow:
            P.add(q, lambda e: e.dma_start(out=out, in_=in_, allow_slow_non_contiguous=True), rd, wr, dma_tok=tok)
        else:
            P.add(q, lambda e: e.dma_start(out=out, in_=in_), rd, wr, dma_tok=tok)

    epsc = {}

    def rsqrt_eps(out, in_, epsval, rd, wr):
        if epsval not in epsc:
            c_ = sb("epsc%d" % len(epsc), [128, 1])
            memset("dve", c_[:, :], float(epsval), [c_.t])
            epsc[epsval] = c_
        c_ = epsc[epsval]
        np_ = out.shape[0]
        act(out, in_, AF.Sqrt, rd + [c_.t], wr, bias=c_[0:np_, 0:1])
        P.add("dve", lambda e: e.reciprocal(out, out), wr, wr)

    _cpi = [0]

    def evac(out, in_, rd, wr):
        _cpi[0] += 1
        cp("act" if _cpi[0] % 2 else "dve", out, in_, rd, wr)

    PS = [T(P, es.enter_context(nc.psum_tensor("ps%d" % i, [128, 512], F32)), "ps%d" % i) for i in range(8)]
    _psi = [0]

    PSROT = [7]

    def ps():
        _psi[0] += 1
        return PS[_psi[0] % PSROT[0]]

    psacc = PS[7]

    ident = sb("ident", [128, 128])
    ones_f = sb("ones_f", [128, 128])
    blk_f = sb("blk_f", [128, 128])
    triu_f = sb("triu_f", [128, 128])
    striu_f = sb("striu_f", [128, 128])
    ones_b = sb("ones_b", [128, 128], BF16)
    blk_b = sb("blk_b", [128, 128], BF16)
    for t_, nm in [(ident, "c_ident"), (ones_f, "c_ones"), (blk_f, "c_blk64"), (triu_f, "c_triu"), (striu_f, "c_striu")]:
        dma("sp", t_[:, :], I[nm][:, :], (), [t_.t], t_.t)
    cp("dve", ones_b[:, :], ones_f[:, :], [ones_f.t], [ones_b.t])
    cp("dve", blk_b[:, :], blk_f[:, :], [blk_f.t], [blk_b.t])
    triu_b = sb("triu_b", [128, 128], BF16)
    cp("dve", triu_b[:, :], triu_f[:, :], [triu_f.t], [triu_b.t])
    stril_b = sb("stril_b", [128, 128], BF16)

    gv = {}
    for nm in ("norm_mix", "norm_mlp", "mem_norm"):
        g = sb("g_" + nm, [128, 32])
        dma("sp", g[:, :], I[nm].rearrange("l (c p) -> p (l c)", p=128), (), [g.t], g.t, slow=True)
        ts("dve", g[:, :], g[:, :], 32.0, None, ALU.mult, None, [g.t], [g.t])
        gv[nm] = g
    qg = sb("qg", [128, 4])
    dma("sp", qg[0:64, :], I["xa_qnorm"].rearrange("l d -> d l"), (), [qg.t], qg.t, slow=True)
    dma("sp", qg[64:128, :], I["xa_qnorm"].rearrange("l d -> d l"), (), [qg.t], qg.t, slow=True)
    ARENA = 15124
    arena_h = es.enter_context(nc.sbuf_tensor("arena", [128, ARENA], F32))
    ast = {"off": 0, "prev": [], "cur": []}

    def arena_reset():
        ast["prev"] = ast["prev"] + ast["cur"]
        ast["cur"] = []
        ast["off"] = 0

    def aalloc(name, shape, dt=F32):
        nel = 1
        for d_ in shape[1:]:
            nel *= d_
        nf = (nel * (4 if dt in (F32, I32) else 2) + 3) // 4
        off = ast["off"]
        ast["off"] += nf
        assert ast["off"] <= ARENA, ("arena overflow", name, ast["off"])
        v = arena_h[0:shape[0], off:off + nf]
        if dt != F32:
            v = v.bitcast(dt)[:, 0:nel]
        if len(shape) == 3:
            v = v.rearrange("p (a b) -> p a b", a=shape[1])
        elif len(shape) == 4:
            v = v.rearrange("p (a b c) -> p a b c", a=shape[1], b=shape[2])
        t_ = T(P, v, name)
        for pt in ast["prev"]:
            if pt.w is not None:
                t_.t.r.append(pt.w)
            t_.t.r.extend(pt.r)
        ast["cur"].append(t_.t)
        return t_

    kg = aalloc("kg", [128, 4, 4, 64])
    for hh in range(4):
        dma("sp", kg[:, :, hh, :], I["xa_knorm"].partition_broadcast(128), (), [kg.t], kg.t, slow=True)

    hT = sb("hT", [128, KC, TT])
    hTt = [P.tok("hT%d" % b) for b in range(5)]
    xin = [sb("xin%d" % i, [128, 1040]) for i in range(2)]
    dma("sp", xin[0][:, 0:128], I["c_stril"][:, :], (), [xin[0].t], xin[0].t)
    cp("dve", stril_b[:, :], xin[0][:, 0:128], [xin[0].t], [stril_b.t])
    iota_i = sb("iota_i", [128, 1], I32)
    iota_f = sb("iota_f", [128, 1])
    dma("sp", iota_i[:, :], I["c_iota"][:, :], (), [iota_i.t], iota_i.t)
    cp("dve", iota_f[:, :], iota_i[:, :], [iota_i.t], [iota_f.t])
    msel = sb("msel", [32, 32], BF16)
    dma("sp", xin[1][0:32, 0:32], I["c_msel"][:, :], (), [xin[1].t], xin[1].t)
    cp("dve", msel[:, :], xin[1][0:32, 0:32], [xin[1].t], [msel.t])

    def load_tokens_fm(src_ap, nrows, dst, dst_off, wtok, i):
        xt = xin[i % 2]
        dma("sp", xt[0:nrows, 0:D], src_ap, (), [xt.t], xt.t)
        for half in range(2):
            p_ = ps()
            for j in range(4):
                c = half * 4 + j
                tr(p_[:, j * 128:j * 128 + nrows], xt[0:nrows, c * 128:(c + 1) * 128], ident[0:nrows, 0:nrows], [xt.t, ident.t], [p_.t])
            evac(dst[:, half * 4:half * 4 + 4, dst_off:dst_off + nrows],
                 p_[:, :].rearrange("p (j n) -> p j n", j=4)[:, :, 0:nrows], [p_.t], [wtok])


    WR = [sb("wr%d" % i, [128, WSLOT], BF16) for i in range(NW)]
    _wi = [0]

    def wload(src3):
        _wi[0] += 1
        w = WR[_wi[0] % NW]
        kc, g = src3.shape[1], src3.shape[2]
        view = w[:, 0:kc * g].rearrange("p (k g) -> p k g", k=kc)
        dma("pool", view, src3, (), [w.t], w.t)
        return w, view

    sqs = [sb("sq%d" % i, [128, 544], BF16) for i in range(2)]
    rq = sb("rq", [128, 512])
    rstd = rq
    rden = rq
    xn = sb("xn", [128, KC, 544], BF16)

    def rmsnorm_fm(src, t0, n, gcols, dst, rd, wr, doff=0):
        p_ = ps()
        for c in range(KC):
            sq = sqs[c % 2]
            act(sq[:, 0:n], src[:, c, t0:t0 + n], AF.Square, rd, [sq.t])
            mm(p_[:, 0:n], ones_b[:, :], sq[:, 0:n], c == 0, c == KC - 1, [ones_b.t, sq.t], [p_.t])
        rsqrt_eps(rstd[:, 0:n], p_[:, 0:n], 1024.0 * EPS, [p_.t], [rstd.t])
        for c in range(KC):
            stt("dve", dst[:, c, doff:doff + n], src[:, c, t0:t0 + n], gcols[:, c:c + 1], rstd[:, 0:n], ALU.mult, ALU.mult,
                rd + [rstd.t], wr)

    for i in range(2):
        load_tokens_fm(I["mem"][i * 128:(i + 1) * 128, :], 128, hT, i * 128, hTt[0], 17 + i)

    onesp = sb("onesp", [128, 2, 128], BF16)
    memset("dve", onesp[:, :, :], 0.0, [onesp.t])
    memset("dve", onesp[:, 0, 0:64], 1.0, [onesp.t])
    memset("dve", onesp[:, 1, 64:128], 1.0, [onesp.t])
    kvs = aalloc("kvs", [128, 512])
    kss = aalloc("kss", [128, 4])
    ksq = aalloc("ksq", [128, 256])

    def prep_mem(l, kn_src_fn, mk, mv):
        pass

    for l in range(4):
        rmsnorm_fm(hT, 0, 256, gv["mem_norm"][:, l * 8:(l + 1) * 8], xn, [hTt[0]], [xn.t])
        w, wv = wload(I["w_mem_kv"][l].rearrange("(k p) g -> p k g", p=128))
        for j in range(2):
            p_ = ps()
            for c in range(KC):
                mm(p_[:, :], xn[:, c, j * 128:(j + 1) * 128], wv[:, c, :], c == 0, c == KC - 1, [xn.t, w.t], [p_.t])
            act(ksq[:, :], p_[:, 0:256], AF.Square, [p_.t], [ksq.t])
            P.add("dve", lambda e, o=kss[:, :], i_=ksq[:, :].rearrange("p (h d) -> p h d", h=4): e.tensor_reduce(o, i_, AX.X, ALU.add),
                  [ksq.t], [kss.t])
            rsqrt_eps(kss[:, :], kss[:, :], 64.0 * EPS, [kss.t], [kss.t])
            tt("dve", kvs[:, 0:256].rearrange("p (h d) -> p h d", h=4), p_[:, 0:256].rearrange("p (h d) -> p h d", h=4),
               kss[:, :].unsqueeze(2).to_broadcast([128, 4, 64]), ALU.mult, [p_.t, kss.t], [kvs.t])
            stt("dve", kvs[:, 0:256].rearrange("p (h d) -> p h d", h=4), kvs[:, 0:256].rearrange("p (h d) -> p h d", h=4), 8.0,
                kg[:, l, :, :], ALU.mult, ALU.mult, [kvs.t, kg.t], [kvs.t])
            cp("act", kvs[:, 256:512], p_[:, 256:512], [p_.t], [kvs.t])
            dma("sp", O["mem_k_p"][l, j * 128:(j + 1) * 128, :], kvs[:, 0:256], [kvs.t], [otok["mem_k_p"]], kvs.t)
            dma("sp", O["mem_v_p"][l, j * 128:(j + 1) * 128, :], kvs[:, 256:512], [kvs.t], [otok["mem_v_p"]], kvs.t)

    for i in range(16):
        load_tokens_fm(I["xp"][i * 128:(i + 1) * 128, :], 128, hT, i * 128, hTt[i // 4], i)
    load_tokens_fm(I["xs"][:, :], 32, hT, TP, hTt[4], 16)

    moT = sb("moT", [128, 10, 544], BF16)
    memset("dve", moT[:, :, :], 0.0, [moT.t])
    qs = sb("qs", [128, 2, 544], BF16)
    sq2 = sqs[1]
    ptile = sb("ptile", [128, 2, 512], BF16)
    mkTs = [sb("mkTs%d" % b, [128, 2, 256], BF16) for b in range(5)]
    mvps = [sb("mvps%d" % b, [128, 2, 4, 128], BF16) for b in range(5)]
    for b in range(5):
        memset("dve", mvps[b][:, :, :, :], 0.0, [mvps[b].t])
    aT = sb("aT", [128, 4, 544], BF16)
    rl = sqs[0]
    W_IN = [I["w_in_a"], I["w_in_b"], I["w_in_c"], I["w_in_d"]]

    def seg_list(sbk):
        out, off = [], 0
        for b in sbk:
            out.append((b, BLKS[b][0], BLKS[b][1], off))
            off += BLKS[b][1]
        return out

    def prep_sample_mem(l):
        for b in range(5):
            for j in range(2):
                xt = xin[(b * 2 + j) % 2]
                if b < 4:
                    dma("sp", xt[:, 0:256], I["cmk"][l, b, j * 128:(j + 1) * 128, :], (), [xt.t], xt.t)
                    dma("sp", xt[:, 256:512], I["cmv"][l, b, j * 128:(j + 1) * 128, :], (), [xt.t], xt.t)
                else:
                    dma("sp", xt[:, 0:256], O["mem_k_p"][l, j * 128:(j + 1) * 128, :], [otok["mem_k_p"]], [xt.t], xt.t)
                    dma("sp", xt[:, 256:512], O["mem_v_p"][l, j * 128:(j + 1) * 128, :], [otok["mem_v_p"]], [xt.t], xt.t)
                pt_ = ps()
                for hp in range(2):
                    tr(pt_[:, hp * 128:(hp + 1) * 128], xt[:, hp * 128:(hp + 1) * 128], ident[:, :], [xt.t, ident.t], [pt_.t])
                evac(mkTs[b][:, :, j * 128:(j + 1) * 128], pt_[:, 0:256].rearrange("p (h n) -> p h n", h=2), [pt_.t], [mkTs[b].t])
                for hh in range(4):
                    half = hh % 2
                    cp("dve", mvps[b][:, j, hh, half * 64:half * 64 + 64], xt[:, 256 + hh * 64:256 + hh * 64 + 64], [xt.t], [mvps[b].t])

    def attend(l, mk, mv, off, n):
        for hp in range(2):
            pnum = ps()
            pden = ps()
            for half in range(2):
                hh = hp * 2 + half
                lo = half * 64
                for j in range(2):
                    p_ = ps()
                    mm(p_[:, 0:n], mk[lo:lo + 64, hp, j * 128:(j + 1) * 128], qs[lo:lo + 64, hp, off:off + n], True, True,
                       [mk.t, qs.t], [p_.t])
                    act(ptile[:, j, 0:n], p_[:, 0:n], AF.Exp, [p_.t], [ptile.t])
                for j in range(2):
                    first = (half == 0 and j == 0)
                    last = (half == 1 and j == 1)
                    mm(pnum[:, 0:n], mv[:, j, hh, :], ptile[:, j, 0:n], first, last, [mv.t, ptile.t], [pnum.t])
                    mm(pden[:, 0:n], onesp[:, half, :], ptile[:, j, 0:n], first, last, [onesp.t, ptile.t], [pden.t])
            P.add("dve", lambda e, o=rden[:, 0:n], i_=pden[:, 0:n]: e.reciprocal(o, i_), [pden.t], [rden.t])
            tt("dve", moT[:, 8 + hp, off:off + n], pnum[:, 0:n], rden[:, 0:n], ALU.mult, [pnum.t, rden.t], [moT.t])

    def xq_and_attend(l, sbk):
        w_in = W_IN[l % 4]
        c0 = w_in.shape[1] - 256
        w, wv = wload(w_in[:, c0:c0 + 256].rearrange("(k p) g -> p k g", p=128))
        for (b, t0, n, off) in seg_list(sbk):
            for hp in range(2):
                p_ = ps()
                for c in range(KC):
                    mm(p_[:, 0:n], wv[:, c, hp * 128:(hp + 1) * 128], xn[:, c, off:off + n], c == 0, c == KC - 1, [w.t, xn.t], [p_.t])
                act(sq2[:, 0:n], p_[:, 0:n], AF.Square, [p_.t], [sq2.t])
                p2 = ps()
                mm(p2[:, 0:n], blk_b[:, :], sq2[:, 0:n], True, True, [blk_b.t, sq2.t], [p2.t])
                rsqrt_eps(rq[:, 0:n], p2[:, 0:n], 64.0 * EPS, [p2.t], [rq.t])
                stt("dve", qs[:, hp, off:off + n], p_[:, 0:n], qg[:, l:l + 1], rq[:, 0:n], ALU.mult, ALU.mult,
                    [p_.t, qg.t, rq.t], [qs.t])
            if b < 4:
                attend(l, mkTs[4], mvps[4], off, n)
            else:
                for sq_ in range(4):
                    attend(l, mkTs[sq_], mvps[sq_], off + sq_ * 8, 8)

    def out_proj(l, sbk):
        segs = seg_list(sbk)
        for g in range(4):
            w, wv = wload(I["w_out"][l][:, g * 256:(g + 1) * 256].rearrange("(k p) g -> p k g", p=128))
            for m in range(2):
                for (b, t0, n, off) in segs:
                    p_ = ps()
                    for c in range(10):
                        mm(p_[:, 0:n], wv[:, c, m * 128:(m + 1) * 128], moT[:, c, off:off + n], c == 0, c == 9, [w.t, moT.t], [p_.t])
                    tt("dve", hT[:, g * 2 + m, t0:t0 + n], hT[:, g * 2 + m, t0:t0 + n], p_[:, 0:n], ALU.add, [p_.t, hTt[b]], [hTt[b]])

    def mlp(l, sbk):
        segs = seg_list(sbk)
        for (b, t0, n, off) in segs:
            rmsnorm_fm(hT, t0, n, gv["norm_mlp"][:, l * 8:(l + 1) * 8], xn, [hTt[b]], [xn.t], doff=off)
        for g in range(8):
            wu, wuv = wload(I["w_up"][l][:, g * 512:(g + 1) * 512].rearrange("(k p) g -> p k g", p=128))
            wd, wdv = wload(I["w_down"][l][g * 512:(g + 1) * 512, :].rearrange("(k p) g -> p k g", p=128))
            for (b, t0, n, off) in segs:
                for f in range(4):
                    p_ = ps()
                    for c in range(KC):
                        mm(p_[:, 0:n], wuv[:, c, f * 128:(f + 1) * 128], xn[:, c, off:off + n], c == 0, c == KC - 1, [wu.t, xn.t], [p_.t])
                    act(rl[:, 0:n], p_[:, 0:n], AF.Relu, [p_.t], [rl.t])
                    tt("dve", aT[:, f, off:off + n], rl[:, 0:n], rl[:, 0:n], ALU.mult, [rl.t], [aT.t])
                for m in range(8):
                    p_ = ps()
                    for f in range(4):
                        mm(p_[:, 0:n], wdv[:, f, m * 128:(m + 1) * 128], aT[:, f, off:off + n], f == 0, f == 3, [wd.t, aT.t], [p_.t])
                    tt("dve", hT[:, m, t0:t0 + n], hT[:, m, t0:t0 + n], p_[:, 0:n], ALU.add, [p_.t, hTt[b]], [hTt[b]])


    MD = {}

    def setup_d():
        arena_reset()
        uT = aalloc("uT", [128, 8, 544], BF16)
        gbt = aalloc("gbt", [128, 2, D])
        dma("sp", gbt[:, 0, :], I["d_ln_g"].partition_broadcast(128), (), [gbt.t], gbt.t, slow=True)
        dma("sp", gbt[:, 1, :], I["d_ln_b"].partition_broadcast(128), (), [gbt.t], gbt.t, slow=True)
        wmT = aalloc("wmT", [128, 8, 128], BF16)
        wmTs = aalloc("wmTs", [32, 8, 32], BF16)
        bs_f = aalloc("bs_f", [1, D])
        bs_b = aalloc("bs_b", [1, D], BF16)
        dma("sp", bs_f[:, :], I["d_bs"].rearrange("g r -> (g r)").unsqueeze(0), (), [bs_f.t], bs_f.t)
        cp("dve", bs_b[:, :], bs_f[:, :], [bs_f.t], [bs_b.t])
        bs_s = aalloc("bs_s", [1, 8, 32], BF16)
        for s_ in range(4):
            cp("dve", bs_s[:, :, s_ * 8:(s_ + 1) * 8], bs_f[:, :].rearrange("o (g r) -> o g r", g=8)[:, :, 0:8], [bs_f.t], [bs_s.t])
        for g in range(8):
            xt = xin[g % 2]
            dma("sp", xt[:, 0:128], I["d_ws"][g], (), [xt.t], xt.t)
            p_ = ps()
            tr(p_[:, 0:128], xt[:, 0:128], ident[:, :], [xt.t, ident.t], [p_.t])
            tt("dve", wmT[:, g, :], p_[:, 0:128], triu_f[:, :], ALU.mult, [p_.t, triu_f.t], [wmT.t])
        memset("dve", wmTs[:, :, :], 0.0, [wmTs.t])
        for s_ in range(4):
            dma("sp", wmTs[s_ * 8:(s_ + 1) * 8, :, s_ * 8:(s_ + 1) * 8], wmT[0:8, :, 0:8], [wmT.t], [wmTs.t], wmTs.t, slow=True)
        MD.update(uT=uT, gbt=gbt, wmT=wmT, wmTs=wmTs, bs_b=bs_b, bs_s=bs_s,
                  g1=aalloc("g1", [128, 512]), g2=aalloc("g2", [128, 512]), vz=aalloc("vz", [128, D]),
                  vb=aalloc("vb", [128, D], BF16), lns=aalloc("lns", [128, 2]))

    if True:
        def gelu_from_psum(p_, npart, n, out_ap, out_tok):
            g1, g2 = MD["g1"], MD["g2"]
            act(g1[0:npart, 0:n], p_[0:npart, 0:n], AF.Square, [p_.t], [g1.t])
            ts("dve", g1[0:npart, 0:n], g1[0:npart, 0:n], 0.044715, 1.0, ALU.mult, ALU.add, [g1.t], [g1.t])
            tt("dve", g2[0:npart, 0:n], g1[0:npart, 0:n], p_[0:npart, 0:n], ALU.mult, [g1.t, p_.t], [g2.t])
            act(g2[0:npart, 0:n], g2[0:npart, 0:n], AF.Sigmoid, [g2.t], [g2.t], scale=1.5957691216)
            tt("dve", out_ap, g2[0:npart, 0:n], p_[0:npart, 0:n], ALU.mult, [g2.t, p_.t], [out_tok])

        def mixer_d(sbk):
            uT, gbt, wmT, wmTs, bs_b, bs_s = MD["uT"], MD["gbt"], MD["wmT"], MD["wmTs"], MD["bs_b"], MD["bs_s"]
            g1, g2, vz, vb, lns = MD["g1"], MD["g2"], MD["vz"], MD["vb"], MD["lns"]
            segs = seg_list(sbk)
            w_in = I["w_in_d"]
            for g in range(2):
                w, wv = wload(w_in[:, g * 512:(g + 1) * 512].rearrange("(k p) g -> p k g", p=128))
                for m in range(4):
                    for (b, t0, n, off) in segs:
                        p_ = ps()
                        for c in range(KC):
                            mm(p_[:, 0:n], wv[:, c, m * 128:(m + 1) * 128], xn[:, c, off:off + n], c == 0, c == KC - 1, [w.t, xn.t], [p_.t])
                        gelu_from_psum(p_, 128, n, uT[:, g * 4 + m, off:off + n], uT.t)
            w0, wv0 = wload(w_in[:, 1024:1536].rearrange("(k p) g -> p k g", p=128))
            w1, wv1 = wload(w_in[:, 1536:2048].rearrange("(k p) g -> p k g", p=128))
            for (b, t0, n, off) in segs:
                ntile = (n + 127) // 128
                for i in range(ntile):
                    nt = min(128, n - i * 128)
                    for hf, (w, wv) in enumerate(((w0, wv0), (w1, wv1))):
                        p_ = ps()
                        for c in range(KC):
                            mm(p_[0:nt, :], xn[:, c, off + i * 128:off + i * 128 + nt], wv[:, c, :], c == 0, c == KC - 1, [xn.t, w.t], [p_.t])
                        gelu_from_psum(p_, nt, 512, vz[0:nt, hf * 512:(hf + 1) * 512], vz.t)
                    P.add("dve", lambda e, o=lns[0:nt, 0:1], i_=vz[0:nt, :]: e.tensor_reduce(o, i_, AX.X, ALU.add), [vz.t], [lns.t])
                    ts("dve", lns[0:nt, 0:1], lns[0:nt, 0:1], -1.0 / 1024.0, None, ALU.mult, None, [lns.t], [lns.t])
                    ts("dve", vz[0:nt, :], vz[0:nt, :], lns[0:nt, 0:1], None, ALU.add, None, [vz.t, lns.t], [vz.t])
                    act(g1[0:nt, :], vz[0:nt, 0:512], AF.Square, [vz.t], [g1.t])
                    act(g2[0:nt, :], vz[0:nt, 512:1024], AF.Square, [vz.t], [g2.t])
                    tt("dve", g1[0:nt, :], g1[0:nt, :], g2[0:nt, :], ALU.add, [g1.t, g2.t], [g1.t])
                    P.add("dve", lambda e, o=lns[0:nt, 1:2], i_=g1[0:nt, :]: e.tensor_reduce(o, i_, AX.X, ALU.add), [g1.t], [lns.t])
                    rsqrt_eps(lns[0:nt, 1:2], lns[0:nt, 1:2], 1024.0 * EPS, [lns.t], [lns.t])
                    stt("dve", vz[0:nt, :], vz[0:nt, :], lns[0:nt, 1:2], gbt[0:nt, 0, :], ALU.mult, ALU.mult, [vz.t, lns.t, gbt.t], [vz.t])
                    stt("dve", vz[0:nt, :], vz[0:nt, :], 32.0, gbt[0:nt, 1, :], ALU.mult, ALU.add, [vz.t, gbt.t], [vz.t])
                    cp("act", vb[0:nt, :], vz[0:nt, :], [vz.t], [vb.t])
                    if b == 4:
                        dma("sp", O["d_v_s"][:, :], vz[0:32, :], [vz.t], [otok["d_v_s"]], vz.t)
                    for hf in range(2):
                        p_ = ps()
                        for g4 in range(4):
                            g = hf * 4 + g4
                            if b < 4:
                                mm(p_[:, g4 * 128:(g4 + 1) * 128], vb[:, g * 128:(g + 1) * 128], wmT[:, g, :], True, False, [vb.t, wmT.t], [p_.t])
                                mm(p_[:, g4 * 128:(g4 + 1) * 128], ones_b[0:1, :], bs_b[0:1, g * 128:(g + 1) * 128], False, True, [ones_b.t, bs_b.t], [p_.t])
                                tt("dve", moT[:, g, off + i * 128:off + (i + 1) * 128], uT[:, g, off + i * 128:off + (i + 1) * 128],
                                   p_[:, g4 * 128:(g4 + 1) * 128], ALU.mult, [uT.t, p_.t], [moT.t])
                            else:
                                mm(p_[:, g4 * 128:g4 * 128 + 32], vb[0:32, g * 128:(g + 1) * 128], wmTs[:, g, :], True, False, [vb.t, wmTs.t], [p_.t])
                                mm(p_[:, g4 * 128:g4 * 128 + 32], ones_b[0:1, :], bs_s[0:1, g, :], False, True, [ones_b.t, bs_s.t], [p_.t])
                                tt("dve", moT[:, g, off:off + 32], uT[:, g, off:off + 32], p_[:, g4 * 128:g4 * 128 + 32], ALU.mult, [uT.t, p_.t], [moT.t])


    MA = {}

    def setup_a():
        arena_reset()
        A = MA
        A["S"] = aalloc("gS", [128, 8, 128])
        A["ctail"] = aalloc("ctail", [128, 24, 3])
        A["ctail_s"] = aalloc("ctail_s", [128, 24, 4, 3])
        A["cst"] = aalloc("cst", [128, 24, 4, 3])
        A["cw"] = aalloc("cw", [128, 24, 4])
        A["nw"] = aalloc("nw", [128, 1])
        A["hp8"] = aalloc("hp8", [8, 3])
        A["sel"] = aalloc("sel", [8, 8, 128])
        A["sl64"] = aalloc("sl64", [64, 64])
        A["sl8"] = aalloc("sl8", [8, 8])
        for nm in ("bt", "gt", "GT"):
            A[nm] = aalloc(nm, [8, 544])
        for nm in ("gtok", "btok", "Gtok", "ekl", "eGtok", "bke"):
            A[nm] = aalloc(nm, [64, 64])
        A["cv"] = [aalloc("cv%d" % i, [128, 544]) for i in range(3)]
        A["zs"] = aalloc("zs", [128, 544], BF16)
        A["kT"] = aalloc("kT", [128, 544], BF16)
        A["kbT"] = aalloc("kbT", [128, 544], BF16)
        A["qT"] = aalloc("qT", [128, 544], BF16)
        A["eGb"] = aalloc("eGb", [128, 544])
        A["RHSk"] = aalloc("RHSk", [64, 8, 128], BF16)
        A["RHSv"] = aalloc("RHSv", [64, 8, 128], BF16)
        A["khat"] = aalloc("khat", [64, 8, 128], BF16)
        A["att"] = aalloc("att", [64, 512], BF16)
        A["TTb"] = aalloc("TTb", [64, 512], BF16)
        A["Xa"] = aalloc("Xa", [64, 512])
        A["XTa"] = aalloc("XTa", [64, 512])
        A["Xb"] = aalloc("Xb", [64, 512])
        A["XTb"] = aalloc("XTb", [64, 512])
        A["Pm"] = aalloc("Pm", [128, 544])
        A["nWT"] = aalloc("nWT", [128, 544])
        A["u"] = [aalloc("u%d" % i, [64, 128], BF16) for i in range(2)]
        A["Sl"] = [aalloc("Sl%d" % i, [128, 128]) for i in range(2)]
        A["So"] = [aalloc("So%d" % i, [128, 128]) for i in range(2)]
        S, ctail, cw, nw, hp8 = A["S"], A["ctail"], A["cw"], A["nw"], A["hp8"]
        memset("dve", S[:, :, :], 0.0, [S.t])
        memset("dve", ctail[:, :, :], 0.0, [ctail.t])
        dma("sp", A["sel"][:, :, :], I["c_sel"].rearrange("k (h m) -> k h m", h=8), (), [A["sel"].t], A["sel"].t)
        dma("sp", A["sl64"][:, :], I["c_sl64"][:, :], (), [A["sl64"].t], A["sl64"].t)
        dma("sp", A["sl8"][:, :], I["c_sl8"][:, :], (), [A["sl8"].t], A["sl8"].t)
        for tap in range(4):
            dma("sp", cw[:, :, tap], I["a_conv_w"][tap].rearrange("(j p) -> p j", p=128), (), [cw.t], cw.t, slow=True)
        dma("sp", nw[:, :], I["a_norm_w"].rearrange("o d -> d o"), (), [nw.t], nw.t, slow=True)
        ts("dve", nw[:, :], nw[:, :], float(np.sqrt(128.0)), None, ALU.mult, None, [nw.t], [nw.t])
        dma("sp", hp8[:, 0:1], I["a_dt_bias"].rearrange("o h -> h o"), (), [hp8.t], hp8.t, slow=True)
        dma("sp", hp8[:, 1:2], I["a_log"].rearrange("o h -> h o"), (), [hp8.t], hp8.t, slow=True)
        act(hp8[:, 1:2], hp8[:, 1:2], AF.Exp, [hp8.t], [hp8.t])
        ts("dve", hp8[:, 1:2], hp8[:, 1:2], -1.0, None, ALU.mult, None, [hp8.t], [hp8.t])
        memset("dve", hp8[:, 2:3], 1.0, [hp8.t])
        cst = A["cst"]
        src = I["a_conv"].rearrange("s r c -> (s r) c")
        for sec in range(3):
            xt = xin[sec % 2]
            dma("sp", xt[0:12, 0:D], src[:, sec * 1024:(sec + 1) * 1024], (), [xt.t], xt.t)
            p_ = ps()
            for jj in range(8):
                tr(p_[:, jj * 12:(jj + 1) * 12], xt[0:12, jj * 128:(jj + 1) * 128], ident[0:12, 0:12], [xt.t, ident.t], [p_.t])
            evac(cst[:, sec * 8:(sec + 1) * 8, :, :], p_[:, 0:96].rearrange("p (j s r) -> p j s r", j=8, s=4), [p_.t], [cst.t])

    def mixer_a(sbk):
        A = MA
        S, ctail, ctail_s, cst, cw, nw, hp8, sel = A["S"], A["ctail"], A["ctail_s"], A["cst"], A["cw"], A["nw"], A["hp8"], A["sel"]
        bt, gt, GT = A["bt"], A["gt"], A["GT"]
        gtok, btok, Gtok, ekl, eGtok, bke = A["gtok"], A["btok"], A["Gtok"], A["ekl"], A["eGtok"], A["bke"]
        cv, zs, kT, kbT, qT, eGb = A["cv"], A["zs"], A["kT"], A["kbT"], A["qT"], A["eGb"]
        RHSk, RHSv, khat, att, TTb = A["RHSk"], A["RHSv"], A["khat"], A["att"], A["TTb"]
        Pm, nWT = A["Pm"], A["nWT"]
        w_in = I["w_in_a"]
        for (b, t0, n, off) in seg_list(sbk):
            prompt = b < 4
            NSEG, SEGL, L, NCH = (1, 512, 64, 8) if prompt else (4, 8, 8, 4)
            sl = A["sl64"] if prompt else A["sl8"]
            W8 = NCH * 8
            nlev = 5 if prompt else 2
            w, wv = wload(w_in[:, 4096:4112].rearrange("(k p) g -> p k g", p=128))
            pb = ps()
            for c in range(KC):
                mm(pb[0:8, 0:n], wv[:, c, 0:8], xn[:, c, off:off + n], c == 0, c == KC - 1, [w.t, xn.t], [pb.t])
            act(bt[0:8, 0:n], pb[0:8, 0:n], AF.Sigmoid, [pb.t], [bt.t])
            pa = ps()
            for c in range(KC):
                mm(pa[0:8, 0:n], wv[:, c, 8:16], xn[:, c, off:off + n], c == 0, c == KC - 1, [w.t, xn.t], [pa.t])
            act(gt[0:8, 0:n], pa[0:8, 0:n], AF.Exp, [pa.t, hp8.t], [gt.t], bias=hp8[:, 0:1])
            act(gt[0:8, 0:n], gt[0:8, 0:n], AF.Ln, [gt.t, hp8.t], [gt.t], bias=hp8[:, 2:3])
            ts("dve", gt[0:8, 0:n], gt[0:8, 0:n], hp8[:, 1:2], None, ALU.mult, None, [gt.t, hp8.t], [gt.t])
            pt1 = ps()
            for c in range(NCH):
                tr(pt1[0:L, c * 8:(c + 1) * 8], gt[0:8, c * L:(c + 1) * L], ident[0:8, 0:8], [gt.t, ident.t], [pt1.t])
            evac(gtok[0:L, 0:W8], pt1[0:L, 0:W8], [pt1.t], [gtok.t])
            pt2 = ps()
            for c in range(NCH):
                tr(pt2[0:L, c * 8:(c + 1) * 8], bt[0:8, c * L:(c + 1) * L], ident[0:8, 0:8], [bt.t, ident.t], [pt2.t])
            evac(btok[0:L, 0:W8], pt2[0:L, 0:W8], [pt2.t], [btok.t])
            pG = ps()
            mm(pG[0:L, 0:W8], triu_f[0:L, 0:L], gtok[0:L, 0:W8], True, True, [triu_f.t, gtok.t], [pG.t])
            evac(Gtok[0:L, 0:W8], pG[0:L, 0:W8], [pG.t], [Gtok.t])
            pGT = ps()
            for c in range(NCH):
                mm(pGT[0:8, c * L:(c + 1) * L], gtok[0:L, c * 8:(c + 1) * 8], triu_f[0:L, 0:L], True, True, [gtok.t, triu_f.t], [pGT.t])
            evac(GT[0:8, 0:n], pGT[0:8, 0:n], [pGT.t], [GT.t])
            pGl = ps()
            mm(pGl[0:L, 0:W8], sl[0:L, 0:L], Gtok[0:L, 0:W8], True, True, [sl.t, Gtok.t], [pGl.t])
            tt("dve", ekl[0:L, 0:W8], pGl[0:L, 0:W8], Gtok[0:L, 0:W8], ALU.subtract, [pGl.t, Gtok.t], [ekl.t])
            act(ekl[0:L, 0:W8], ekl[0:L, 0:W8], AF.Exp, [ekl.t], [ekl.t])
            act(eGtok[0:L, 0:W8], Gtok[0:L, 0:W8], AF.Exp, [Gtok.t], [eGtok.t])
            tt("dve", bke[0:L, 0:W8], btok[0:L, 0:W8], eGtok[0:L, 0:W8], ALU.mult, [btok.t, eGtok.t], [bke.t])

            def hcol(t_, h):
                return t_[0:L, 0:W8].rearrange("p (c h) -> p c h", h=8)[:, :, h].unsqueeze(2).to_broadcast([L, NCH, 128])

            for h in range(8):
                _wi[0] += 1
                w = WR[_wi[0] % NW]
                wv = w[:, 0:4096].rearrange("p (k s g) -> p k s g", k=8, s=4)
                for sec in range(4):
                    c0 = sec * 1024 + h * 128
                    dma("pool", wv[:, :, sec, :], w_in[:, c0:c0 + 128].rearrange("(k p) g -> p k g", p=128), (), [w.t], w.t)
                xx3 = Pm[:, 0:NSEG * (3 + SEGL)].rearrange("p (s l) -> p s l", s=NSEG)
                for sec in range(3):
                    j = sec * 8 + h
                    p_ = ps()
                    for c in range(KC):
                        mm(p_[:, 0:n], wv[:, c, sec, :], xn[:, c, off:off + n], c == 0, c == KC - 1, [w.t, xn.t], [p_.t])
                    if prompt:
                        cp("dve", xx3[:, 0, 0:3], ctail[:, j, :], [ctail.t], [Pm.t])
                    else:
                        cp("dve", xx3[:, :, 0:3], cst[:, j, :, :], [cst.t], [Pm.t])
                    evac(xx3[:, :, 3:3 + SEGL], p_[:, 0:n].rearrange("p (s l) -> p s l", s=NSEG), [p_.t], [Pm.t])
                    if prompt:
                        cp("dve", ctail[:, j, :], xx3[:, 0, SEGL:SEGL + 3], [Pm.t], [ctail.t])
                    else:
                        cp("dve", ctail_s[:, j, :, :], xx3[:, :, SEGL:SEGL + 3], [Pm.t], [ctail_s.t])
                    o3 = cv[sec][:, 0:n].rearrange("p (s l) -> p s l", s=NSEG)
                    ts("dve", o3, xx3[:, :, 0:SEGL], cw[:, j, 0:1], None, ALU.mult, None, [Pm.t, cw.t], [cv[sec].t])
                    for i in range(1, 4):
                        stt("dve", o3, xx3[:, :, i:i + SEGL], cw[:, j, i:i + 1], o3, ALU.mult, ALU.add, [Pm.t, cw.t, cv[sec].t], [cv[sec].t])
                    act(cv[sec][:, 0:n], cv[sec][:, 0:n], AF.Silu, [cv[sec].t], [cv[sec].t])
                p_ = ps()
                for c in range(KC):
                    mm(p_[:, 0:n], wv[:, c, 3, :], xn[:, c, off:off + n], c == 0, c == KC - 1, [w.t, xn.t], [p_.t])
                act(zs[:, 0:n], p_[:, 0:n], AF.Silu, [p_.t], [zs.t])
                for sec in range(2):
                    sq = sqs[sec]
                    act(sq[:, 0:n], cv[sec][:, 0:n], AF.Square, [cv[sec].t], [sq.t])
                    p2 = ps()
                    mm(p2[:, 0:n], ones_b[:, :], sq[:, 0:n], True, True, [ones_b.t, sq.t], [p2.t])
                    rsqrt_eps(rq[:, 0:n], p2[:, 0:n], EPS, [p2.t], [rq.t])
                    stt("dve", cv[sec][:, 0:n], cv[sec][:, 0:n], (128.0 ** -0.5) if sec == 0 else 1.0, rq[:, 0:n], ALU.mult, ALU.mult,
                        [cv[sec].t, rq.t], [cv[sec].t])
                qn, kn, vc = cv[0], cv[1], cv[2]
                dec = A["XTb"]
                pGb = ps()
                mm(pGb[:, 0:n], sel[0:8, h, :], GT[0:8, 0:n], True, True, [sel.t, GT.t], [pGb.t])
                for c in range(NCH):
                    ts("dve", dec[0:L, c * L:(c + 1) * L], pGb[0:L, c * L:(c + 1) * L], Gtok[0:L, c * 8 + h:c * 8 + h + 1], 0.0,
                       ALU.subtract, ALU.min, [pGb.t, Gtok.t], [dec.t])
                act(eGb[:, 0:n], pGb[:, 0:n], AF.Exp, [pGb.t], [eGb.t])
                act(dec[0:L, 0:n], dec[0:L, 0:n], AF.Exp, [dec.t], [dec.t])
                d3 = dec[0:L, 0:n].rearrange("p (c l) -> p c l", c=NCH)
                tt("dve", d3, d3, triu_f[0:L, 0:L].unsqueeze(1).to_broadcast([L, NCH, L]), ALU.mult, [dec.t, triu_f.t], [dec.t])
                pBb = ps()
                mm(pBb[:, 0:n], sel[0:8, h, :], bt[0:8, 0:n], True, True, [sel.t, bt.t], [pBb.t])
                tt("dve", kbT[:, 0:n], kn[:, 0:n], pBb[:, 0:n], ALU.mult, [kn.t, pBb.t], [kbT.t])
                cp("act", kT[:, 0:n], kn[:, 0:n], [kn.t], [kT.t])
                cp("act", qT[:, 0:n], qn[:, 0:n], [qn.t], [qT.t])
                for half in range((NCH + 3) // 4):
                    cs_ = list(range(half * 4, min(NCH, half * 4 + 4)))
                    pk = ps()
                    for ci, c in enumerate(cs_):
                        tr(pk[0:L, ci * 128:(ci + 1) * 128], kn[:, c * L:(c + 1) * L], ident[:, :], [kn.t, ident.t], [pk.t])
                    nc_ = len(cs_)
                    pk3 = pk[0:L, 0:nc_ * 128].rearrange("p (c d) -> p c d", c=nc_)
                    tt("dve", RHSk[0:L, cs_[0]:cs_[0] + nc_, :], pk3, hcol(bke, h)[:, cs_[0]:cs_[0] + nc_, :], ALU.mult, [pk.t, bke.t], [RHSk.t])
                    tt("dve", khat[0:L, cs_[0]:cs_[0] + nc_, :], pk3, hcol(ekl, h)[:, cs_[0]:cs_[0] + nc_, :], ALU.mult, [pk.t, ekl.t], [khat.t])
                    pv_ = ps()
                    for ci, c in enumerate(cs_):
                        tr(pv_[0:L, ci * 128:(ci + 1) * 128], vc[:, c * L:(c + 1) * L], ident[:, :], [vc.t, ident.t], [pv_.t])
                    pv3 = pv_[0:L, 0:nc_ * 128].rearrange("p (c d) -> p c d", c=nc_)
                    tt("dve", RHSv[0:L, cs_[0]:cs_[0] + nc_, :], pv3, hcol(btok, h)[:, cs_[0]:cs_[0] + nc_, :], ALU.mult, [pv_.t, btok.t], [RHSv.t])
                tt("dve", qn[:, 0:n], qn[:, 0:n], eGb[:, 0:n], ALU.mult, [qn.t, eGb.t], [qn.t])
                qtil = qn
                X, XT, X2, X2T = A["Xa"], A["XTa"], A["Xb"], A["XTb"]
                pKK = ps()
                for c in range(NCH):
                    mm(pKK[0:L, c * L:(c + 1) * L], kT[:, c * L:(c + 1) * L], kbT[:, c * L:(c + 1) * L], True, True, [kT.t, kbT.t], [pKK.t])
                tt("dve", X[0:L, 0:n], pKK[0:L, 0:n], dec[0:L, 0:n], ALU.mult, [pKK.t, dec.t], [X.t])
                x3 = X[0:L, 0:n].rearrange("p (c l) -> p c l", c=NCH)
                stt("dve", x3, x3, -1.0, striu_f[0:L, 0:L].unsqueeze(1).to_broadcast([L, NCH, L]), ALU.mult, ALU.mult, [X.t, striu_f.t], [X.t])
                pQK = ps()
                for c in range(NCH):
                    mm(pQK[0:L, c * L:(c + 1) * L], kT[:, c * L:(c + 1) * L], qT[:, c * L:(c + 1) * L], True, True, [kT.t, qT.t], [pQK.t])
                tt("dve", att[0:L, 0:n], pQK[0:L, 0:n], dec[0:L, 0:n], ALU.mult, [pQK.t, dec.t], [att.t])
                pXT = ps()
                for c in range(NCH):
                    tr(pXT[0:L, c * L:(c + 1) * L], X[0:L, c * L:(c + 1) * L], ident[0:L, 0:L], [X.t, ident.t], [pXT.t])
                evac(XT[0:L, 0:n], pXT[0:L, 0:n], [pXT.t], [XT.t])
                p3 = Pm[0:L, 0:n].rearrange("p (c l) -> p c l", c=NCH)
                tt("dve", p3, x3, ident[0:L, 0:L].unsqueeze(1).to_broadcast([L, NCH, L]), ALU.add, [X.t, ident.t], [Pm.t])
                for lev in range(nlev):
                    pX2 = ps()
                    for c in range(NCH):
                        mm(pX2[0:L, c * L:(c + 1) * L], XT[0:L, c * L:(c + 1) * L], X[0:L, c * L:(c + 1) * L], True, True, [XT.t, X.t], [pX2.t])
                    pX2T = ps()
                    for c in range(NCH):
                        mm(pX2T[0:L, c * L:(c + 1) * L], X[0:L, c * L:(c + 1) * L], XT[0:L, c * L:(c + 1) * L], True, True, [XT.t, X.t], [pX2T.t])
                    evac(X2[0:L, 0:n], pX2[0:L, 0:n], [pX2.t], [X2.t])
                    evac(X2T[0:L, 0:n], pX2T[0:L, 0:n], [pX2T.t], [X2T.t])
                    pP = ps()
                    for c in range(NCH):
                        mm(pP[0:L, c * L:(c + 1) * L], X2T[0:L, c * L:(c + 1) * L], Pm[0:L, c * L:(c + 1) * L], True, True, [X2T.t, Pm.t], [pP.t])
                    tt("dve", Pm[0:L, 0:n], Pm[0:L, 0:n], pP[0:L, 0:n], ALU.add, [Pm.t, pP.t], [Pm.t])
                    X, XT, X2, X2T = X2, X2T, X, XT
                cp("act", TTb[0:L, 0:n], Pm[0:L, 0:n], [Pm.t], [TTb.t])
                pW = ps()
                for c in range(NCH):
                    mm(pW[:, c * L:(c + 1) * L], RHSk[0:L, c, :], TTb[0:L, c * L:(c + 1) * L], True, True, [RHSk.t, TTb.t], [pW.t])
                ts("dve", nWT[:, 0:n], pW[:, 0:n], -1.0, None, ALU.mult, None, [pW.t], [nWT.t])
                po = psacc
                for c in range(NCH):
                    csl = slice(c * L, (c + 1) * L)
                    if prompt:
                        S_ap, S_tok = S[:, h, :], S.t
                    else:
                        Sl = A["Sl"][c % 2]
                        dma("sp", Sl[:, :], I["a_ssm"][c, h], (), [Sl.t], Sl.t)
                        S_ap, S_tok = Sl[:, :], Sl.t
                    u = A["u"][c % 2]
                    pu = ps()
                    mm(pu[0:L, 0:128], TTb[0:L, csl], RHSv[0:L, c, :], True, False, [TTb.t, RHSv.t], [pu.t])
                    mm(pu[0:L, 0:128], nWT[:, csl], S_ap, False, True, [nWT.t, S_tok], [pu.t])
                    evac(u[0:L, :], pu[0:L, 0:128], [pu.t], [u.t])
                    mm(po[:, csl], S_ap, qtil[:, csl], True, False, [S_tok, qtil.t], [po.t])
                    mm(po[:, csl], u[0:L, :], att[0:L, csl], False, True, [u.t, att.t], [po.t])
                    pS = ps()
                    mm(pS[:, 0:128], khat[0:L, c, :], u[0:L, :], True, True, [khat.t, u.t], [pS.t])
                    egl = eGb[:, c * L + L - 1:c * L + L]
                    if prompt:
                        stt("dve", S[:, h, :], S[:, h, :], egl, pS[:, 0:128], ALU.mult, ALU.add, [S.t, eGb.t, pS.t], [S.t])
                    else:
                        So = A["So"][c % 2]
                        stt("dve", So[:, :], S_ap, egl, pS[:, 0:128], ALU.mult, ALU.add, [S_tok, eGb.t, pS.t], [So.t])
                        dma("sp", O["a_ssm_s"][c, h], So[:, :], [So.t], [otok["a_ssm_s"]], So.t)
                sq = sqs[0]
                act(sq[:, 0:n], po[:, 0:n], AF.Square, [po.t], [sq.t])
                p2 = ps()
                mm(p2[:, 0:n], ones_b[:, :], sq[:, 0:n], True, True, [ones_b.t, sq.t], [p2.t])
                rsqrt_eps(rq[:, 0:n], p2[:, 0:n], 128.0 * EPS, [p2.t], [rq.t])
                stt("dve", rq[:, 0:n], po[:, 0:n], nw[:, 0:1], rq[:, 0:n], ALU.mult, ALU.mult, [po.t, nw.t, rq.t], [rq.t])
                tt("dve", moT[:, h, off:off + n], rq[:, 0:n], zs[:, 0:n], ALU.mult, [rq.t, zs.t], [moT.t])

    def finish_a():
        A = MA
        S, ctail, ctail_s = A["S"], A["ctail"], A["ctail_s"]
        dma("sp", O["a_ssm_p"].rearrange("h k v -> k h v"), S[:, :, :], [S.t], [otok["a_ssm_p"]], S.t)
        cnt = 0
        for s_ in range(5):
            for sec in range(3):
                xt = xin[cnt % 2]
                cnt += 1
                for half in range(2):
                    p_ = ps()
                    for jj in range(4):
                        j = sec * 8 + half * 4 + jj
                        src = ctail[:, j, :] if s_ == 4 else ctail_s[:, j, s_, :]
                        tr(p_[0:3, jj * 128:(jj + 1) * 128], src, ident[:, :], [ctail.t, ctail_s.t, ident.t], [p_.t])
                    evac(xt[0:3, half * 512:(half + 1) * 512], p_[0:3, :], [p_.t], [xt.t])
                if s_ == 4:
                    dma("sp", O["a_conv_p"][:, sec * 1024:(sec + 1) * 1024], xt[0:3, 0:D], [xt.t], [otok["a_conv_p"]], xt.t)
                else:
                    dma("sp", O["a_conv_s"][s_, :, sec * 1024:(sec + 1) * 1024], xt[0:3, 0:D], [xt.t], [otok["a_conv_s"]], xt.t)


    MB = {}

    def setup_b(l):
        arena_reset()
        B = MB
        B["S"] = aalloc("bS", [128, 8, 128])
        B["lb4"] = aalloc("lb4", [128, 8, 4])
        B["lbc"] = aalloc("lbc", [128, 8])
        B["oml"] = aalloc("oml", [128, 8])
        B["noml"] = aalloc("noml", [128, 8])
        B["nw"] = aalloc("bnw", [128, 1])
        B["sg"] = aalloc("sg", [128, 544])
        B["c0"] = aalloc("c0", [128, 544])
        B["c1"] = aalloc("c1", [128, 544])
        B["qf"] = aalloc("qf", [128, 544])
        B["kk"] = aalloc("kk", [128, 544])
        B["kh"] = aalloc("kh", [128, 544])
        B["iT"] = aalloc("iT", [128, 544])
        B["zs"] = aalloc("bzs", [128, 544], BF16)
        B["qtb"] = aalloc("qtb", [128, 544], BF16)
        B["ktb"] = aalloc("ktb", [128, 544], BF16)
        B["attm"] = aalloc("attm", [32, 512], BF16)
        B["vtok"] = aalloc("vtok", [32, 16, 128], BF16)
        B["ktok"] = aalloc("ktok", [32, 16, 128], BF16)
        B["eBl"] = aalloc("eBl", [128, 16])
        B["Sl"] = [aalloc("bSl%d" % i, [128, 128]) for i in range(2)]
        B["So"] = [aalloc("bSo%d" % i, [128, 128]) for i in range(2)]
        S, lb4, lbc, oml, noml, nw = B["S"], B["lb4"], B["lbc"], B["oml"], B["noml"], B["nw"]
        memset("dve", S[:, :, :], 0.0, [S.t])
        for li in range(4):
            dma("sp", lb4[:, :, li], I["hg_lb"][li].rearrange("(c p) -> p c", p=128), (), [lb4.t], lb4.t, slow=True)
        act(lb4[:, :, :], lb4[:, :, :], AF.Exp, [lb4.t], [lb4.t])
        P.add("dve", lambda e, o=oml[:, :], i_=lb4[:, :, :]: e.tensor_reduce(o, i_, AX.X, ALU.add), [lb4.t], [oml.t])
        P.add("dve", lambda e, o=oml[:, :]: e.reciprocal(o, o), [oml.t], [oml.t])
        memset("dve", lbc[:, :], 0.0, [lbc.t])
        for li in range(1, l + 1):
            tt("dve", lbc[:, :], lbc[:, :], lb4[:, :, li], ALU.add, [lbc.t, lb4.t], [lbc.t])
        tt("dve", lbc[:, :], lbc[:, :], oml[:, :], ALU.mult, [lbc.t, oml.t], [lbc.t])
        ts("dve", oml[:, :], lbc[:, :], -1.0, 1.0, ALU.mult, ALU.add, [lbc.t], [oml.t])
        ts("dve", noml[:, :], oml[:, :], -1.0, None, ALU.mult, None, [oml.t], [noml.t])
        dma("sp", nw[:, :], I["b_norm_w"].rearrange("o d -> d o"), (), [nw.t], nw.t, slow=True)
        ts("dve", nw[:, :], nw[:, :], float(np.sqrt(128.0)), None, ALU.mult, None, [nw.t], [nw.t])

    def mixer_b(sbk):
        B = MB
        S, lbc, oml, noml, nw = B["S"], B["lbc"], B["oml"], B["noml"], B["nw"]
        sg, c0, c1, qf, kk, kh, iT, zs, qtb, ktb = B["sg"], B["c0"], B["c1"], B["qf"], B["kk"], B["kh"], B["iT"], B["zs"], B["qtb"], B["ktb"]
        attm, vtok, ktok, eBl = B["attm"], B["vtok"], B["ktok"], B["eBl"]
        w_in = I["w_in_b"]
        for (b, t0, n, off) in seg_list(sbk):
            prompt = b < 4
            L, NCH = (32, 16) if prompt else (8, 4)
            for h in range(8):
                _wi[0] += 1
                w = WR[_wi[0] % NW]
                wv = w[:, 0:4096].rearrange("p (k s g) -> p k s g", k=8, s=4)
                for sec in range(4):
                    c0_ = sec * 1024 + h * 128
                    dma("pool", wv[:, :, sec, :], w_in[:, c0_:c0_ + 128].rearrange("(k p) g -> p k g", p=128), (), [w.t], w.t)

                def proj(sec):
                    p_ = ps()
                    for c in range(KC):
                        mm(p_[:, 0:n], wv[:, c, sec, :], xn[:, c, off:off + n], c == 0, c == KC - 1, [w.t, xn.t], [p_.t])
                    return p_
                p_ = proj(0)
                act(qf[:, 0:n], p_[:, 0:n], AF.Silu, [p_.t], [qf.t])
                p_ = proj(1)
                act(sg[:, 0:n], p_[:, 0:n], AF.Sigmoid, [p_.t], [sg.t])
                ts("dve", kk[:, 0:n], sg[:, 0:n], noml[:, h:h + 1], oml[:, h:h + 1], ALU.mult, ALU.add, [sg.t, noml.t, oml.t], [kk.t])
                ts("dve", c0[:, 0:n], sg[:, 0:n], oml[:, h:h + 1], lbc[:, h:h + 1], ALU.mult, ALU.add, [sg.t, oml.t, lbc.t], [c0.t])
                act(c0[:, 0:n], c0[:, 0:n], AF.Ln, [c0.t], [c0.t])
                p_ = proj(2)
                evac(iT[:, 0:n], p_[:, 0:n], [p_.t], [iT.t])
                p_ = proj(3)
                act(zs[:, 0:n], p_[:, 0:n], AF.Silu, [p_.t], [zs.t])
                src, dst = c0, c1
                k_ = 1
                while k_ < L:
                    s3 = src[:, 0:n].rearrange("p (c l) -> p c l", c=NCH)
                    d3 = dst[:, 0:n].rearrange("p (c l) -> p c l", c=NCH)
                    cp("act", d3[:, :, 0:k_], s3[:, :, 0:k_], [src.t], [dst.t])
                    tt("dve", d3[:, :, k_:L], s3[:, :, k_:L], s3[:, :, 0:L - k_], ALU.add, [src.t], [dst.t])
                    src, dst = dst, src
                    k_ *= 2
                Bc, tmp = src, dst
                Bc3 = Bc[:, 0:n].rearrange("p (c l) -> p c l", c=NCH)
                t3 = tmp[:, 0:n].rearrange("p (c l) -> p c l", c=NCH)
                tt("dve", t3, Bc3[:, :, L - 1:L].to_broadcast([128, NCH, L]), Bc3, ALU.subtract, [Bc.t], [tmp.t])
                act(tmp[:, 0:n], tmp[:, 0:n], AF.Exp, [tmp.t], [tmp.t])
                tt("dve", kh[:, 0:n], kk[:, 0:n], tmp[:, 0:n], ALU.mult, [kk.t, tmp.t], [kh.t])
                act(eBl[:, 0:NCH], Bc3[:, :, L - 1], AF.Exp, [Bc.t], [eBl.t])
                act(tmp[:, 0:n], Bc[:, 0:n], AF.Exp, [Bc.t], [tmp.t])
                stt("dve", qf[:, 0:n], qf[:, 0:n], 128.0 ** -0.5, tmp[:, 0:n], ALU.mult, ALU.mult, [qf.t, tmp.t], [qf.t])
                cp("act", qtb[:, 0:n], qf[:, 0:n], [qf.t], [qtb.t])
                act(tmp[:, 0:n], Bc[:, 0:n], AF.Exp, [Bc.t], [tmp.t], scale=-1.0)
                tt("dve", ktb[:, 0:n], kk[:, 0:n], tmp[:, 0:n], ALU.mult, [kk.t, tmp.t], [ktb.t])
                for half in range((NCH * L + 511) // 512):
                    pass
                pA = ps()
                for c in range(NCH):
                    mm(pA[0:L, c * L:(c + 1) * L], ktb[:, c * L:(c + 1) * L], qtb[:, c * L:(c + 1) * L], True, True, [ktb.t, qtb.t], [pA.t])
                tt("dve", attm[0:L, 0:n].rearrange("p (c l) -> p c l", c=NCH), pA[0:L, 0:n].rearrange("p (c l) -> p c l", c=NCH),
                   triu_f[0:L, 0:L].unsqueeze(1).to_broadcast([L, NCH, L]), ALU.mult, [pA.t, triu_f.t], [attm.t])
                for q4 in range((NCH + 3) // 4):
                    cs_ = list(range(q4 * 4, min(NCH, q4 * 4 + 4)))
                    nc_ = len(cs_)
                    pk = ps()
                    for ci, c in enumerate(cs_):
                        tr(pk[0:L, ci * 128:(ci + 1) * 128], kh[:, c * L:(c + 1) * L], ident[:, :], [kh.t, ident.t], [pk.t])
                    evac(ktok[0:L, cs_[0]:cs_[0] + nc_, :], pk[0:L, 0:nc_ * 128].rearrange("p (c d) -> p c d", c=nc_), [pk.t], [ktok.t])
                    pv_ = ps()
                    for ci, c in enumerate(cs_):
                        tr(pv_[0:L, ci * 128:(ci + 1) * 128], iT[:, c * L:(c + 1) * L], ident[:, :], [iT.t, ident.t], [pv_.t])
                    evac(vtok[0:L, cs_[0]:cs_[0] + nc_, :], pv_[0:L, 0:nc_ * 128].rearrange("p (c d) -> p c d", c=nc_), [pv_.t], [vtok.t])
                po = psacc
                for c in range(NCH):
                    csl = slice(c * L, (c + 1) * L)
                    if prompt:
                        S_ap, S_tok = S[:, h, :], S.t
                    else:
                        Sl = B["Sl"][c % 2]
                        dma("sp", Sl[:, :], I["b_ssm"][c, h], (), [Sl.t], Sl.t)
                        S_ap, S_tok = Sl[:, :], Sl.t
                    mm(po[:, csl], S_ap, qf[:, csl], True, False, [S_tok, qf.t], [po.t])
                    mm(po[:, csl], vtok[0:L, c, :], attm[0:L, csl], False, True, [vtok.t, attm.t], [po.t])
                    pS = ps()
                    mm(pS[:, 0:128], ktok[0:L, c, :], vtok[0:L, c, :], True, True, [ktok.t, vtok.t], [pS.t])
                    if prompt:
                        stt("dve", S[:, h, :], S[:, h, :], eBl[:, c:c + 1], pS[:, 0:128], ALU.mult, ALU.add, [S.t, eBl.t, pS.t], [S.t])
                    else:
                        So = B["So"][c % 2]
                        stt("dve", So[:, :], S_ap, eBl[:, c:c + 1], pS[:, 0:128], ALU.mult, ALU.add, [S_tok, eBl.t, pS.t], [So.t])
                        dma("sp", O["b_ssm_s"][c, h], So[:, :], [So.t], [otok["b_ssm_s"]], So.t)
                sq = sqs[0]
                act(sq[:, 0:n], po[:, 0:n], AF.Square, [po.t], [sq.t])
                p2 = ps()
                mm(p2[:, 0:n], ones_b[:, :], sq[:, 0:n], True, True, [ones_b.t, sq.t], [p2.t])
                rsqrt_eps(rq[:, 0:n], p2[:, 0:n], 128.0 * EPS, [p2.t], [rq.t])
                stt("dve", rq[:, 0:n], po[:, 0:n], nw[:, 0:1], rq[:, 0:n], ALU.mult, ALU.mult, [po.t, nw.t, rq.t], [rq.t])
                tt("dve", moT[:, h, off:off + n], rq[:, 0:n], zs[:, 0:n], ALU.mult, [rq.t, zs.t], [moT.t])

    def finish_b():
        S = MB["S"]
        dma("sp", O["b_ssm_p"].rearrange("h k v -> k h v"), S[:, :, :], [S.t], [otok["b_ssm_p"]], S.t)


    MC = {}

    def setup_c():
        arena_reset()
        C = MC
        C["qgc"] = aalloc("qgc", [128, 1])
        C["kgc"] = aalloc("kgc", [128, 1])
        C["kgb"] = aalloc("kgb", [128, 64])
        C["fb"] = aalloc("fb", [128, 16])
        C["lf"] = aalloc("lf", [128, 17, 16])
        C["Ft"] = aalloc("Ft", [128, 16, 16])
        C["Pre"] = aalloc("Pre", [128, 17, 16])
        C["bias"] = aalloc("bias", [128, 16])
        C["off_q"] = ast["off"]
        C["qTn"] = aalloc("qTn", [128, 8, 544], BF16)
        C["ogs"] = aalloc("ogs", [128, 8, 544], BF16)
        C["ksq"] = T(P, aT.h[:, :, :].rearrange("p a b -> p (a b)").bitcast(F32)[:, 0:512], "cksq")
        C["ksq"].t = aT.t
        C["kss"] = aalloc("ckss", [128, 16])
        C["pt"] = [aalloc("cpt%d" % i, [128, 128], BF16) for i in range(3)]
        C["Va"] = [aalloc("Va%d" % i, [128, 16, 65], BF16) for i in range(2)]
        C["ot"] = xin[0]
        C["rec"] = aalloc("rec", [128, 16])
        C["lfh"] = aalloc("lfh", [128, 16], BF16)
        C["lfl"] = aalloc("lfl", [128, 16], BF16)
        C["lft"] = aalloc("lft", [128, 16])
        C["off_kt"] = ast["off"]
        C["KT"] = aalloc("KT", [128, 8, 2048], BF16)
        C["off_kts"] = ast["off"]
        C["kTs"] = aalloc("kTs", [128, 8, 32], BF16)
        C["qTs"] = aalloc("qTs", [128, 8, 32], BF16)
        C["ogss"] = aalloc("ogss", [128, 8, 32], BF16)
        qgc, kgc, kgb, fb, Pre = C["qgc"], C["kgc"], C["kgb"], C["fb"], C["Pre"]
        for half in range(2):
            dma("sp", qgc[half * 64:(half + 1) * 64, :], I["c_qnorm"].rearrange("o d -> d o"), (), [qgc.t], qgc.t, slow=True)
            dma("sp", kgc[half * 64:(half + 1) * 64, :], I["c_knorm"].rearrange("o d -> d o"), (), [kgc.t], kgc.t, slow=True)
        ts("dve", kgc[:, :], kgc[:, :], 8.0, None, ALU.mult, None, [kgc.t], [kgc.t])
        dma("sp", kgb[:, :], I["c_knorm"].partition_broadcast(128), (), [kgb.t], kgb.t, slow=True)
        ts("dve", kgb[:, :], kgb[:, :], 8.0, None, ALU.mult, None, [kgb.t], [kgb.t])
        dma("sp", fb[:, :], I["c_fbias"].partition_broadcast(128), (), [fb.t], fb.t, slow=True)
        memset("dve", Pre[:, :, :], 0.0, [Pre.t])
        for v_ in C["Va"]:
            memset("dve", v_[:, :, :], 1.0, [v_.t])
        C["vtok"] = [P.tok("cv_tile%d" % j) for j in range(17)]
        C["vi"] = 0

    def mixer_c(sbk):
        C = MC
        qgc, kgc, kgb, fb, lf, Ft, Pre, bias = C["qgc"], C["kgc"], C["kgb"], C["fb"], C["lf"], C["Ft"], C["Pre"], C["bias"]
        qTn, ogs, ksq, kss, KT, kTs, ot, rec = C["qTn"], C["ogs"], C["ksq"], C["kss"], C["KT"], C["kTs"], C["ot"], C["rec"]
        w_in = I["w_in_c"]
        segs = seg_list(sbk)
        PSROT[0] = 2
        psets = [[PS[2], PS[3], PS[4]], [PS[5], PS[6], PS[7]]]
        accs = xin[1]
        npair = [0]
        for sec in range(2):
            for g in range(2):
                w, wv = wload(w_in[:, sec * 1024 + g * 512:sec * 1024 + (g + 1) * 512].rearrange("(k p) g -> p k g", p=128))
                for m in range(4):
                    hp = g * 4 + m
                    for (b, t0, n, off) in segs:
                        p_ = ps()
                        for c in range(KC):
                            mm(p_[:, 0:n], wv[:, c, m * 128:(m + 1) * 128], xn[:, c, off:off + n], c == 0, c == KC - 1, [w.t, xn.t], [p_.t])
                        act(sq2[:, 0:n], p_[:, 0:n], AF.Square, [p_.t], [sq2.t])
                        p2 = ps()
                        mm(p2[:, 0:n], blk_b[:, :], sq2[:, 0:n], True, True, [blk_b.t, sq2.t], [p2.t])
                        rsqrt_eps(rq[:, 0:n], p2[:, 0:n], 64.0 * EPS, [p2.t], [rq.t])
                        if sec == 0:
                            stt("dve", qTn[:, hp, off:off + n], p_[:, 0:n], qgc[:, 0:1], rq[:, 0:n], ALU.mult, ALU.mult, [p_.t, qgc.t, rq.t], [qTn.t])
                        elif b < 4:
                            stt("dve", KT[:, hp, t0:t0 + n], p_[:, 0:n], kgc[:, 0:1], rq[:, 0:n], ALU.mult, ALU.mult, [p_.t, kgc.t, rq.t], [KT.t])
                        else:
                            stt("dve", kTs[:, hp, 0:n], p_[:, 0:n], kgc[:, 0:1], rq[:, 0:n], ALU.mult, ALU.mult, [p_.t, kgc.t, rq.t], [kTs.t])
        for g in range(2):
            w, wv = wload(w_in[:, 3072 + g * 512:3072 + (g + 1) * 512].rearrange("(k p) g -> p k g", p=128))
            for m in range(4):
                for (b, t0, n, off) in segs:
                    p_ = ps()
                    for c in range(KC):
                        mm(p_[:, 0:n], wv[:, c, m * 128:(m + 1) * 128], xn[:, c, off:off + n], c == 0, c == KC - 1, [w.t, xn.t], [p_.t])
                    act(ogs[:, g * 4 + m, off:off + n], p_[:, 0:n], AF.Sigmoid, [p_.t], [ogs.t])
        wk0, wk0v = wload(w_in[:, 1024:1536].rearrange("(k p) g -> p k g", p=128))
        wk1, wk1v = wload(w_in[:, 1536:2048].rearrange("(k p) g -> p k g", p=128))
        tiles = []
        for (b, t0, n, off) in segs:
            for i in range((n + 127) // 128):
                nt = min(128, n - i * 128)
                tiles.append((b, t0 + i * 128, off + i * 128, nt))
        for (b, tt0, toff, nt) in tiles:
            xt = xin[C["vi"] % 2]
            C["vi"] += 1
            for hf, (w, wv) in enumerate(((wk0, wk0v), (wk1, wk1v))):
                p_ = ps()
                for c in range(KC):
                    mm(p_[0:nt, :], xn[:, c, toff:toff + nt], wv[:, c, :], c == 0, c == KC - 1, [xn.t, w.t], [p_.t])
                act(ksq[0:nt, :], p_[0:nt, :], AF.Square, [p_.t], [ksq.t])
                P.add("dve", lambda e, o=kss[0:nt, 0:8], i_=ksq[0:nt, :].rearrange("p (h d) -> p h d", h=8): e.tensor_reduce(o, i_, AX.X, ALU.add),
                      [ksq.t], [kss.t])
                rsqrt_eps(kss[0:nt, 0:8], kss[0:nt, 0:8], 64.0 * EPS, [kss.t], [kss.t])
                o3 = xt[0:nt, hf * 512:(hf + 1) * 512].rearrange("p (h d) -> p h d", h=8)
                tt("dve", o3, p_[0:nt, :].rearrange("p (h d) -> p h d", h=8), kss[0:nt, 0:8].unsqueeze(2).to_broadcast([nt, 8, 64]), ALU.mult,
                   [p_.t, kss.t], [xt.t])
                tt("dve", o3, o3, kgb[0:nt, :].unsqueeze(1).to_broadcast([nt, 8, 64]), ALU.mult, [xt.t, kgb.t], [xt.t])
            if b < 4:
                dma("sp", O["c_k_p"][tt0:tt0 + nt, :], xt[0:nt, 0:D], [xt.t], [otok["c_k_p"]], xt.t)
            else:
                dma("sp", O["c_k_s"][:, :], xt[0:nt, 0:D], [xt.t], [otok["c_k_s"]], xt.t)
        wv0, wv0v = wload(w_in[:, 2048:2560].rearrange("(k p) g -> p k g", p=128))
        wv1, wv1v = wload(w_in[:, 2560:3072].rearrange("(k p) g -> p k g", p=128))
        for (b, tt0, toff, nt) in tiles:
            xt = xin[C["vi"] % 2]
            C["vi"] += 1
            for hf, (w, wv) in enumerate(((wv0, wv0v), (wv1, wv1v))):
                p_ = ps()
                for c in range(KC):
                    mm(p_[0:nt, :], xn[:, c, toff:toff + nt], wv[:, c, :], c == 0, c == KC - 1, [xn.t, w.t], [p_.t])
                evac(xt[0:nt, hf * 512:(hf + 1) * 512], p_[0:nt, :], [p_.t], [xt.t])
            if b < 4:
                dma("sp", O["c_v_p"][tt0:tt0 + nt, :], xt[0:nt, 0:D], [xt.t], [C["vtok"][tt0 // 128]], xt.t)
            else:
                dma("sp", O["c_v_s"][:, :], xt[0:nt, 0:D], [xt.t], [C["vtok"][16]], xt.t)
        wf, wfv = wload(w_in[:, 4096:4112].rearrange("(k p) g -> p k g", p=128))
        for (b, tt0, toff, nt) in tiles:
            ti = tt0 // 128
            p_ = ps()
            for c in range(KC):
                mm(p_[0:nt, 0:16], xn[:, c, toff:toff + nt], wfv[:, c, :], c == 0, c == KC - 1, [xn.t, wf.t], [p_.t])
            tt("dve", lf[0:nt, ti, :], p_[0:nt, 0:16], fb[0:nt, :], ALU.add, [p_.t, fb.t], [lf.t])
            act(lf[0:nt, ti, :], lf[0:nt, ti, :], AF.Sigmoid, [lf.t], [lf.t])
            act(lf[0:nt, ti, :], lf[0:nt, ti, :], AF.Ln, [lf.t], [lf.t])
            if b < 4:
                dma("sp", O["c_lf_p"][tt0:tt0 + nt, :], lf[0:nt, ti, :], [lf.t], [otok["c_lf_p"]], lf.t)
                lfh, lfl, lft = C["lfh"], C["lfl"], C["lft"]
                cp("dve", lfh[:, :], lf[:, ti, :], [lf.t], [lfh.t])
                cp("dve", lft[:, :], lfh[:, :], [lfh.t], [lft.t])
                tt("dve", lft[:, :], lf[:, ti, :], lft[:, :], ALU.subtract, [lf.t, lft.t], [lft.t])
                cp("dve", lfl[:, :], lft[:, :], [lft.t], [lfl.t])
                pF = ps()
                mm(pF[:, 0:16], triu_b[:, :], lfh[:, :], True, False, [triu_b.t, lfh.t], [pF.t])
                mm(pF[:, 0:16], triu_b[:, :], lfl[:, :], False, True, [triu_b.t, lfl.t], [pF.t])
                tt("dve", Ft[:, ti, :], pF[:, 0:16], Pre[:, ti, :], ALU.add, [pF.t, Pre.t], [Ft.t])
                pT = ps()
                mm(pT[:, 0:16], ones_b[:, :], lfh[:, :], True, False, [ones_b.t, lfh.t], [pT.t])
                mm(pT[:, 0:16], ones_b[:, :], lfl[:, :], False, True, [ones_b.t, lfl.t], [pT.t])
                tt("dve", Pre[:, ti + 1, :], pT[:, 0:16], Pre[:, ti, :], ALU.add, [pT.t, Pre.t], [Pre.t])
            else:
                dma("sp", O["c_lf_s"][:, :], lf[0:nt, ti, :], [lf.t], [otok["c_lf_s"]], lf.t)
        for (b, tt0, toff, nt) in tiles:
            if b == 4:
                continue
            i = tt0 // 128
            for j in range(i + 1):
                Va = C["Va"][(C["vi"]) % 2]
                C["vi"] += 1
                dma("pool", Va[:, :, 0:64], O["c_v_p"][j * 128:(j + 1) * 128, :].rearrange("t (h d) -> t h d", h=16),
                    [C["vtok"][j]], [Va.t], Va.t)
                tt("dve", bias[:, :], Pre[:, i, :], Ft[:, j, :], ALU.subtract, [Pre.t, Ft.t], [bias.t])
                Aset = psets[npair[0] % 2]
                npair[0] += 1
                for hh in range(16):
                    hp, lo = hh // 2, (hh % 2) * 64
                    p_ = ps()
                    mm(p_[:, 0:128], KT[lo:lo + 64, hp, j * 128:(j + 1) * 128], qTn[lo:lo + 64, hp, toff:toff + 128], True, True,
                       [KT.t, qTn.t], [p_.t])
                    pt_ = C["pt"][hh % 3]
                    act(pt_[:, :], p_[:, 0:128], AF.Exp, [p_.t, bias.t], [pt_.t], bias=bias[:, hh:hh + 1])
                    if j == i:
                        tt("dve", pt_[:, :], pt_[:, :], triu_f[:, :], ALU.mult, [pt_.t, triu_f.t], [pt_.t])
                    a_ = Aset[hh // 7]
                    col = (hh % 7) * 65
                    mm(a_[:, col:col + 65], pt_[:, :], Va[:, hh, :], True, True, [pt_.t, Va.t], [a_.t])
                for bk in range(3):
                    ncol = 455 if bk < 2 else 130
                    if j == 0:
                        cp("dve", accs[:, bk * 455:bk * 455 + ncol], Aset[bk][:, 0:ncol], [Aset[bk].t], [accs.t])
                    else:
                        tt("dve", accs[:, bk * 455:bk * 455 + ncol], accs[:, bk * 455:bk * 455 + ncol], Aset[bk][:, 0:ncol], ALU.add,
                           [Aset[bk].t, accs.t], [accs.t])
            a3 = accs[:, 0:1040].rearrange("p (h d) -> p h d", h=16)
            P.add("dve", lambda e, o=rec[:, :].unsqueeze(2), i_=a3[:, :, 64:65]: e.reciprocal(o, i_), [accs.t], [rec.t])
            tt("dve", ot[:, 0:1024].rearrange("p (h d) -> p h d", h=16), a3[:, :, 0:64], rec[:, :].unsqueeze(2).to_broadcast([128, 16, 64]), ALU.mult,
               [accs.t, rec.t], [ot.t])
            if DEBUG and i == 1:
                dma("sp", O["dbg2"][:, :], ot[:, :], [ot.t], [dbgtok], ot.t)
                dma("sp", O["dbg1"][:, 0:272], Pre[:, :, :].rearrange("p a b -> p (a b)"), [Pre.t], [dbgtok], Pre.t)
                dma("sp", O["dbg1"][:, 272:528], Ft[:, :, :].rearrange("p a b -> p (a b)"), [Ft.t], [dbgtok], Ft.t)
            for half in range(2):
                p_ = ps()
                for jj in range(4):
                    hp = half * 4 + jj
                    tr(p_[:, jj * 128:(jj + 1) * 128], ot[:, hp * 128:(hp + 1) * 128], ident[:, :], [ot.t, ident.t], [p_.t])
                tt("dve", moT[:, half * 4:half * 4 + 4, toff:toff + 128], p_[:, :].rearrange("p (j n) -> p j n", j=4),
                   ogs[:, half * 4:half * 4 + 4, toff:toff + 128], ALU.mult, [p_.t, ogs.t], [moT.t])
        PSROT[0] = 7
        if sbk[-1] == 4 and not globals().get("SKIP_C_SAMPLE", False):
            mixer_c_sample(sbk)

    def mixer_c_sample(sbk):
        C = MC
        (b, t0, n, off) = seg_list(sbk)[-1]
        qTn, ogs, kTs, qTs, ogss, lf = C["qTn"], C["ogs"], C["kTs"], C["qTs"], C["ogss"], C["lf"]
        cp("dve", qTs[:, :, :], qTn[:, :, off:off + 32], [qTn.t], [qTs.t])
        cp("dve", ogss[:, :, :], ogs[:, :, off:off + 32], [ogs.t], [ogss.t])
        ast["prev"] = ast["prev"] + list(ast["cur"])
        save_off = ast["off"]
        ast["off"] = C["off_q"]
        lfp = aalloc("lfp", [128, 64, 16])
        lph = aalloc("lph", [128, 1024], BF16)
        lpl = aalloc("lpl", [128, 1024], BF16)
        totA = aalloc("totA", [128, 64, 16])
        kst = [aalloc("kst%d" % i, [128, 1024]) for i in range(2)]
        totB = T(P, kst[0].h[:, :].rearrange("p (a b) -> p a b", a=64), "totB")
        totB.t = kst[0].t
        vst = [aalloc("vst%d" % i, [128, 1024]) for i in range(2)]
        KTp = [aalloc("KTp%d" % i, [128, 8, 128], BF16) for i in range(2)]
        Vap = [aalloc("Vap%d" % i, [128, 16, 65], BF16) for i in range(2)]
        ptp = [aalloc("ptp%d" % i, [128, 16, 32], BF16) for i in range(2)]
        stmp = aalloc("stmp", [128, 16, 32])
        smask = aalloc("smask", [128, 4, 32], BF16)
        bd32 = aalloc("bd32", [32, 32])
        ptb = aalloc("ptb", [128, 64], I32)
        ptf = aalloc("ptf", [128, 64])
        idx = aalloc("idx", [128, 64], I32)
        vnew = aalloc("vnew", [32, 16, 65], BF16)
        nfn = aalloc("nfn", [32, 16])
        ptn = aalloc("ptn", [32, 16, 32], BF16)
        rec8 = aalloc("rec8", [32, 16])
        ot8 = aalloc("ot8", [32, 1024])
        lf32b = aalloc("lf32b", [32, 16], BF16)
        qzs = aalloc("qzs", [128, 16, 32], BF16)
        assert ast["off"] <= C["off_kts"], ("sample arena overflow", ast["off"], C["off_kts"])
        for v_ in Vap:
            memset("dve", v_[:, :, :], 1.0, [v_.t])
        memset("dve", vnew[:, :, :], 1.0, [vnew.t])
        cp("dve", lf32b[:, :], lf[0:32, 16, :], [lf.t], [lf32b.t])
        memset("dve", qzs[:, :, :], 0.0, [qzs.t])
        qz4 = qzs[:, :, :].rearrange("p (a two) q -> p a two q", two=2)
        cp("dve", qz4[0:64, :, 0, :], qTs[0:64, :, :], [qTs.t], [qzs.t])
        cp("dve", qz4[64:128, :, 1, :], qTs[64:128, :, :], [qTs.t], [qzs.t])
        dma("sp", stmp[:, 0:4, :], I["c_smask"].partition_broadcast(128), (), [stmp.t], stmp.t, slow=True)
        cp("dve", smask[:, :, :], stmp[:, 0:4, :], [stmp.t], [smask.t])
        dma("sp", bd32[:, :], I["c_bd32"][:, :], (), [bd32.t], bd32.t)
        PSROT[0] = 2
        psets = [[PS[2], PS[3], PS[4]], [PS[5], PS[6], PS[7]]]
        acc8 = xin[1]
        cnt = [0]
        first = [True]

        def accumulate(Aset):
            for bk in range(3):
                ncol = 455 if bk < 2 else 130
                if first[0]:
                    cp("dve", acc8[0:32, bk * 455:bk * 455 + ncol], Aset[bk][0:32, 0:ncol], [Aset[bk].t], [acc8.t])
                else:
                    tt("dve", acc8[0:32, bk * 455:bk * 455 + ncol], acc8[0:32, bk * 455:bk * 455 + ncol], Aset[bk][0:32, 0:ncol], ALU.add,
                       [Aset[bk].t, acc8.t], [acc8.t])
            first[0] = False

        for s_ in range(4):
            dma("sp", ptb[:, :], I["pt"][s_].partition_broadcast(128), (), [ptb.t], ptb.t, slow=True)
            cp("dve", ptf[:, :], ptb[:, :], [ptb.t], [ptf.t])
            ts("dve", ptf[:, :], ptf[:, :], 128.0, iota_f[:, 0:1], ALU.mult, ALU.add, [ptf.t, iota_f.t], [ptf.t])
            cp("dve", idx[:, :], ptf[:, :], [ptf.t], [idx.t])
            for pg in range(64):
                P.add("pool", lambda e, o=lfp[:, :, :].rearrange("p a b -> p (a b)")[:, pg * 16:(pg + 1) * 16], ix=idx[:, pg:pg + 1]: e.indirect_dma_start(
                    out=o, out_offset=None, in_=I["clf"][:, :], in_offset=bass.IndirectOffsetOnAxis(ap=ix, axis=0)),
                    [idx.t], [lfp.t], dma_tok=lfp.t)
            lfp2 = lfp[:, :, :].rearrange("p a b -> p (a b)")
            tA2 = totA[:, :, :].rearrange("p a b -> p (a b)")
            cp("dve", lph[:, :], lfp2, [lfp.t], [lph.t])
            cp("dve", tA2, lph[:, :], [lph.t], [totA.t])
            tt("dve", tA2, lfp2, tA2, ALU.subtract, [lfp.t, totA.t], [totA.t])
            cp("dve", lpl[:, :], tA2, [totA.t], [lpl.t])
            for hf in range(2):
                p_ = ps()
                mm(p_[:, :], ones_b[:, :], lph[:, hf * 512:(hf + 1) * 512], True, False, [ones_b.t, lph.t], [p_.t])
                mm(p_[:, :], ones_b[:, :], lpl[:, hf * 512:(hf + 1) * 512], False, True, [ones_b.t, lpl.t], [p_.t])
                evac(totA[:, hf * 32:(hf + 1) * 32, :], p_[:, :].rearrange("p (a b) -> p a b", a=32), [p_.t], [totA.t])
            src, dst = totA, totB
            k_ = 1
            while k_ < 64:
                cp("act", dst[:, 64 - k_:64, :], src[:, 64 - k_:64, :], [src.t], [dst.t])
                tt("dve", dst[:, 0:64 - k_, :], src[:, 0:64 - k_, :], src[:, k_:64, :], ALU.add, [src.t], [dst.t])
                src, dst = dst, src
                k_ *= 2
            incl = src
            assert incl is totA
            for hf in range(2):
                p_ = ps()
                mm(p_[:, :], stril_b[:, :], lph[:, hf * 512:(hf + 1) * 512], True, False, [stril_b.t, lph.t], [p_.t])
                mm(p_[:, :], stril_b[:, :], lpl[:, hf * 512:(hf + 1) * 512], False, True, [stril_b.t, lpl.t], [p_.t])
                p3 = p_[:, :].rearrange("p (a b) -> p a b", a=32)
                if hf == 0:
                    tt("dve", lfp[:, 0:32, :], p3, incl[:, 1:33, :], ALU.add, [p_.t, incl.t], [lfp.t])
                else:
                    tt("dve", lfp[:, 32:63, :], p3[:, 0:31, :], incl[:, 33:64, :], ALU.add, [p_.t, incl.t], [lfp.t])
                    cp("dve", lfp[:, 63:64, :], p3[:, 31:32, :], [p_.t], [lfp.t])
            rest = lfp
            SLV = globals().get("SLEVEL", "Z")
            for pg in range(64 if SLV != "P1" else 0):
                ci = cnt[0]
                cnt[0] += 1
                ks, vs, kt, va, pp = kst[ci % 2], vst[ci % 2], KTp[ci % 2], Vap[ci % 2], ptp[ci % 2]
                P.add("pool", lambda e, o=ks[:, :], ix=idx[:, pg:pg + 1]: e.indirect_dma_start(
                    out=o, out_offset=None, in_=I["ck"][:, :], in_offset=bass.IndirectOffsetOnAxis(ap=ix, axis=0)),
                    [idx.t], [ks.t], dma_tok=ks.t)
                P.add("pool", lambda e, o=vs[:, :], ix=idx[:, pg:pg + 1]: e.indirect_dma_start(
                    out=o, out_offset=None, in_=I["cv"][:, :], in_offset=bass.IndirectOffsetOnAxis(ap=ix, axis=0)),
                    [idx.t], [vs.t], dma_tok=vs.t)
                if SLV == "P11":
                    continue
                for half in range(2):
                    p_ = ps()
                    for jj in range(4):
                        hp = half * 4 + jj
                        tr(p_[:, jj * 128:(jj + 1) * 128], ks[:, hp * 128:(hp + 1) * 128], ident[:, :], [ks.t, ident.t], [p_.t])
                    evac(kt[:, half * 4:half * 4 + 4, :], p_[:, :].rearrange("p (j n) -> p j n", j=4), [p_.t], [kt.t])
                if SLV == "P12":
                    continue
                cp("dve" if pg % 2 else "act", va[:, :, 0:64], vs[:, :].rearrange("p (h d) -> p h d", h=16), [vs.t], [va.t])
                if SLV == "P13":
                    continue
                pS = ps()
                for hh in range(16):
                    hp, lo = hh // 2, (hh % 2) * 64
                    mm(pS[:, hh * 32:(hh + 1) * 32], kt[:, hp, :], qzs[:, hh, :], True, True, [kt.t, qzs.t], [pS.t])
                if SLV == "P15":
                    continue
                tt("dve", stmp[:, :, :], pS[:, :].rearrange("p (h q) -> p h q", h=16),
                   rest[:, pg, :].unsqueeze(2).to_broadcast([128, 16, 32]), ALU.add, [pS.t, rest.t], [stmp.t])
                if SLV == "P17":
                    continue
                act(pp[:, :, :], stmp[:, :, :], AF.Exp, [stmp.t], [pp.t])
                if SLV == "P2":
                    continue
                tt("dve", pp[:, :, :], pp[:, :, :], smask[:, s_, :].unsqueeze(1).to_broadcast([128, 16, 32]), ALU.mult, [pp.t, smask.t], [pp.t])
                if SLV == "P25":
                    continue
                Aset = psets[ci % 2]
                for hh in range(16):
                    a_ = Aset[hh // 7]
                    col = (hh % 7) * 65
                    mm(a_[0:32, col:col + 65], pp[:, hh, :], va[:, hh, :], True, True, [pp.t, va.t], [a_.t])
                if SLV == "P3":
                    continue
                accumulate(Aset)
        if SLV in ("P1", "P11", "P12", "P13", "P15", "P17", "P2", "P25", "P3", "P4"):
            PSROT[0] = 7
            ast["off"] = save_off
            return
        dma("pool", vnew[:, :, 0:64], O["c_v_s"][:, :].rearrange("t (h d) -> t h d", h=16), [C["vtok"][16]], [vnew.t], vnew.t)
        if globals().get("SLEVEL", "Z") == "A":
            PSROT[0] = 7
            ast["off"] = save_off
            return
        pf_ = ps()
        mm(pf_[0:32, 0:16], msel[:, :], lf32b[:, :], True, True, [msel.t, lf32b.t], [pf_.t])
        ts("dve", nfn[:, :], pf_[0:32, 0:16], -1.0, None, ALU.mult, None, [pf_.t], [nfn.t])
        if globals().get("SLEVEL", "Z") == "B":
            PSROT[0] = 7
            ast["off"] = save_off
            return
        pS = ps()
        for hh in range(16):
            hp, lo = hh // 2, (hh % 2) * 64
            mm(pS[0:32, hh * 32:(hh + 1) * 32], kTs[:, hp, :], qzs[:, hh, :], True, True, [kTs.t, qzs.t], [pS.t])
        if globals().get("SLEVEL", "Z") == "C":
            PSROT[0] = 7
            ast["off"] = save_off
            return
        tt("dve", stmp[0:32, :, :], pS[0:32, :].rearrange("p (h q) -> p h q", h=16), nfn[:, :].unsqueeze(2).to_broadcast([32, 16, 32]), ALU.add,
           [pS.t, nfn.t], [stmp.t])
        act(stmp[0:32, :, :], stmp[0:32, :, :], AF.Exp, [stmp.t], [stmp.t])
        tt("dve", ptn[:, :, :], stmp[0:32, :, :], bd32[:, :].unsqueeze(1).to_broadcast([32, 16, 32]), ALU.mult, [stmp.t, bd32.t], [ptn.t])
        if globals().get("SLEVEL", "Z") == "D":
            PSROT[0] = 7
            ast["off"] = save_off
            return
        Aset = psets[0]
        for hh in range(16):
            a_ = Aset[hh // 7]
            col = (hh % 7) * 65
            mm(a_[0:32, col:col + 65], ptn[:, hh, :], vnew[:, hh, :], True, True, [ptn.t, vnew.t], [a_.t])
        accumulate(Aset)
        if globals().get("SLEVEL", "Z") == "E":
            PSROT[0] = 7
            ast["off"] = save_off
            return
        a3 = acc8[0:32, 0:1040].rearrange("p (h d) -> p h d", h=16)
        P.add("dve", lambda e, o=rec8[:, :].unsqueeze(2), i_=a3[:, :, 64:65]: e.reciprocal(o, i_), [acc8.t], [rec8.t])
        tt("dve", ot8[:, :].rearrange("p (h d) -> p h d", h=16), a3[:, :, 0:64], rec8[:, :].unsqueeze(2).to_broadcast([32, 16, 64]), ALU.mult,
           [acc8.t, rec8.t], [ot8.t])
        p_ = ps()
        for hp in range(8):
            tr(p_[:, hp * 32:(hp + 1) * 32], ot8[:, hp * 128:(hp + 1) * 128], ident[0:32, 0:32], [ot8.t, ident.t], [p_.t])
        tt("dve", moT[:, 0:8, off:off + 32], p_[:, 0:256].rearrange("p (j n) -> p j n", j=8), ogss[:, :, :], ALU.mult, [p_.t, ogss.t], [moT.t])
        PSROT[0] = 7
        ast["off"] = save_off

    for l in layers:
        if l % 4 == 2 and 2 in mixers:
            setup_c()
        if l % 4 == 1 and 1 in mixers:
            setup_b(l)
        if l % 4 == 3 and 3 in mixers:
            setup_d()
        if l % 4 == 0 and 0 in mixers:
            setup_a()
        prep_sample_mem(l)
        for sbk in SUPER:
            for (b, t0, n, off) in seg_list(sbk):
                rmsnorm_fm(hT, t0, n, gv["norm_mix"][:, l * 8:(l + 1) * 8], xn, [hTt[b]], [xn.t], doff=off)
            xq_and_attend(l, sbk)
            if l % 4 == 3 and 3 in mixers:
                mixer_d(sbk)
            if l % 4 == 0 and 0 in mixers:
                mixer_a(sbk)
            if l % 4 == 1 and 1 in mixers:
                mixer_b(sbk)
            if l % 4 == 2 and 2 in mixers:
                mixer_c(sbk)
            out_proj(l, sbk)
            mlp(l, sbk)
        if l % 4 == 0 and 0 in mixers:
            finish_a()
        if l % 4 == 1 and 1 in mixers:
            finish_b()

    yst = xin

    def store_tokens(src, t0, nrows, dst_ap, rd_tok, otk, i):
        y = yst[i % 2]
        for half in range(2):
            p_ = ps()
            for j in range(4):
                c = half * 4 + j
                tr(p_[0:nrows, j * 128:(j + 1) * 128], src[:, c, t0:t0 + nrows], ident[:, :], [rd_tok, ident.t], [p_.t])
            evac(y[0:nrows, half * 512:(half + 1) * 512], p_[0:nrows, :], [p_.t], [y.t])
        dma("sp", dst_ap, y[0:nrows, 0:D], [y.t], [otk], y.t)

    for i in range(16):
        store_tokens(hT, i * 128, 128, O["y_p"][i * 128:(i + 1) * 128, :], hTt[i // 4], otok["y_p"], i)
    store_tokens(hT, TP, 32, O["y_s"][:, :], hTt[4], otok["y_s"], 16)

    return nc, P, es, locals()


def host_consts():
    i = np.arange(128)
    c = {}
    c["c_ident"] = np.eye(128, dtype=np.float32)
    c["c_ones"] = np.ones((128, 128), np.float32)
    c["c_blk64"] = ((i[:, None] // 64) == (i[None, :] // 64)).astype(np.float32)
    c["c_triu"] = (i[:, None] <= i[None, :]).astype(np.float32)
    c["c_striu"] = (i[:, None] < i[None, :]).astype(np.float32)
    sel = np.zeros((8, 8, 128), np.float32)
    for h in range(8):
        sel[h, h, :] = 1.0
    c["c_sel"] = sel.reshape(8, 1024)
    sl64 = np.zeros((64, 64), np.float32); sl64[63, :] = 1.0
    sl8 = np.zeros((8, 8), np.float32); sl8[7, :] = 1.0
    c["c_sl64"] = sl64
    c["c_sl8"] = sl8
    c["c_stril"] = (i[:, None] > i[None, :]).astype(np.float32)
    msel = np.zeros((32, 4, 8), np.float32)
    for s_ in range(4):
        for t_ in range(8):
            msel[8 * s_:8 * s_ + t_ + 1, s_, t_] = 1.0
    c["c_msel"] = msel.reshape(32, 32)
    i32_ = np.arange(32)
    c["c_bd32"] = (((i32_[:, None] // 8) == (i32_[None, :] // 8)) & (i32_[:, None] <= i32_[None, :])).astype(np.float32)
    c["c_smask"] = ((i32_[None, :] // 8) == np.arange(4)[:, None]).astype(np.float32)
    i32 = np.arange(128, dtype=np.int32).reshape(128, 1)
    c["c_iota"] = i32
    return c


def make_in_maps(inp, npool=2560):
    f = lambda a: np.ascontiguousarray(np.asarray(a))
    shared = {
        "ck": f(inp["cache_c_k"][0][:npool]).reshape(npool * 128, 1024),
        "cv": f(inp["cache_c_v"][0][:npool]).reshape(npool * 128, 1024),
        "clf": f(inp["cache_c_logf"][0][:npool]).reshape(npool * 128, 16),
        "norm_mix": f(inp["norm_mix"]), "w_out": f(inp["w_out"]), "norm_mlp": f(inp["norm_mlp"]),
        "w_up": f(inp["w_up"]), "w_down": f(inp["w_down"]), "mem_norm": f(inp["mem_norm"]),
        "w_mem_kv": f(inp["w_mem_kv"]), "xa_qnorm": f(inp["xa_qnorm"]), "xa_knorm": f(inp["xa_knorm"]),
        "w_in_a": f(inp["w_in_a"][0]), "a_conv_w": f(inp["a_conv_w"][0]), "a_log": f(inp["a_log"]),
        "a_dt_bias": f(inp["a_dt_bias"]), "a_norm_w": f(inp["a_norm_w"]), "w_in_b": f(inp["w_in_b"][0]),
        "hg_lb": f(inp["hg_lb"]), "b_norm_w": f(inp["b_norm_w"]), "w_in_c": f(inp["w_in_c"][0]),
        "c_fbias": f(inp["c_fbias"]), "c_qnorm": f(inp["c_qnorm"]), "c_knorm": f(inp["c_knorm"]),
        "w_in_d": f(inp["w_in_d"][0]), "d_ln_g": f(inp["d_ln_g"]), "d_ln_b": f(inp["d_ln_b"]),
        "d_ws": f(inp["d_ws"][0]), "d_bs": f(inp["d_bs"][0]),
    }
    shared.update(host_consts())
    maps = []
    for c in range(NCORES):
        s = slice(4 * c, 4 * c + 4)
        m = dict(shared)
        m["xp"] = f(inp["x_prompt"][c])
        m["xs"] = f(inp["x_sample"][s]).reshape(TS, D)
        m["mem"] = f(inp["mem_prompt"][c])
        m["a_conv"] = f(inp["state_a_conv"][0, s])
        m["a_ssm"] = f(inp["state_a_ssm"][0, s])
        m["b_ssm"] = f(inp["state_b_ssm"][0, s])
        m["cmk"] = f(inp["cache_mem_k"][:, s]).reshape(4, 4, 256, 256)
        m["cmv"] = f(inp["cache_mem_v"][:, s]).reshape(4, 4, 256, 256)
        m["pt"] = f(inp["page_table"][s]).astype(np.int32)
        maps.append(m)
    return maps


def assemble(results):
    g = lambda nm: [np.asarray(r[nm]) for r in results]
    y_p = np.stack(g("y_p"))
    y_s = np.concatenate(g("y_s")).reshape(32, 8, D)
    a_conv_p = np.stack(g("a_conv_p"))[None]
    a_conv_s = np.concatenate(g("a_conv_s"))[None]
    a_ssm_p = np.stack(g("a_ssm_p"))[None]
    a_ssm_s = np.concatenate(g("a_ssm_s"))[None]
    b_ssm_p = np.stack(g("b_ssm_p"))[None]
    b_ssm_s = np.concatenate(g("b_ssm_s"))[None]
    c_k_p = np.stack(g("c_k_p")).reshape(1, 8, TP, 16, 64)
    c_v_p = np.stack(g("c_v_p")).reshape(1, 8, TP, 16, 64)
    c_lf_p = np.stack(g("c_lf_p")).reshape(1, 8, TP, 16)
    c_k_s = np.concatenate(g("c_k_s")).reshape(1, 32, 8, 16, 64)
    c_v_s = np.concatenate(g("c_v_s")).reshape(1, 32, 8, 16, 64)
    c_lf_s = np.concatenate(g("c_lf_s")).reshape(1, 32, 8, 16)
    d_v_s = np.concatenate(g("d_v_s")).reshape(1, 32, 8, D)
    mem_k_p = np.stack(g("mem_k_p"), axis=1).reshape(4, 8, 256, 4, 64)
    mem_v_p = np.stack(g("mem_v_p"), axis=1).reshape(4, 8, 256, 4, 64)
    outs = (y_p, y_s, a_conv_p, a_conv_s, a_ssm_p, a_ssm_s, b_ssm_p, b_ssm_s, c_k_p, c_v_p, c_lf_p,
            c_k_s, c_v_s, c_lf_s, d_v_s, mem_k_p, mem_v_p)
    return tuple(np.ascontiguousarray(o, dtype=np.float32) for o in outs)


_CACHE = {}


def get_program(npool=2560):
    if npool not in _CACHE:
        nc, P, es, _ = build(npool=npool)
        P.finalize()
        P.emit()
        _CACHE[npool] = nc
    return _CACHE[npool]


def kernel(**inputs):
    nc = get_program()
    in_maps = make_in_maps(inputs)
    res = run_bass_kernel_spmd(nc, in_maps, core_ids=list(range(NCORES)))
    return assemble(res.results)
```

```python
import numpy as np
import concourse.bass as bass
import concourse.mybir as mybir
from concourse.bass_utils import run_bass_kernel_spmd

F32 = mybir.dt.float32
BF16 = mybir.dt.bfloat16
I32 = mybir.dt.int32
AF = mybir.ActivationFunctionType
ALU = mybir.AluOpType
AX = mybir.AxisListType

NCORES = 8
D = 1024
KC = 8
TP = 2048
TS = 32
TT = TP + TS
EPS = 1e-6
SAME_ENG_SYNC = True


class Tok:
    __slots__ = ("name", "w", "r", "sem", "cnt")

    def __init__(self, name):
        self.name = name
        self.w = None
        self.r = []
        self.sem = None
        self.cnt = 0


class Prog:
    ENGS = ("sp", "act", "dve", "pool", "pe")

    def __init__(self, nc):
        self.nc = nc
        self.ops = []
        self.toks = []

    def tok(self, name):
        t = Tok(name)
        self.toks.append(t)
        return t

    def add(self, eng, fn, rd=(), wr=(), dma_tok=None):
        oid = len(self.ops)
        deps = set()
        for t in rd:
            if t.w is not None:
                deps.add(t.w)
        for t in wr:
            if t.w is not None:
                deps.add(t.w)
            deps.update(t.r)
        for t in rd:
            t.r.append(oid)
        for t in wr:
            t.w = oid
            t.r = []
        op = {"id": oid, "eng": eng, "fn": fn, "deps": deps, "dma_tok": dma_tok,
              "dma_val": None, "sig": False, "sigidx": None}
        if dma_tok is not None:
            dma_tok.cnt += 16
            op["dma_val"] = dma_tok.cnt
        self.ops.append(op)
        return oid

    def finalize(self):
        nc = self.nc
        ops = self.ops
        for op in ops:
            need = []
            for d in op["deps"]:
                a = ops[d]
                if a["dma_tok"] is not None:
                    need.append(d)
                elif a["eng"] == op["eng"]:
                    if a["eng"] == "pe" or not SAME_ENG_SYNC:
                        continue
                    if a["fn"] is None:
                        continue
                    need.append(d)
                    a["sig"] = True
                else:
                    if a["fn"] is None:
                        a["sig"] = True
                        need.append(d)
                    else:
                        a["sig"] = True
                        need.append(d)
            op["need"] = need
        cnt = {e: 0 for e in self.ENGS}
        for op in ops:
            if op["sig"] and op["dma_tok"] is None:
                cnt[op["eng"]] += 1
                op["sigidx"] = cnt[op["eng"]]
        self.esem = {e: nc.alloc_semaphore(name="sem_" + e) for e in self.ENGS}
        for t in self.toks:
            if t.cnt > 0:
                t.sem = nc.alloc_semaphore(name="dsem_" + t.name)
        self.nsig = cnt

    def emit(self):
        nc = self.nc
        ops = self.ops
        per = {e: [op for op in ops if op["eng"] == e] for e in self.ENGS}
        esem = self.esem

        def run(e, eng):
            waited = {}
            for op in per[e]:
                for d in op["need"]:
                    a = ops[d]
                    if a["dma_tok"] is not None:
                        sem, val = a["dma_tok"].sem, a["dma_val"]
                    else:
                        sem, val = esem[a["eng"]], a["sigidx"]
                    key = sem.num
                    if waited.get(key, 0) >= val:
                        continue
                    eng.wait_ge(sem, val)
                    waited[key] = val
                if op["fn"] is None:
                    ins = eng.nop() if op["sig"] else None
                else:
                    ins = op["fn"](eng)
                if ins is not None:
                    if op["dma_tok"] is not None:
                        ins.then_inc(op["dma_tok"].sem, 16)
                    elif op["sig"]:
                        ins.then_inc(esem[e], 1)
            if e == "sp":
                for t in self.toks:
                    if t.sem is not None and waited.get(t.sem.num, 0) < t.cnt:
                        eng.wait_ge(t.sem, t.cnt)

        with nc.Block() as block:
            @block.sync
            def _(eng):
                run("sp", eng)

            @block.scalar
            def _(eng):
                run("act", eng)

            @block.vector
            def _(eng):
                run("dve", eng)

            @block.gpsimd
            def _(eng):
                run("pool", eng)

            @block.tensor
            def _(eng):
                run("pe", eng)


class T:
    def __init__(self, P, handle, name):
        self.h = handle
        self.t = P.tok(name)

    def __getitem__(self, idx):
        return self.h[idx]


from contextlib import ExitStack

BLKS = [(0, 512), (512, 512), (1024, 512), (1536, 512), (2048, 32)]
SUPER = [[0], [1], [2], [3, 4]]
NW = 3
WSLOT = 4096

OUT_SPECS = [
    ("y_p", [TP, D]), ("y_s", [TS, D]),
    ("a_conv_p", [3, 3072]), ("a_conv_s", [4, 3, 3072]),
    ("a_ssm_p", [8, 128, 128]), ("a_ssm_s", [4, 8, 128, 128]),
    ("b_ssm_p", [8, 128, 128]), ("b_ssm_s", [4, 8, 128, 128]),
    ("c_k_p", [TP, D]), ("c_v_p", [TP, D]), ("c_lf_p", [TP, 16]),
    ("c_k_s", [TS, D]), ("c_v_s", [TS, D]), ("c_lf_s", [TS, 16]),
    ("d_v_s", [TS, D]), ("mem_k_p", [4, 256, 256]), ("mem_v_p", [4, 256, 256]),
]


def build(npool=2560, layers=(0, 1, 2, 3), mixers=(0, 1, 2, 3)):
    nc = bass.Bass("TRN2", target_bir_lowering=False)
    P = Prog(nc)
    es = ExitStack()

    def din(name, shape, dt=F32):
        return nc.dram_tensor(name, list(shape), dt, kind="ExternalInput").ap()

    def dout(name, shape, dt=F32):
        return nc.dram_tensor(name, list(shape), dt, kind="ExternalOutput").ap()

    def sb(name, shape, dt=F32):
        return T(P, es.enter_context(nc.sbuf_tensor(name, list(shape), dt)), name)

    I = {}
    I["xp"] = din("xp", [TP, D])
    I["xs"] = din("xs", [TS, D])
    I["mem"] = din("mem", [256, D])
    I["a_conv"] = din("a_conv", [4, 3, 3072])
    I["a_ssm"] = din("a_ssm", [4, 8, 128, 128])
    I["b_ssm"] = din("b_ssm", [4, 8, 128, 128])
    I["ck"] = din("ck", [npool * 128, 1024])
    I["cv"] = din("cv", [npool * 128, 1024])
    I["clf"] = din("clf", [npool * 128, 16])
    I["cmk"] = din("cmk", [4, 4, 256, 256])
    I["cmv"] = din("cmv", [4, 4, 256, 256])
    I["pt"] = din("pt", [4, 64], I32)
    I["c_iota"] = din("c_iota", [128, 1], I32)
    for nm, shp in [("norm_mix", [4, D]), ("w_out", [4, 1280, D]), ("norm_mlp", [4, D]),
                    ("w_up", [4, D, 4096]), ("w_down", [4, 4096, D]), ("mem_norm", [4, D]),
                    ("w_mem_kv", [4, D, 512]), ("xa_qnorm", [4, 64]), ("xa_knorm", [4, 64]),
                    ("w_in_a", [D, 4368]), ("a_conv_w", [4, 3072]), ("a_log", [1, 8]),
                    ("a_dt_bias", [1, 8]), ("a_norm_w", [1, 128]), ("w_in_b", [D, 4352]),
                    ("hg_lb", [4, D]), ("b_norm_w", [1, 128]), ("w_in_c", [D, 4368]),
                    ("c_fbias", [1, 16]), ("c_qnorm", [1, 64]), ("c_knorm", [1, 64]),
                    ("w_in_d", [D, 2304]), ("d_ln_g", [1, D]), ("d_ln_b", [1, D]),
                    ("d_ws", [8, 128, 128]), ("d_bs", [8, 128]),
                    ("c_ident", [128, 128]), ("c_ones", [128, 128]), ("c_blk64", [128, 128]),
                    ("c_triu", [128, 128]), ("c_striu", [128, 128]),
                    ("c_sel", [8, 1024]), ("c_sl64", [64, 64]), ("c_sl8", [8, 8]),
                    ("c_stril", [128, 128]), ("c_msel", [32, 32]), ("c_bd32", [32, 32]), ("c_smask", [4, 32])]:
        I[nm] = din(nm, shp)
    O = {nm: dout(nm, shp) for nm, shp in OUT_SPECS}
    DEBUG = globals().get("KDBG", False)
    if DEBUG:
        O["dbg1"] = dout("dbg1", [128, 1024])
        O["dbg2"] = dout("dbg2", [128, 1024])
        dbgtok = P.tok("dbgtok")
    otok = {nm: P.tok("o_" + nm) for nm, _ in OUT_SPECS}

    def mm(out, lhsT, rhs, start, stop, rd, wr):
        P.add("pe", lambda e: e.matmul(out, lhsT, rhs, start=start, stop=stop), rd, wr)

    def tr(out, in_, ident, rd, wr):
        P.add("pe", lambda e: e.transpose(out, in_, ident), rd, wr)

    def act(out, in_, func, rd, wr, bias=None, scale=1.0):
        if bias is None:
            P.add("act", lambda e: e.activation(out, in_, func, scale=scale), rd, wr)
        else:
            P.add("act", lambda e: e.activation(out, in_, func, bias=bias, scale=scale), rd, wr)

    def tt(eng, out, in0, in1, op, rd, wr):
        P.add(eng, lambda e: e.tensor_tensor(out, in0, in1, op), rd, wr)

    def ts(eng, out, in0, s1, s2, op0, op1, rd, wr):
        if s2 is None:
            P.add(eng, lambda e: e.tensor_scalar(out, in0, s1, None, op0), rd, wr)
        else:
            P.add(eng, lambda e: e.tensor_scalar(out, in0, s1, s2, op0, op1), rd, wr)

    def stt(eng, out, in0, scalar, in1, op0, op1, rd, wr):
        P.add(eng, lambda e: e.scalar_tensor_tensor(out, in0, scalar, in1, op0, op1), rd, wr)

    def cp(eng, out, in_, rd, wr):
        if eng == "act":
            P.add("act", lambda e: e.copy(out, in_), rd, wr)
        else:
            P.add(eng, lambda e: e.tensor_copy(out, in_), rd, wr)

    def memset(eng, ap, val, wr):
        P.add(eng, lambda e: e.memset(ap, val), (), wr)

    def dma(q, out, in_, rd, wr, tok, slow=False):
        if slow:
            P.add(q, lambda e: e.dma_start(out=out, in_=in_, allow_slow_non_contiguous=True), rd, wr, dma_tok=tok)
        else:
            P.add(q, lambda e: e.dma_start(out=out, in_=in_), rd, wr, dma_tok=tok)

    epsc = {}

    def rsqrt_eps(out, in_, epsval, rd, wr):
        if epsval not in epsc:
            c_ = sb("epsc%d" % len(epsc), [128, 1])
            memset("dve", c_[:, :], float(epsval), [c_.t])
            epsc[epsval] = c_
        c_ = epsc[epsval]
        np_ = out.shape[0]
        act(out, in_, AF.Sqrt, rd + [c_.t], wr, bias=c_[0:np_, 0:1])
        P.add("dve", lambda e: e.reciprocal(out, out), wr, wr)

    _cpi = [0]

    def evac(out, in_, rd, wr):
        _cpi[0] += 1
        cp("act" if _cpi[0] % 2 else "dve", out, in_, rd, wr)

    PS = [T(P, es.enter_context(nc.psum_tensor("ps%d" % i, [128, 512], F32)), "ps%d" % i) for i in range(8)]
    _psi = [0]

    PSROT = [7]

    def ps():
        _psi[0] += 1
        return PS[_psi[0] % PSROT[0]]

    psacc = PS[7]

    ident = sb("ident", [128, 128])
    ones_f = sb("ones_f", [128, 128])
    blk_f = sb("blk_f", [128, 128])
    triu_f = sb("triu_f", [128, 128])
    striu_f = sb("striu_f", [128, 128])
    ones_b = sb("ones_b", [128, 128], BF16)
    blk_b = sb("blk_b", [128, 128], BF16)
    for t_, nm in [(ident, "c_ident"), (ones_f, "c_ones"), (blk_f, "c_blk64"), (triu_f, "c_triu"), (striu_f, "c_striu")]:
        dma("sp", t_[:, :], I[nm][:, :], (), [t_.t], t_.t)
    cp("dve", ones_b[:, :], ones_f[:, :], [ones_f.t], [ones_b.t])
    cp("dve", blk_b[:, :], blk_f[:, :], [blk_f.t], [blk_b.t])
    triu_b = sb("triu_b", [128, 128], BF16)
    cp("dve", triu_b[:, :], triu_f[:, :], [triu_f.t], [triu_b.t])
    stril_b = sb("stril_b", [128, 128], BF16)

    gv = {}
    for nm in ("norm_mix", "norm_mlp", "mem_norm"):
        g = sb("g_" + nm, [128, 32])
        dma("sp", g[:, :], I[nm].rearrange("l (c p) -> p (l c)", p=128), (), [g.t], g.t, slow=True)
        ts("dve", g[:, :], g[:, :], 32.0, None, ALU.mult, None, [g.t], [g.t])
        gv[nm] = g
    qg = sb("qg", [128, 4])
    dma("sp", qg[0:64, :], I["xa_qnorm"].rearrange("l d -> d l"), (), [qg.t], qg.t, slow=True)
    dma("sp", qg[64:128, :], I["xa_qnorm"].rearrange("l d -> d l"), (), [qg.t], qg.t, slow=True)
    ARENA = 15124
    arena_h = es.enter_context(nc.sbuf_tensor("arena", [128, ARENA], F32))
    ast = {"off": 0, "prev": [], "cur": []}

    def arena_reset():
        ast["prev"] = ast["prev"] + ast["cur"]
        ast["cur"] = []
        ast["off"] = 0

    def aalloc(name, shape, dt=F32):
        nel = 1
        for d_ in shape[1:]:
            nel *= d_
        nf = (nel * (4 if dt in (F32, I32) else 2) + 3) // 4
        off = ast["off"]
        ast["off"] += nf
        assert ast["off"] <= ARENA, ("arena overflow", name, ast["off"])
        v = arena_h[0:shape[0], off:off + nf]
        if dt != F32:
            v = v.bitcast(dt)[:, 0:nel]
        if len(shape) == 3:
            v = v.rearrange("p (a b) -> p a b", a=shape[1])
        elif len(shape) == 4:
            v = v.rearrange("p (a b c) -> p a b c", a=shape[1], b=shape[2])
        t_ = T(P, v, name)
        for pt in ast["prev"]:
            if pt.w is not None:
                t_.t.r.append(pt.w)
            t_.t.r.extend(pt.r)
        ast["cur"].append(t_.t)
        return t_

    kg = aalloc("kg", [128, 4, 4, 64])
    for hh in range(4):
        dma("sp", kg[:, :, hh, :], I["xa_knorm"].partition_broadcast(128), (), [kg.t], kg.t, slow=True)

    hT = sb("hT", [128, KC, TT])
    hTt = [P.tok("hT%d" % b) for b in range(5)]
    xin = [sb("xin%d" % i, [128, 1040]) for i in range(2)]
    dma("sp", xin[0][:, 0:128], I["c_stril"][:, :], (), [xin[0].t], xin[0].t)
    cp("dve", stril_b[:, :], xin[0][:, 0:128], [xin[0].t], [stril_b.t])
    iota_i = sb("iota_i", [128, 1], I32)
    iota_f = sb("iota_f", [128, 1])
    dma("sp", iota_i[:, :], I["c_iota"][:, :], (), [iota_i.t], iota_i.t)
    cp("dve", iota_f[:, :], iota_i[:, :], [iota_i.t], [iota_f.t])
    msel = sb("msel", [32, 32], BF16)
    dma("sp", xin[1][0:32, 0:32], I["c_msel"][:, :], (), [xin[1].t], xin[1].t)
    cp("dve", msel[:, :], xin[1][0:32, 0:32], [xin[1].t], [msel.t])

    def load_tokens_fm(src_ap, nrows, dst, dst_off, wtok, i):
        xt = xin[i % 2]
        dma("sp", xt[0:nrows, 0:D], src_ap, (), [xt.t], xt.t)
        for half in range(2):
            p_ = ps()
            for j in range(4):
                c = half * 4 + j
                tr(p_[:, j * 128:j * 128 + nrows], xt[0:nrows, c * 128:(c + 1) * 128], ident[0:nrows, 0:nrows], [xt.t, ident.t], [p_.t])
            evac(dst[:, half * 4:half * 4 + 4, dst_off:dst_off + nrows],
                 p_[:, :].rearrange("p (j n) -> p j n", j=4)[:, :, 0:nrows], [p_.t], [wtok])


    WR = [sb("wr%d" % i, [128, WSLOT], BF16) for i in range(NW)]
    _wi = [0]

    def wload(src3):
        _wi[0] += 1
        w = WR[_wi[0] % NW]
        kc, g = src3.shape[1], src3.shape[2]
        view = w[:, 0:kc * g].rearrange("p (k g) -> p k g", k=kc)
        dma("pool", view, src3, (), [w.t], w.t)
        return w, view

    sqs = [sb("sq%d" % i, [128, 544], BF16) for i in range(2)]
    rq = sb("rq", [128, 512])
    rstd = rq
    rden = rq
    xn = sb("xn", [128, KC, 544], BF16)

    def rmsnorm_fm(src, t0, n, gcols, dst, rd, wr, doff=0):
        p_ = ps()
        for c in range(KC):
            sq = sqs[c % 2]
            act(sq[:, 0:n], src[:, c, t0:t0 + n], AF.Square, rd, [sq.t])
            mm(p_[:, 0:n], ones_b[:, :], sq[:, 0:n], c == 0, c == KC - 1, [ones_b.t, sq.t], [p_.t])
        rsqrt_eps(rstd[:, 0:n], p_[:, 0:n], 1024.0 * EPS, [p_.t], [rstd.t])
        for c in range(KC):
            stt("dve", dst[:, c, doff:doff + n], src[:, c, t0:t0 + n], gcols[:, c:c + 1], rstd[:, 0:n], ALU.mult, ALU.mult,
                rd + [rstd.t], wr)

    for i in range(2):
        load_tokens_fm(I["mem"][i * 128:(i + 1) * 128, :], 128, hT, i * 128, hTt[0], 17 + i)

    onesp = sb("onesp", [128, 2, 128], BF16)
    memset("dve", onesp[:, :, :], 0.0, [onesp.t])
    memset("dve", onesp[:, 0, 0:64], 1.0, [onesp.t])
    memset("dve", onesp[:, 1, 64:128], 1.0, [onesp.t])
    kvs = aalloc("kvs", [128, 512])
    kss = aalloc("kss", [128, 4])
    ksq = aalloc("ksq", [128, 256])

    def prep_mem(l, kn_src_fn, mk, mv):
        pass

    for l in range(4):
        rmsnorm_fm(hT, 0, 256, gv["mem_norm"][:, l * 8:(l + 1) * 8], xn, [hTt[0]], [xn.t])
        w, wv = wload(I["w_mem_kv"][l].rearrange("(k p) g -> p k g", p=128))
        for j in range(2):
            p_ = ps()
            for c in range(KC):
                mm(p_[:, :], xn[:, c, j * 128:(j + 1) * 128], wv[:, c, :], c == 0, c == KC - 1, [xn.t, w.t], [p_.t])
            act(ksq[:, :], p_[:, 0:256], AF.Square, [p_.t], [ksq.t])
            P.add("dve", lambda e, o=kss[:, :], i_=ksq[:, :].rearrange("p (h d) -> p h d", h=4): e.tensor_reduce(o, i_, AX.X, ALU.add),
                  [ksq.t], [kss.t])
            rsqrt_eps(kss[:, :], kss[:, :], 64.0 * EPS, [kss.t], [kss.t])
            tt("dve", kvs[:, 0:256].rearrange("p (h d) -> p h d", h=4), p_[:, 0:256].rearrange("p (h d) -> p h d", h=4),
               kss[:, :].unsqueeze(2).to_broadcast([128, 4, 64]), ALU.mult, [p_.t, kss.t], [kvs.t])
            stt("dve", kvs[:, 0:256].rearrange("p (h d) -> p h d", h=4), kvs[:, 0:256].rearrange("p (h d) -> p h d", h=4), 8.0,
                kg[:, l, :, :], ALU.mult, ALU.mult, [kvs.t, kg.t], [kvs.t])
            cp("act", kvs[:, 256:512], p_[:, 256:512], [p_.t], [kvs.t])
            dma("sp", O["mem_k_p"][l, j * 128:(j + 1) * 128, :], kvs[:, 0:256], [kvs.t], [otok["mem_k_p"]], kvs.t)
            dma("sp", O["mem_v_p"][l, j * 128:(j + 1) * 128, :], kvs[:, 256:512], [kvs.t], [otok["mem_v_p"]], kvs.t)

    for i in range(16):
        load_tokens_fm(I["xp"][i * 128:(i + 1) * 128, :], 128, hT, i * 128, hTt[i // 4], i)
    load_tokens_fm(I["xs"][:, :], 32, hT, TP, hTt[4], 16)

    moT = sb("moT", [128, 10, 544], BF16)
    memset("dve", moT[:, :, :], 0.0, [moT.t])
    qs = sb("qs", [128, 2, 544], BF16)
    sq2 = sqs[1]
    ptile = sb("ptile", [128, 2, 512], BF16)
    mkTs = [sb("mkTs%d" % b, [128, 2, 256], BF16) for b in range(5)]
    mvps = [sb("mvps%d" % b, [128, 2, 4, 128], BF16) for b in range(5)]
    for b in range(5):
        memset("dve", mvps[b][:, :, :, :], 0.0, [mvps[b].t])
    aT = sb("aT", [128, 4, 544], BF16)
    rl = sqs[0]
    W_IN = [I["w_in_a"], I["w_in_b"], I["w_in_c"], I["w_in_d"]]

    def seg_list(sbk):
        out, off = [], 0
        for b in sbk:
            out.append((b, BLKS[b][0], BLKS[b][1], off))
            off += BLKS[b][1]
        return out

    def prep_sample_mem(l):
        for b in range(5):
            for j in range(2):
                xt = xin[(b * 2 + j) % 2]
                if b < 4:
                    dma("sp", xt[:, 0:256], I["cmk"][l, b, j * 128:(j + 1) * 128, :], (), [xt.t], xt.t)
                    dma("sp", xt[:, 256:512], I["cmv"][l, b, j * 128:(j + 1) * 128, :], (), [xt.t], xt.t)
                else:
                    dma("sp", xt[:, 0:256], O["mem_k_p"][l, j * 128:(j + 1) * 128, :], [otok["mem_k_p"]], [xt.t], xt.t)
                    dma("sp", xt[:, 256:512], O["mem_v_p"][l, j * 128:(j + 1) * 128, :], [otok["mem_v_p"]], [xt.t], xt.t)
                pt_ = ps()
                for hp in range(2):
                    tr(pt_[:, hp * 128:(hp + 1) * 128], xt[:, hp * 128:(hp + 1) * 128], ident[:, :], [xt.t, ident.t], [pt_.t])
                evac(mkTs[b][:, :, j * 128:(j + 1) * 128], pt_[:, 0:256].rearrange("p (h n) -> p h n", h=2), [pt_.t], [mkTs[b].t])
                for hh in range(4):
                    half = hh % 2
                    cp("dve", mvps[b][:, j, hh, half * 64:half * 64 + 64], xt[:, 256 + hh * 64:256 + hh * 64 + 64], [xt.t], [mvps[b].t])

    def attend(l, mk, mv, off, n):
        for hp in range(2):
            pnum = ps()
            pden = ps()
            for half in range(2):
                hh = hp * 2 + half
                lo = half * 64
                for j in range(2):
                    p_ = ps()
                    mm(p_[:, 0:n], mk[lo:lo + 64, hp, j * 128:(j + 1) * 128], qs[lo:lo + 64, hp, off:off + n], True, True,
                       [mk.t, qs.t], [p_.t])
                    act(ptile[:, j, 0:n], p_[:, 0:n], AF.Exp, [p_.t], [ptile.t])
                for j in range(2):
                    first = (half == 0 and j == 0)
                    last = (half == 1 and j == 1)
                    mm(pnum[:, 0:n], mv[:, j, hh, :], ptile[:, j, 0:n], first, last, [mv.t, ptile.t], [pnum.t])
                    mm(pden[:, 0:n], onesp[:, half, :], ptile[:, j, 0:n], first, last, [onesp.t, ptile.t], [pden.t])
            P.add("dve", lambda e, o=rden[:, 0:n], i_=pden[:, 0:n]: e.reciprocal(o, i_), [pden.t], [rden.t])
            tt("dve", moT[:, 8 + hp, off:off + n], pnum[:, 0:n], rden[:, 0:n], ALU.mult, [pnum.t, rden.t], [moT.t])

    def xq_and_attend(l, sbk):
        w_in = W_IN[l % 4]
        c0 = w_in.shape[1] - 256
        w, wv = wload(w_in[:, c0:c0 + 256].rearrange("(k p) g -> p k g", p=128))
        for (b, t0, n, off) in seg_list(sbk):
            for hp in range(2):
                p_ = ps()
                for c in range(KC):
                    mm(p_[:, 0:n], wv[:, c, hp * 128:(hp + 1) * 128], xn[:, c, off:off + n], c == 0, c == KC - 1, [w.t, xn.t], [p_.t])
                act(sq2[:, 0:n], p_[:, 0:n], AF.Square, [p_.t], [sq2.t])
                p2 = ps()
                mm(p2[:, 0:n], blk_b[:, :], sq2[:, 0:n], True, True, [blk_b.t, sq2.t], [p2.t])
                rsqrt_eps(rq[:, 0:n], p2[:, 0:n], 64.0 * EPS, [p2.t], [rq.t])
                stt("dve", qs[:, hp, off:off + n], p_[:, 0:n], qg[:, l:l + 1], rq[:, 0:n], ALU.mult, ALU.mult,
                    [p_.t, qg.t, rq.t], [qs.t])
            if b < 4:
                attend(l, mkTs[4], mvps[4], off, n)
            else:
                for sq_ in range(4):
                    attend(l, mkTs[sq_], mvps[sq_], off + sq_ * 8, 8)

    def out_proj(l, sbk):
        segs = seg_list(sbk)
        for g in range(4):
            w, wv = wload(I["w_out"][l][:, g * 256:(g + 1) * 256].rearrange("(k p) g -> p k g", p=128))
            for m in range(2):
                for (b, t0, n, off) in segs:
                    p_ = ps()
                    for c in range(10):
                        mm(p_[:, 0:n], wv[:, c, m * 128:(m + 1) * 128], moT[:, c, off:off + n], c == 0, c == 9, [w.t, moT.t], [p_.t])
                    tt("dve", hT[:, g * 2 + m, t0:t0 + n], hT[:, g * 2 + m, t0:t0 + n], p_[:, 0:n], ALU.add, [p_.t, hTt[b]], [hTt[b]])

    def mlp(l, sbk):
        segs = seg_list(sbk)
        for (b, t0, n, off) in segs:
            rmsnorm_fm(hT, t0, n, gv["norm_mlp"][:, l * 8:(l + 1) * 8], xn, [hTt[b]], [xn.t], doff=off)
        for g in range(8):
            wu, wuv = wload(I["w_up"][l][:, g * 512:(g + 1) * 512].rearrange("(k p) g -> p k g", p=128))
            wd, wdv = wload(I["w_down"][l][g * 512:(g + 1) * 512, :].rearrange("(k p) g -> p k g", p=128))
            for (b, t0, n, off) in segs:
                for f in range(4):
                    p_ = ps()
                    for c in range(KC):
                        mm(p_[:, 0:n], wuv[:, c, f * 128:(f + 1) * 128], xn[:, c, off:off + n], c == 0, c == KC - 1, [wu.t, xn.t], [p_.t])
                    act(rl[:, 0:n], p_[:, 0:n], AF.Relu, [p_.t], [rl.t])
                    tt("dve", aT[:, f, off:off + n], rl[:, 0:n], rl[:, 0:n], ALU.mult, [rl.t], [aT.t])
                for m in range(8):
                    p_ = ps()
                    for f in range(4):
                        mm(p_[:, 0:n], wdv[:, f, m * 128:(m + 1) * 128], aT[:, f, off:off + n], f == 0, f == 3, [wd.t, aT.t], [p_.t])
                    tt("dve", hT[:, m, t0:t0 + n], hT[:, m, t0:t0 + n], p_[:, 0:n], ALU.add, [p_.t, hTt[b]], [hTt[b]])


    MD = {}

    def setup_d():
        arena_reset()
        uT = aalloc("uT", [128, 8, 544], BF16)
        gbt = aalloc("gbt", [128, 2, D])
        dma("sp", gbt[:, 0, :], I["d_ln_g"].partition_broadcast(128), (), [gbt.t], gbt.t, slow=True)
        dma("sp", gbt[:, 1, :], I["d_ln_b"].partition_broadcast(128), (), [gbt.t], gbt.t, slow=True)
        wmT = aalloc("wmT", [128, 8, 128], BF16)
        wmTs = aalloc("wmTs", [32, 8, 32], BF16)
        bs_f = aalloc("bs_f", [1, D])
        bs_b = aalloc("bs_b", [1, D], BF16)
        dma("sp", bs_f[:, :], I["d_bs"].rearrange("g r -> (g r)").unsqueeze(0), (), [bs_f.t], bs_f.t)
        cp("dve", bs_b[:, :], bs_f[:, :], [bs_f.t], [bs_b.t])
        bs_s = aalloc("bs_s", [1, 8, 32], BF16)
        for s_ in range(4):
            cp("dve", bs_s[:, :, s_ * 8:(s_ + 1) * 8], bs_f[:, :].rearrange("o (g r) -> o g r", g=8)[:, :, 0:8], [bs_f.t], [bs_s.t])
        for g in range(8):
            xt = xin[g % 2]
            dma("sp", xt[:, 0:128], I["d_ws"][g], (), [xt.t], xt.t)
            p_ = ps()
            tr(p_[:, 0:128], xt[:, 0:128], ident[:, :], [xt.t, ident.t], [p_.t])
            tt("dve", wmT[:, g, :], p_[:, 0:128], triu_f[:, :], ALU.mult, [p_.t, triu_f.t], [wmT.t])
        memset("dve", wmTs[:, :, :], 0.0, [wmTs.t])
        for s_ in range(4):
            dma("sp", wmTs[s_ * 8:(s_ + 1) * 8, :, s_ * 8:(s_ + 1) * 8], wmT[0:8, :, 0:8], [wmT.t], [wmTs.t], wmTs.t, slow=True)
        MD.update(uT=uT, gbt=gbt, wmT=wmT, wmTs=wmTs, bs_b=bs_b, bs_s=bs_s,
                  g1=aalloc("g1", [128, 512]), g2=aalloc("g2", [128, 512]), vz=aalloc("vz", [128, D]),
                  vb=aalloc("vb", [128, D], BF16), lns=aalloc("lns", [128, 2]))

    if True:
        def gelu_from_psum(p_, npart, n, out_ap, out_tok):
            g1, g2 = MD["g1"], MD["g2"]
            act(g1[0:npart, 0:n], p_[0:npart, 0:n], AF.Square, [p_.t], [g1.t])
            ts("dve", g1[0:npart, 0:n], g1[0:npart, 0:n], 0.044715, 1.0, ALU.mult, ALU.add, [g1.t], [g1.t])
            tt("dve", g2[0:npart, 0:n], g1[0:npart, 0:n], p_[0:npart, 0:n], ALU.mult, [g1.t, p_.t], [g2.t])
            act(g2[0:npart, 0:n], g2[0:npart, 0:n], AF.Sigmoid, [g2.t], [g2.t], scale=1.5957691216)
            tt("dve", out_ap, g2[0:npart, 0:n], p_[0:npart, 0:n], ALU.mult, [g2.t, p_.t], [out_tok])

        def mixer_d(sbk):
            uT, gbt, wmT, wmTs, bs_b, bs_s = MD["uT"], MD["gbt"], MD["wmT"], MD["wmTs"], MD["bs_b"], MD["bs_s"]
            g1, g2, vz, vb, lns = MD["g1"], MD["g2"], MD["vz"], MD["vb"], MD["lns"]
            segs = seg_list(sbk)
            w_in = I["w_in_d"]
            for g in range(2):
                w, wv = wload(w_in[:, g * 512:(g + 1) * 512].rearrange("(k p) g -> p k g", p=128))
                for m in range(4):
                    for (b, t0, n, off) in segs:
                        p_ = ps()
                        for c in range(KC):
                            mm(p_[:, 0:n], wv[:, c, m * 128:(m + 1) * 128], xn[:, c, off:off + n], c == 0, c == KC - 1, [w.t, xn.t], [p_.t])
                        gelu_from_psum(p_, 128, n, uT[:, g * 4 + m, off:off + n], uT.t)
            w0, wv0 = wload(w_in[:, 1024:1536].rearrange("(k p) g -> p k g", p=128))
            w1, wv1 = wload(w_in[:, 1536:2048].rearrange("(k p) g -> p k g", p=128))
            for (b, t0, n, off) in segs:
                ntile = (n + 127) // 128
                for i in range(ntile):
                    nt = min(128, n - i * 128)
                    for hf, (w, wv) in enumerate(((w0, wv0), (w1, wv1))):
                        p_ = ps()
                        for c in range(KC):
                            mm(p_[0:nt, :], xn[:, c, off + i * 128:off + i * 128 + nt], wv[:, c, :], c == 0, c == KC - 1, [xn.t, w.t], [p_.t])
                        gelu_from_psum(p_, nt, 512, vz[0:nt, hf * 512:(hf + 1) * 512], vz.t)
                    P.add("dve", lambda e, o=lns[0:nt, 0:1], i_=vz[0:nt, :]: e.tensor_reduce(o, i_, AX.X, ALU.add), [vz.t], [lns.t])
                    ts("dve", lns[0:nt, 0:1], lns[0:nt, 0:1], -1.0 / 1024.0, None, ALU.mult, None, [lns.t], [lns.t])
                    ts("dve", vz[0:nt, :], vz[0:nt, :], lns[0:nt, 0:1], None, ALU.add, None, [vz.t, lns.t], [vz.t])
                    act(g1[0:nt, :], vz[0:nt, 0:512], AF.Square, [vz.t], [g1.t])
                    act(g2[0:nt, :], vz[0:nt, 512:1024], AF.Square, [vz.t], [g2.t])
                    tt("dve", g1[0:nt, :], g1[0:nt, :], g2[0:nt, :], ALU.add, [g1.t, g2.t], [g1.t])
                    P.add("dve", lambda e, o=lns[0:nt, 1:2], i_=g1[0:nt, :]: e.tensor_reduce(o, i_, AX.X, ALU.add), [g1.t], [lns.t])
                    rsqrt_eps(lns[0:nt, 1:2], lns[0:nt, 1:2], 1024.0 * EPS, [lns.t], [lns.t])
                    stt("dve", vz[0:nt, :], vz[0:nt, :], lns[0:nt, 1:2], gbt[0:nt, 0, :], ALU.mult, ALU.mult, [vz.t, lns.t, gbt.t], [vz.t])
                    stt("dve", vz[0:nt, :], vz[0:nt, :], 32.0, gbt[0:nt, 1, :], ALU.mult, ALU.add, [vz.t, gbt.t], [vz.t])
                    cp("act", vb[0:nt, :], vz[0:nt, :], [vz.t], [vb.t])
                    if b == 4:
                        dma("sp", O["d_v_s"][:, :], vz[0:32, :], [vz.t], [otok["d_v_s"]], vz.t)
                    for hf in range(2):
                        p_ = ps()
                        for g4 in range(4):
                            g = hf * 4 + g4
                            if b < 4:
                                mm(p_[:, g4 * 128:(g4 + 1) * 128], vb[:, g * 128:(g + 1) * 128], wmT[:, g, :], True, False, [vb.t, wmT.t], [p_.t])
                                mm(p_[:, g4 * 128:(g4 + 1) * 128], ones_b[0:1, :], bs_b[0:1, g * 128:(g + 1) * 128], False, True, [ones_b.t, bs_b.t], [p_.t])
                                tt("dve", moT[:, g, off + i * 128:off + (i + 1) * 128], uT[:, g, off + i * 128:off + (i + 1) * 128],
                                   p_[:, g4 * 128:(g4 + 1) * 128], ALU.mult, [uT.t, p_.t], [moT.t])
                            else:
                                mm(p_[:, g4 * 128:g4 * 128 + 32], vb[0:32, g * 128:(g + 1) * 128], wmTs[:, g, :], True, False, [vb.t, wmTs.t], [p_.t])
                                mm(p_[:, g4 * 128:g4 * 128 + 32], ones_b[0:1, :], bs_s[0:1, g, :], False, True, [ones_b.t, bs_s.t], [p_.t])
                                tt("dve", moT[:, g, off:off + 32], uT[:, g, off:off + 32], p_[:, g4 * 128:g4 * 128 + 32], ALU.mult, [uT.t, p_.t], [moT.t])


    MA = {}

    def setup_a():
        arena_reset()
        A = MA
        A["S"] = aalloc("gS", [128, 8, 128])
        A["ctail"] = aalloc("ctail", [128, 24, 3])
        A["ctail_s"] = aalloc("ctail_s", [128, 24, 4, 3])
        A["cst"] = aalloc("cst", [128, 24, 4, 3])
        A["cw"] = aalloc("cw", [128, 24, 4])
        A["nw"] = aalloc("nw", [128, 1])
        A["hp8"] = aalloc("hp8", [8, 3])
        A["sel"] = aalloc("sel", [8, 8, 128])
        A["sl64"] = aalloc("sl64", [64, 64])
        A["sl8"] = aalloc("sl8", [8, 8])
        for nm in ("bt", "gt", "GT"):
            A[nm] = aalloc(nm, [8, 544])
        for nm in ("gtok", "btok", "Gtok", "ekl", "eGtok", "bke"):
            A[nm] = aalloc(nm, [64, 64])
        A["cv"] = [aalloc("cv%d" % i, [128, 544]) for i in range(3)]
        A["zs"] = aalloc("zs", [128, 544], BF16)
        A["kT"] = aalloc("kT", [128, 544], BF16)
        A["kbT"] = aalloc("kbT", [128, 544], BF16)
        A["qT"] = aalloc("qT", [128, 544], BF16)
        A["eGb"] = aalloc("eGb", [128, 544])
        A["RHSk"] = aalloc("RHSk", [64, 8, 128], BF16)
        A["RHSv"] = aalloc("RHSv", [64, 8, 128], BF16)
        A["khat"] = aalloc("khat", [64, 8, 128], BF16)
        A["att"] = aalloc("att", [64, 512], BF16)
        A["TTb"] = aalloc("TTb", [64, 512], BF16)
        A["Xa"] = aalloc("Xa", [64, 512])
        A["XTa"] = aalloc("XTa", [64, 512])
        A["Xb"] = aalloc("Xb", [64, 512])
        A["XTb"] = aalloc("XTb", [64, 512])
        A["Pm"] = aalloc("Pm", [128, 544])
        A["nWT"] = aalloc("nWT", [128, 544])
        A["u"] = [aalloc("u%d" % i, [64, 128], BF16) for i in range(2)]
        A["Sl"] = [aalloc("Sl%d" % i, [128, 128]) for i in range(2)]
        A["So"] = [aalloc("So%d" % i, [128, 128]) for i in range(2)]
        S, ctail, cw, nw, hp8 = A["S"], A["ctail"], A["cw"], A["nw"], A["hp8"]
        memset("dve", S[:, :, :], 0.0, [S.t])
        memset("dve", ctail[:, :, :], 0.0, [ctail.t])
        dma("sp", A["sel"][:, :, :], I["c_sel"].rearrange("k (h m) -> k h m", h=8), (), [A["sel"].t], A["sel"].t)
        dma("sp", A["sl64"][:, :], I["c_sl64"][:, :], (), [A["sl64"].t], A["sl64"].t)
        dma("sp", A["sl8"][:, :], I["c_sl8"][:, :], (), [A["sl8"].t], A["sl8"].t)
        for tap in range(4):
            dma("sp", cw[:, :, tap], I["a_conv_w"][tap].rearrange("(j p) -> p j", p=128), (), [cw.t], cw.t, slow=True)
        dma("sp", nw[:, :], I["a_norm_w"].rearrange("o d -> d o"), (), [nw.t], nw.t, slow=True)
        ts("dve", nw[:, :], nw[:, :], float(np.sqrt(128.0)), None, ALU.mult, None, [nw.t], [nw.t])
        dma("sp", hp8[:, 0:1], I["a_dt_bias"].rearrange("o h -> h o"), (), [hp8.t], hp8.t, slow=True)
        dma("sp", hp8[:, 1:2], I["a_log"].rearrange("o h -> h o"), (), [hp8.t], hp8.t, slow=True)
        act(hp8[:, 1:2], hp8[:, 1:2], AF.Exp, [hp8.t], [hp8.t])
        ts("dve", hp8[:, 1:2], hp8[:, 1:2], -1.0, None, ALU.mult, None, [hp8.t], [hp8.t])
        memset("dve", hp8[:, 2:3], 1.0, [hp8.t])
        cst = A["cst"]
        src = I["a_conv"].rearrange("s r c -> (s r) c")
        for sec in range(3):
            xt = xin[sec % 2]
            dma("sp", xt[0:12, 0:D], src[:, sec * 1024:(sec + 1) * 1024], (), [xt.t], xt.t)
            p_ = ps()
            for jj in range(8):
                tr(p_[:, jj * 12:(jj + 1) * 12], xt[0:12, jj * 128:(jj + 1) * 128], ident[0:12, 0:12], [xt.t, ident.t], [p_.t])
            evac(cst[:, sec * 8:(sec + 1) * 8, :, :], p_[:, 0:96].rearrange("p (j s r) -> p j s r", j=8, s=4), [p_.t], [cst.t])

    def mixer_a(sbk):
        A = MA
        S, ctail, ctail_s, cst, cw, nw, hp8, sel = A["S"], A["ctail"], A["ctail_s"], A["cst"], A["cw"], A["nw"], A["hp8"], A["sel"]
        bt, gt, GT = A["bt"], A["gt"], A["GT"]
        gtok, btok, Gtok, ekl, eGtok, bke = A["gtok"], A["btok"], A["Gtok"], A["ekl"], A["eGtok"], A["bke"]
        cv, zs, kT, kbT, qT, eGb = A["cv"], A["zs"], A["kT"], A["kbT"], A["qT"], A["eGb"]
        RHSk, RHSv, khat, att, TTb = A["RHSk"], A["RHSv"], A["khat"], A["att"], A["TTb"]
        Pm, nWT = A["Pm"], A["nWT"]
        w_in = I["w_in_a"]
        for (b, t0, n, off) in seg_list(sbk):
            prompt = b < 4
            NSEG, SEGL, L, NCH = (1, 512, 64, 8) if prompt else (4, 8, 8, 4)
            sl = A["sl64"] if prompt else A["sl8"]
            W8 = NCH * 8
            nlev = 5 if prompt else 2
            w, wv = wload(w_in[:, 4096:4112].rearrange("(k p) g -> p k g", p=128))
            pb = ps()
            for c in range(KC):
                mm(pb[0:8, 0:n], wv[:, c, 0:8], xn[:, c, off:off + n], c == 0, c == KC - 1, [w.t, xn.t], [pb.t])
            act(bt[0:8, 0:n], pb[0:8, 0:n], AF.Sigmoid, [pb.t], [bt.t])
            pa = ps()
            for c in range(KC):
                mm(pa[0:8, 0:n], wv[:, c, 8:16], xn[:, c, off:off + n], c == 0, c == KC - 1, [w.t, xn.t], [pa.t])
            act(gt[0:8, 0:n], pa[0:8, 0:n], AF.Exp, [pa.t, hp8.t], [gt.t], bias=hp8[:, 0:1])
            act(gt[0:8, 0:n], gt[0:8, 0:n], AF.Ln, [gt.t, hp8.t], [gt.t], bias=hp8[:, 2:3])
            ts("dve", gt[0:8, 0:n], gt[0:8, 0:n], hp8[:, 1:2], None, ALU.mult, None, [gt.t, hp8.t], [gt.t])
            pt1 = ps()
            for c in range(NCH):
                tr(pt1[0:L, c * 8:(c + 1) * 8], gt[0:8, c * L:(c + 1) * L], ident[0:8, 0:8], [gt.t, ident.t], [pt1.t])
            evac(gtok[0:L, 0:W8], pt1[0:L, 0:W8], [pt1.t], [gtok.t])
            pt2 = ps()
            for c in range(NCH):
                tr(pt2[0:L, c * 8:(c + 1) * 8], bt[0:8, c * L:(c + 1) * L], ident[0:8, 0:8], [bt.t, ident.t], [pt2.t])
            evac(btok[0:L, 0:W8], pt2[0:L, 0:W8], [pt2.t], [btok.t])
            pG = ps()
            mm(pG[0:L, 0:W8], triu_f[0:L, 0:L], gtok[0:L, 0:W8], True, True, [triu_f.t, gtok.t], [pG.t])
            evac(Gtok[0:L, 0:W8], pG[0:L, 0:W8], [pG.t], [Gtok.t])
            pGT = ps()
            for c in range(NCH):
                mm(pGT[0:8, c * L:(c + 1) * L], gtok[0:L, c * 8:(c + 1) * 8], triu_f[0:L, 0:L], True, True, [gtok.t, triu_f.t], [pGT.t])
            evac(GT[0:8, 0:n], pGT[0:8, 0:n], [pGT.t], [GT.t])
            pGl = ps()
            mm(pGl[0:L, 0:W8], sl[0:L, 0:L], Gtok[0:L, 0:W8], True, True, [sl.t, Gtok.t], [pGl.t])
            tt("dve", ekl[0:L, 0:W8], pGl[0:L, 0:W8], Gtok[0:L, 0:W8], ALU.subtract, [pGl.t, Gtok.t], [ekl.t])
            act(ekl[0:L, 0:W8], ekl[0:L, 0:W8], AF.Exp, [ekl.t], [ekl.t])
            act(eGtok[0:L, 0:W8], Gtok[0:L, 0:W8], AF.Exp, [Gtok.t], [eGtok.t])
            tt("dve", bke[0:L, 0:W8], btok[0:L, 0:W8], eGtok[0:L, 0:W8], ALU.mult, [btok.t, eGtok.t], [bke.t])

            def hcol(t_, h):
                return t_[0:L, 0:W8].rearrange("p (c h) -> p c h", h=8)[:, :, h].unsqueeze(2).to_broadcast([L, NCH, 128])

            for h in range(8):
                _wi[0] += 1
                w = WR[_wi[0] % NW]
                wv = w[:, 0:4096].rearrange("p (k s g) -> p k s g", k=8, s=4)
                for sec in range(4):
                    c0 = sec * 1024 + h * 128
                    dma("pool", wv[:, :, sec, :], w_in[:, c0:c0 + 128].rearrange("(k p) g -> p k g", p=128), (), [w.t], w.t)
                xx3 = Pm[:, 0:NSEG * (3 + SEGL)].rearrange("p (s l) -> p s l", s=NSEG)
                for sec in range(3):
                    j = sec * 8 + h
                    p_ = ps()
                    for c in range(KC):
                        mm(p_[:, 0:n], wv[:, c, sec, :], xn[:, c, off:off + n], c == 0, c == KC - 1, [w.t, xn.t], [p_.t])
                    if prompt:
                        cp("dve", xx3[:, 0, 0:3], ctail[:, j, :], [ctail.t], [Pm.t])
                    else:
                        cp("dve", xx3[:, :, 0:3], cst[:, j, :, :], [cst.t], [Pm.t])
                    evac(xx3[:, :, 3:3 + SEGL], p_[:, 0:n].rearrange("p (s l) -> p s l", s=NSEG), [p_.t], [Pm.t])
                    if prompt:
                        cp("dve", ctail[:, j, :], xx3[:, 0, SEGL:SEGL + 3], [Pm.t], [ctail.t])
                    else:
                        cp("dve", ctail_s[:, j, :, :], xx3[:, :, SEGL:SEGL + 3], [Pm.t], [ctail_s.t])
                    o3 = cv[sec][:, 0:n].rearrange("p (s l) -> p s l", s=NSEG)
                    ts("dve", o3, xx3[:, :, 0:SEGL], cw[:, j, 0:1], None, ALU.mult, None, [Pm.t, cw.t], [cv[sec].t])
                    for i in range(1, 4):
                        stt("dve", o3, xx3[:, :, i:i + SEGL], cw[:, j, i:i + 1], o3, ALU.mult, ALU.add, [Pm.t, cw.t, cv[sec].t], [cv[sec].t])
                    act(cv[sec][:, 0:n], cv[sec][:, 0:n], AF.Silu, [cv[sec].t], [cv[sec].t])
                p_ = ps()
                for c in range(KC):
                    mm(p_[:, 0:n], wv[:, c, 3, :], xn[:, c, off:off + n], c == 0, c == KC - 1, [w.t, xn.t], [p_.t])
                act(zs[:, 0:n], p_[:, 0:n], AF.Silu, [p_.t], [zs.t])
                for sec in range(2):
                    sq = sqs[sec]
                    act(sq[:, 0:n], cv[sec][:, 0:n], AF.Square, [cv[sec].t], [sq.t])
                    p2 = ps()
                    mm(p2[:, 0:n], ones_b[:, :], sq[:, 0:n], True, True, [ones_b.t, sq.t], [p2.t])
                    rsqrt_eps(rq[:, 0:n], p2[:, 0:n], EPS, [p2.t], [rq.t])
                    stt("dve", cv[sec][:, 0:n], cv[sec][:, 0:n], (128.0 ** -0.5) if sec == 0 else 1.0, rq[:, 0:n], ALU.mult, ALU.mult,
                        [cv[sec].t, rq.t], [cv[sec].t])
                qn, kn, vc = cv[0], cv[1], cv[2]
                dec = A["XTb"]
                pGb = ps()
                mm(pGb[:, 0:n], sel[0:8, h, :], GT[0:8, 0:n], True, True, [sel.t, GT.t], [pGb.t])
                for c in range(NCH):
                    ts("dve", dec[0:L, c * L:(c + 1) * L], pGb[0:L, c * L:(c + 1) * L], Gtok[0:L, c * 8 + h:c * 8 + h + 1], 0.0,
                       ALU.subtract, ALU.min, [pGb.t, Gtok.t], [dec.t])
                act(eGb[:, 0:n], pGb[:, 0:n], AF.Exp, [pGb.t], [eGb.t])
                act(dec[0:L, 0:n], dec[0:L, 0:n], AF.Exp, [dec.t], [dec.t])
                d3 = dec[0:L, 0:n].rearrange("p (c l) -> p c l", c=NCH)
                tt("dve", d3, d3, triu_f[0:L, 0:L].unsqueeze(1).to_broadcast([L, NCH, L]), ALU.mult, [dec.t, triu_f.t], [dec.t])
                pBb = ps()
                mm(pBb[:, 0:n], sel[0:8, h, :], bt[0:8, 0:n], True, True, [sel.t, bt.t], [pBb.t])
                tt("dve", kbT[:, 0:n], kn[:, 0:n], pBb[:, 0:n], ALU.mult, [kn.t, pBb.t], [kbT.t])
                cp("act", kT[:, 0:n], kn[:, 0:n], [kn.t], [kT.t])
                cp("act", qT[:, 0:n], qn[:, 0:n], [qn.t], [qT.t])
                for half in range((NCH + 3) // 4):
                    cs_ = list(range(half * 4, min(NCH, half * 4 + 4)))
                    pk = ps()
                    for ci, c in enumerate(cs_):
                        tr(pk[0:L, ci * 128:(ci + 1) * 128], kn[:, c * L:(c + 1) * L], ident[:, :], [kn.t, ident.t], [pk.t])
                    nc_ = len(cs_)
                    pk3 = pk[0:L, 0:nc_ * 128].rearrange("p (c d) -> p c d", c=nc_)
                    tt("dve", RHSk[0:L, cs_[0]:cs_[0] + nc_, :], pk3, hcol(bke, h)[:, cs_[0]:cs_[0] + nc_, :], ALU.mult, [pk.t, bke.t], [RHSk.t])
                    tt("dve", khat[0:L, cs_[0]:cs_[0] + nc_, :], pk3, hcol(ekl, h)[:, cs_[0]:cs_[0] + nc_, :], ALU.mult, [pk.t, ekl.t], [khat.t])
                    pv_ = ps()
                    for ci, c in enumerate(cs_):
                        tr(pv_[0:L, ci * 128:(ci + 1) * 128], vc[:, c * L:(c + 1) * L], ident[:, :], [vc.t, ident.t], [pv_.t])
                    pv3 = pv_[0:L, 0:nc_ * 128].rearrange("p (c d) -> p c d", c=nc_)
                    tt("dve", RHSv[0:L, cs_[0]:cs_[0] + nc_, :], pv3, hcol(btok, h)[:, cs_[0]:cs_[0] + nc_, :], ALU.mult, [pv_.t, btok.t], [RHSv.t])
                tt("dve", qn[:, 0:n], qn[:, 0:n], eGb[:, 0:n], ALU.mult, [qn.t, eGb.t], [qn.t])
                qtil = qn
                X, XT, X2, X2T = A["Xa"], A["XTa"], A["Xb"], A["XTb"]
                pKK = ps()
                for c in range(NCH):
                    mm(pKK[0:L, c * L:(c + 1) * L], kT[:, c * L:(c + 1) * L], kbT[:, c * L:(c + 1) * L], True, True, [kT.t, kbT.t], [pKK.t])
                tt("dve", X[0:L, 0:n], pKK[0:L, 0:n], dec[0:L, 0:n], ALU.mult, [pKK.t, dec.t], [X.t])
                x3 = X[0:L, 0:n].rearrange("p (c l) -> p c l", c=NCH)
                stt("dve", x3, x3, -1.0, striu_f[0:L, 0:L].unsqueeze(1).to_broadcast([L, NCH, L]), ALU.mult, ALU.mult, [X.t, striu_f.t], [X.t])
                pQK = ps()
                for c in range(NCH):
                    mm(pQK[0:L, c * L:(c + 1) * L], kT[:, c * L:(c + 1) * L], qT[:, c * L:(c + 1) * L], True, True, [kT.t, qT.t], [pQK.t])
                tt("dve", att[0:L, 0:n], pQK[0:L, 0:n], dec[0:L, 0:n], ALU.mult, [pQK.t, dec.t], [att.t])
                pXT = ps()
                for c in range(NCH):
                    tr(pXT[0:L, c * L:(c + 1) * L], X[0:L, c * L:(c + 1) * L], ident[0:L, 0:L], [X.t, ident.t], [pXT.t])
                evac(XT[0:L, 0:n], pXT[0:L, 0:n], [pXT.t], [XT.t])
                p3 = Pm[0:L, 0:n].rearrange("p (c l) -> p c l", c=NCH)
                tt("dve", p3, x3, ident[0:L, 0:L].unsqueeze(1).to_broadcast([L, NCH, L]), ALU.add, [X.t, ident.t], [Pm.t])
                for lev in range(nlev):
                    pX2 = ps()
                    for c in range(NCH):
                        mm(pX2[0:L, c * L:(c + 1) * L], XT[0:L, c * L:(c + 1) * L], X[0:L, c * L:(c + 1) * L], True, True, [XT.t, X.t], [pX2.t])
                    pX2T = ps()
                    for c in range(NCH):
                        mm(pX2T[0:L, c * L:(c + 1) * L], X[0:L, c * L:(c + 1) * L], XT[0:L, c * L:(c + 1) * L], True, True, [XT.t, X.t], [pX2T.t])
                    evac(X2[0:L, 0:n], pX2[0:L, 0:n], [pX2.t], [X2.t])
                    evac(X2T[0:L, 0:n], pX2T[0:L, 0:n], [pX2T.t], [X2T.t])
                    pP = ps()
                    for c in range(NCH):
                        mm(pP[0:L, c * L:(c + 1) * L], X2T[0:L, c * L:(c + 1) * L], Pm[0:L, c * L:(c + 1) * L], True, True, [X2T.t, Pm.t], [pP.t])
                    tt("dve", Pm[0:L, 0:n], Pm[0:L, 0:n], pP[0:L, 0:n], ALU.add, [Pm.t, pP.t], [Pm.t])
                    X, XT, X2, X2T = X2, X2T, X, XT
                cp("act", TTb[0:L, 0:n], Pm[0:L, 0:n], [Pm.t], [TTb.t])
                pW = ps()
                for c in range(NCH):
                    mm(pW[:, c * L:(c + 1) * L], RHSk[0:L, c, :], TTb[0:L, c * L:(c + 1) * L], True, True, [RHSk.t, TTb.t], [pW.t])
                ts("dve", nWT[:, 0:n], pW[:, 0:n], -1.0, None, ALU.mult, None, [pW.t], [nWT.t])
                po = psacc
                for c in range(NCH):
                    csl = slice(c * L, (c + 1) * L)
                    if prompt:
                        S_ap, S_tok = S[:, h, :], S.t
                    else:
                        Sl = A["Sl"][c % 2]
                        dma("sp", Sl[:, :], I["a_ssm"][c, h], (), [Sl.t], Sl.t)
                        S_ap, S_tok = Sl[:, :], Sl.t
                    u = A["u"][c % 2]
                    pu = ps()
                    mm(pu[0:L, 0:128], TTb[0:L, csl], RHSv[0:L, c, :], True, False, [TTb.t, RHSv.t], [pu.t])
                    mm(pu[0:L, 0:128], nWT[:, csl], S_ap, False, True, [nWT.t, S_tok], [pu.t])
                    evac(u[0:L, :], pu[0:L, 0:128], [pu.t], [u.t])
                    mm(po[:, csl], S_ap, qtil[:, csl], True, False, [S_tok, qtil.t], [po.t])
                    mm(po[:, csl], u[0:L, :], att[0:L, csl], False, True, [u.t, att.t], [po.t])
                    pS = ps()
                    mm(pS[:, 0:128], khat[0:L, c, :], u[0:L, :], True, True, [khat.t, u.t], [pS.t])
                    egl = eGb[:, c * L + L - 1:c * L + L]
                    if prompt:
                        stt("dve", S[:, h, :], S[:, h, :], egl, pS[:, 0:128], ALU.mult, ALU.add, [S.t, eGb.t, pS.t], [S.t])
                    else:
                        So = A["So"][c % 2]
                        stt("dve", So[:, :], S_ap, egl, pS[:, 0:128], ALU.mult, ALU.add, [S_tok, eGb.t, pS.t], [So.t])
                        dma("sp", O["a_ssm_s"][c, h], So[:, :], [So.t], [otok["a_ssm_s"]], So.t)
                sq = sqs[0]
                act(sq[:, 0:n], po[:, 0:n], AF.Square, [po.t], [sq.t])
                p2 = ps()
                mm(p2[:, 0:n], ones_b[:, :], sq[:, 0:n], True, True, [ones_b.t, sq.t], [p2.t])
                rsqrt_eps(rq[:, 0:n], p2[:, 0:n], 128.0 * EPS, [p2.t], [rq.t])
                stt("dve", rq[:, 0:n], po[:, 0:n], nw[:, 0:1], rq[:, 0:n], ALU.mult, ALU.mult, [po.t, nw.t, rq.t], [rq.t])
                tt("dve", moT[:, h, off:off + n], rq[:, 0:n], zs[:, 0:n], ALU.mult, [rq.t, zs.t], [moT.t])

    def finish_a():
        A = MA
        S, ctail, ctail_s = A["S"], A["ctail"], A["ctail_s"]
        dma("sp", O["a_ssm_p"].rearrange("h k v -> k h v"), S[:, :, :], [S.t], [otok["a_ssm_p"]], S.t)
        cnt = 0
        for s_ in range(5):
            for sec in range(3):
                xt = xin[cnt % 2]
                cnt += 1
                for half in range(2):
                    p_ = ps()
                    for jj in range(4):
                        j = sec * 8 + half * 4 + jj
                        src = ctail[:, j, :] if s_ == 4 else ctail_s[:, j, s_, :]
                        tr(p_[0:3, jj * 128:(jj + 1) * 128], src, ident[:, :], [ctail.t, ctail_s.t, ident.t], [p_.t])
                    evac(xt[0:3, half * 512:(half + 1) * 512], p_[0:3, :], [p_.t], [xt.t])
                if s_ == 4:
                    dma("sp", O["a_conv_p"][:, sec * 1024:(sec + 1) * 1024], xt[0:3, 0:D], [xt.t], [otok["a_conv_p"]], xt.t)
                else:
                    dma("sp", O["a_conv_s"][s_, :, sec * 1024:(sec + 1) * 1024], xt[0:3, 0:D], [xt.t], [otok["a_conv_s"]], xt.t)


    MB = {}

    def setup_b(l):
        arena_reset()
        B = MB
        B["S"] = aalloc("bS", [128, 8, 128])
        B["lb4"] = aalloc("lb4", [128, 8, 4])
        B["lbc"] = aalloc("lbc", [128, 8])
        B["oml"] = aalloc("oml", [128, 8])
        B["noml"] = aalloc("noml", [128, 8])
        B["nw"] = aalloc("bnw", [128, 1])
        B["sg"] = aalloc("sg", [128, 544])
        B["c0"] = aalloc("c0", [128, 544])
        B["c1"] = aalloc("c1", [128, 544])
        B["qf"] = aalloc("qf", [128, 544])
        B["kk"] = aalloc("kk", [128, 544])
        B["kh"] = aalloc("kh", [128, 544])
        B["iT"] = aalloc("iT", [128, 544])
        B["zs"] = aalloc("bzs", [128, 544], BF16)
        B["qtb"] = aalloc("qtb", [128, 544], BF16)
        B["ktb"] = aalloc("ktb", [128, 544], BF16)
        B["attm"] = aalloc("attm", [32, 512], BF16)
        B["vtok"] = aalloc("vtok", [32, 16, 128], BF16)
        B["ktok"] = aalloc("ktok", [32, 16, 128], BF16)
        B["eBl"] = aalloc("eBl", [128, 16])
        B["Sl"] = [aalloc("bSl%d" % i, [128, 128]) for i in range(2)]
        B["So"] = [aalloc("bSo%d" % i, [128, 128]) for i in range(2)]
        S, lb4, lbc, oml, noml, nw = B["S"], B["lb4"], B["lbc"], B["oml"], B["noml"], B["nw"]
        memset("dve", S[:, :, :], 0.0, [S.t])
        for li in range(4):
            dma("sp", lb4[:, :, li], I["hg_lb"][li].rearrange("(c p) -> p c", p=128), (), [lb4.t], lb4.t, slow=True)
        act(lb4[:, :, :], lb4[:, :, :], AF.Exp, [lb4.t], [lb4.t])
        P.add("dve", lambda e, o=oml[:, :], i_=lb4[:, :, :]: e.tensor_reduce(o, i_, AX.X, ALU.add), [lb4.t], [oml.t])
        P.add("dve", lambda e, o=oml[:, :]: e.reciprocal(o, o), [oml.t], [oml.t])
        memset("dve", lbc[:, :], 0.0, [lbc.t])
        for li in range(1, l + 1):
            tt("dve", lbc[:, :], lbc[:, :], lb4[:, :, li], ALU.add, [lbc.t, lb4.t], [lbc.t])
        tt("dve", lbc[:, :], lbc[:, :], oml[:, :], ALU.mult, [lbc.t, oml.t], [lbc.t])
        ts("dve", oml[:, :], lbc[:, :], -1.0, 1.0, ALU.mult, ALU.add, [lbc.t], [oml.t])
        ts("dve", noml[:, :], oml[:, :], -1.0, None, ALU.mult, None, [oml.t], [noml.t])
        dma("sp", nw[:, :], I["b_norm_w"].rearrange("o d -> d o"), (), [nw.t], nw.t, slow=True)
        ts("dve", nw[:, :], nw[:, :], float(np.sqrt(128.0)), None, ALU.mult, None, [nw.t], [nw.t])

    def mixer_b(sbk):
        B = MB
        S, lbc, oml, noml, nw = B["S"], B["lbc"], B["oml"], B["noml"], B["nw"]
        sg, c0, c1, qf, kk, kh, iT, zs, qtb, ktb = B["sg"], B["c0"], B["c1"], B["qf"], B["kk"], B["kh"], B["iT"], B["zs"], B["qtb"], B["ktb"]
        attm, vtok, ktok, eBl = B["attm"], B["vtok"], B["ktok"], B["eBl"]
        w_in = I["w_in_b"]
        for (b, t0, n, off) in seg_list(sbk):
            prompt = b < 4
            L, NCH = (32, 16) if prompt else (8, 4)
            for h in range(8):
                _wi[0] += 1
                w = WR[_wi[0] % NW]
                wv = w[:, 0:4096].rearrange("p (k s g) -> p k s g", k=8, s=4)
                for sec in range(4):
                    c0_ = sec * 1024 + h * 128
                    dma("pool", wv[:, :, sec, :], w_in[:, c0_:c0_ + 128].rearrange("(k p) g -> p k g", p=128), (), [w.t], w.t)

                def proj(sec):
                    p_ = ps()
                    for c in range(KC):
                        mm(p_[:, 0:n], wv[:, c, sec, :], xn[:, c, off:off + n], c == 0, c == KC - 1, [w.t, xn.t], [p_.t])
                    return p_
                p_ = proj(0)
                act(qf[:, 0:n], p_[:, 0:n], AF.Silu, [p_.t], [qf.t])
                p_ = proj(1)
                act(sg[:, 0:n], p_[:, 0:n], AF.Sigmoid, [p_.t], [sg.t])
                ts("dve", kk[:, 0:n], sg[:, 0:n], noml[:, h:h + 1], oml[:, h:h + 1], ALU.mult, ALU.add, [sg.t, noml.t, oml.t], [kk.t])
                ts("dve", c0[:, 0:n], sg[:, 0:n], oml[:, h:h + 1], lbc[:, h:h + 1], ALU.mult, ALU.add, [sg.t, oml.t, lbc.t], [c0.t])
                act(c0[:, 0:n], c0[:, 0:n], AF.Ln, [c0.t], [c0.t])
                p_ = proj(2)
                evac(iT[:, 0:n], p_[:, 0:n], [p_.t], [iT.t])
                p_ = proj(3)
                act(zs[:, 0:n], p_[:, 0:n], AF.Silu, [p_.t], [zs.t])
                src, dst = c0, c1
                k_ = 1
                while k_ < L:
                    s3 = src[:, 0:n].rearrange("p (c l) -> p c l", c=NCH)
                    d3 = dst[:, 0:n].rearrange("p (c l) -> p c l", c=NCH)
                    cp("act", d3[:, :, 0:k_], s3[:, :, 0:k_], [src.t], [dst.t])
                    tt("dve", d3[:, :, k_:L], s3[:, :, k_:L], s3[:, :, 0:L - k_], ALU.add, [src.t], [dst.t])
                    src, dst = dst, src
                    k_ *= 2
                Bc, tmp = src, dst
                Bc3 = Bc[:, 0:n].rearrange("p (c l) -> p c l", c=NCH)
                t3 = tmp[:, 0:n].rearrange("p (c l) -> p c l", c=NCH)
                tt("dve", t3, Bc3[:, :, L - 1:L].to_broadcast([128, NCH, L]), Bc3, ALU.subtract, [Bc.t], [tmp.t])
                act(tmp[:, 0:n], tmp[:, 0:n], AF.Exp, [tmp.t], [tmp.t])
                tt("dve", kh[:, 0:n], kk[:, 0:n], tmp[:, 0:n], ALU.mult, [kk.t, tmp.t], [kh.t])
                act(eBl[:, 0:NCH], Bc3[:, :, L - 1], AF.Exp, [Bc.t], [eBl.t])
                act(tmp[:, 0:n], Bc[:, 0:n], AF.Exp, [Bc.t], [tmp.t])
                stt("dve", qf[:, 0:n], qf[:, 0:n], 128.0 ** -0.5, tmp[:, 0:n], ALU.mult, ALU.mult, [qf.t, tmp.t], [qf.t])
                cp("act", qtb[:, 0:n], qf[:, 0:n], [qf.t], [qtb.t])
                act(tmp[:, 0:n], Bc[:, 0:n], AF.Exp, [Bc.t], [tmp.t], scale=-1.0)
                tt("dve", ktb[:, 0:n], kk[:, 0:n], tmp[:, 0:n], ALU.mult, [kk.t, tmp.t], [ktb.t])
                for half in range((NCH * L + 511) // 512):
                    pass
                pA = ps()
                for c in range(NCH):
                    mm(pA[0:L, c * L:(c + 1) * L], ktb[:, c * L:(c + 1) * L], qtb[:, c * L:(c + 1) * L], True, True, [ktb.t, qtb.t], [pA.t])
                tt("dve", attm[0:L, 0:n].rearrange("p (c l) -> p c l", c=NCH), pA[0:L, 0:n].rearrange("p (c l) -> p c l", c=NCH),
                   triu_f[0:L, 0:L].unsqueeze(1).to_broadcast([L, NCH, L]), ALU.mult, [pA.t, triu_f.t], [attm.t])
                for q4 in range((NCH + 3) // 4):
                    cs_ = list(range(q4 * 4, min(NCH, q4 * 4 + 4)))
                    nc_ = len(cs_)
                    pk = ps()
                    for ci, c in enumerate(cs_):
                        tr(pk[0:L, ci * 128:(ci + 1) * 128], kh[:, c * L:(c + 1) * L], ident[:, :], [kh.t, ident.t], [pk.t])
                    evac(ktok[0:L, cs_[0]:cs_[0] + nc_, :], pk[0:L, 0:nc_ * 128].rearrange("p (c d) -> p c d", c=nc_), [pk.t], [ktok.t])
                    pv_ = ps()
                    for ci, c in enumerate(cs_):
                        tr(pv_[0:L, ci * 128:(ci + 1) * 128], iT[:, c * L:(c + 1) * L], ident[:, :], [iT.t, ident.t], [pv_.t])
                    evac(vtok[0:L, cs_[0]:cs_[0] + nc_, :], pv_[0:L, 0:nc_ * 128].rearrange("p (c d) -> p c d", c=nc_), [pv_.t], [vtok.t])
                po = psacc
                for c in range(NCH):
                    csl = slice(c * L, (c + 1) * L)
                    if prompt:
                        S_ap, S_tok = S[:, h, :], S.t
                    else:
                        Sl = B["Sl"][c % 2]
                        dma("sp", Sl[:, :], I["b_ssm"][c, h], (), [Sl.t], Sl.t)
                        S_ap, S_tok = Sl[:, :], Sl.t
                    mm(po[:, csl], S_ap, qf[:, csl], True, False, [S_tok, qf.t], [po.t])
                    mm(po[:, csl], vtok[0:L, c, :], attm[0:L, csl], False, True, [vtok.t, attm.t], [po.t])
                    pS = ps()
                    mm(pS[:, 0:128], ktok[0:L, c, :], vtok[0:L, c, :], True, True, [ktok.t, vtok.t], [pS.t])
                    if prompt:
                        stt("dve", S[:, h, :], S[:, h, :], eBl[:, c:c + 1], pS[:, 0:128], ALU.mult, ALU.add, [S.t, eBl.t, pS.t], [S.t])
                    else:
                        So = B["So"][c % 2]
                        stt("dve", So[:, :], S_ap, eBl[:, c:c + 1], pS[:, 0:128], ALU.mult, ALU.add, [S_tok, eBl.t, pS.t], [So.t])
                        dma("sp", O["b_ssm_s"][c, h], So[:, :], [So.t], [otok["b_ssm_s"]], So.t)
                sq = sqs[0]
                act(sq[:, 0:n], po[:, 0:n], AF.Square, [po.t], [sq.t])
                p2 = ps()
                mm(p2[:, 0:n], ones_b[:, :], sq[:, 0:n], True, True, [ones_b.t, sq.t], [p2.t])
                rsqrt_eps(rq[:, 0:n], p2[:, 0:n], 128.0 * EPS, [p2.t], [rq.t])
                stt("dve", rq[:, 0:n], po[:, 0:n], nw[:, 0:1], rq[:, 0:n], ALU.mult, ALU.mult, [po.t, nw.t, rq.t], [rq.t])
                tt("dve", moT[:, h, off:off + n], rq[:, 0:n], zs[:, 0:n], ALU.mult, [rq.t, zs.t], [moT.t])

    def finish_b():
        S = MB["S"]
        dma("sp", O["b_ssm_p"].rearrange("h k v -> k h v"), S[:, :, :], [S.t], [otok["b_ssm_p"]], S.t)


    MC = {}

    def setup_c():
        arena_reset()
        C = MC
        C["qgc"] = aalloc("qgc", [128, 1])
        C["kgc"] = aalloc("kgc", [128, 1])
        C["kgb"] = aalloc("kgb", [128, 64])
        C["fb"] = aalloc("fb", [128, 16])
        C["lf"] = aalloc("lf", [128, 17, 16])
        C["Ft"] = aalloc("Ft", [128, 16, 16])
        C["Pre"] = aalloc("Pre", [128, 17, 16])
        C["bias"] = aalloc("bias", [128, 16])
        C["off_q"] = ast["off"]
        C["qTn"] = aalloc("qTn", [128, 8, 544], BF16)
        C["ogs"] = aalloc("ogs", [128, 8, 544], BF16)
        C["ksq"] = T(P, aT.h[:, :, :].rearrange("p a b -> p (a b)").bitcast(F32)[:, 0:512], "cksq")
        C["ksq"].t = aT.t
        C["kss"] = aalloc("ckss", [128, 16])
        C["pt"] = [aalloc("cpt%d" % i, [128, 128], BF16) for i in range(3)]
        C["Va"] = [aalloc("Va%d" % i, [128, 16, 65], BF16) for i in range(2)]
        C["ot"] = xin[0]
        C["rec"] = aalloc("rec", [128, 16])
        C["lfh"] = aalloc("lfh", [128, 16], BF16)
        C["lfl"] = aalloc("lfl", [128, 16], BF16)
        C["lft"] = aalloc("lft", [128, 16])
        C["off_kt"] = ast["off"]
        C["KT"] = aalloc("KT", [128, 8, 2048], BF16)
        C["off_kts"] = ast["off"]
        C["kTs"] = aalloc("kTs", [128, 8, 32], BF16)
        C["qTs"] = aalloc("qTs", [128, 8, 32], BF16)
        C["ogss"] = aalloc("ogss", [128, 8, 32], BF16)
        qgc, kgc, kgb, fb, Pre = C["qgc"], C["kgc"], C["kgb"], C["fb"], C["Pre"]
        for half in range(2):
            dma("sp", qgc[half * 64:(half + 1) * 64, :], I["c_qnorm"].rearrange("o d -> d o"), (), [qgc.t], qgc.t, slow=True)
            dma("sp", kgc[half * 64:(half + 1) * 64, :], I["c_knorm"].rearrange("o d -> d o"), (), [kgc.t], kgc.t, slow=True)
        ts("dve", kgc[:, :], kgc[:, :], 8.0, None, ALU.mult, None, [kgc.t], [kgc.t])
        dma("sp", kgb[:, :], I["c_knorm"].partition_broadcast(128), (), [kgb.t], kgb.t, slow=True)
        ts("dve", kgb[:, :], kgb[:, :], 8.0, None, ALU.mult, None, [kgb.t], [kgb.t])
        dma("sp", fb[:, :], I["c_fbias"].partition_broadcast(128), (), [fb.t], fb.t, slow=True)
        memset("dve", Pre[:, :, :], 0.0, [Pre.t])
        for v_ in C["Va"]:
            memset("dve", v_[:, :, :], 1.0, [v_.t])
        C["vtok"] = [P.tok("cv_tile%d" % j) for j in range(17)]
        C["vi"] = 0

    def mixer_c(sbk):
        C = MC
        qgc, kgc, kgb, fb, lf, Ft, Pre, bias = C["qgc"], C["kgc"], C["kgb"], C["fb"], C["lf"], C["Ft"], C["Pre"], C["bias"]
        qTn, ogs, ksq, kss, KT, kTs, ot, rec = C["qTn"], C["ogs"], C["ksq"], C["kss"], C["KT"], C["kTs"], C["ot"], C["rec"]
        w_in = I["w_in_c"]
        segs = seg_list(sbk)
        PSROT[0] = 2
        psets = [[PS[2], PS[3], PS[4]], [PS[5], PS[6], PS[7]]]
        accs = xin[1]
        npair = [0]
        for sec in range(2):
            for g in range(2):
                w, wv = wload(w_in[:, sec * 1024 + g * 512:sec * 1024 + (g + 1) * 512].rearrange("(k p) g -> p k g", p=128))
                for m in range(4):
                    hp = g * 4 + m
                    for (b, t0, n, off) in segs:
                        p_ = ps()
                        for c in range(KC):
                            mm(p_[:, 0:n], wv[:, c, m * 128:(m + 1) * 128], xn[:, c, off:off + n], c == 0, c == KC - 1, [w.t, xn.t], [p_.t])
                        act(sq2[:, 0:n], p_[:, 0:n], AF.Square, [p_.t], [sq2.t])
                        p2 = ps()
                        mm(p2[:, 0:n], blk_b[:, :], sq2[:, 0:n], True, True, [blk_b.t, sq2.t], [p2.t])
                        rsqrt_eps(rq[:, 0:n], p2[:, 0:n], 64.0 * EPS, [p2.t], [rq.t])
                        if sec == 0:
                            stt("dve", qTn[:, hp, off:off + n], p_[:, 0:n], qgc[:, 0:1], rq[:, 0:n], ALU.mult, ALU.mult, [p_.t, qgc.t, rq.t], [qTn.t])
                        elif b < 4:
                            stt("dve", KT[:, hp, t0:t0 + n], p_[:, 0:n], kgc[:, 0:1], rq[:, 0:n], ALU.mult, ALU.mult, [p_.t, kgc.t, rq.t], [KT.t])
                        else:
                            stt("dve", kTs[:, hp, 0:n], p_[:, 0:n], kgc[:, 0:1], rq[:, 0:n], ALU.mult, ALU.mult, [p_.t, kgc.t, rq.t], [kTs.t])
        for g in range(2):
            w, wv = wload(w_in[:, 3072 + g * 512:3072 + (g + 1) * 512].rearrange("(k p) g -> p k g", p=128))
            for m in range(4):
                for (b, t0, n, off) in segs:
                    p_ = ps()
                    for c in range(KC):
                        mm(p_[:, 0:n], wv[:, c, m * 128:(m + 1) * 128], xn[:, c, off:off + n], c == 0, c == KC - 1, [w.t, xn.t], [p_.t])
                    act(ogs[:, g * 4 + m, off:off + n], p_[:, 0:n], AF.Sigmoid, [p_.t], [ogs.t])
        wk0, wk0v = wload(w_in[:, 1024:1536].rearrange("(k p) g -> p k g", p=128))
        wk1, wk1v = wload(w_in[:, 1536:2048].rearrange("(k p) g -> p k g", p=128))
        tiles = []
        for (b, t0, n, off) in segs:
            for i in range((n + 127) // 128):
                nt = min(128, n - i * 128)
                tiles.append((b, t0 + i * 128, off + i * 128, nt))
        for (b, tt0, toff, nt) in tiles:
            xt = xin[C["vi"] % 2]
            C["vi"] += 1
            for hf, (w, wv) in enumerate(((wk0, wk0v), (wk1, wk1v))):
                p_ = ps()
                for c in range(KC):
                    mm(p_[0:nt, :], xn[:, c, toff:toff + nt], wv[:, c, :], c == 0, c == KC - 1, [xn.t, w.t], [p_.t])
                act(ksq[0:nt, :], p_[0:nt, :], AF.Square, [p_.t], [ksq.t])
                P.add("dve", lambda e, o=kss[0:nt, 0:8], i_=ksq[0:nt, :].rearrange("p (h d) -> p h d", h=8): e.tensor_reduce(o, i_, AX.X, ALU.add),
                      [ksq.t], [kss.t])
                rsqrt_eps(kss[0:nt, 0:8], kss[0:nt, 0:8], 64.0 * EPS, [kss.t], [kss.t])
                o3 = xt[0:nt, hf * 512:(hf + 1) * 512].rearrange("p (h d) -> p h d", h=8)
                tt("dve", o3, p_[0:nt, :].rearrange("p (h d) -> p h d", h=8), kss[0:nt, 0:8].unsqueeze(2).to_broadcast([nt, 8, 64]), ALU.mult,
                   [p_.t, kss.t], [xt.t])
                tt("dve", o3, o3, kgb[0:nt, :].unsqueeze(1).to_broadcast([nt, 8, 64]), ALU.mult, [xt.t, kgb.t], [xt.t])
            if b < 4:
                dma("sp", O["c_k_p"][tt0:tt0 + nt, :], xt[0:nt, 0:D], [xt.t], [otok["c_k_p"]], xt.t)
            else:
                dma("sp", O["c_k_s"][:, :], xt[0:nt, 0:D], [xt.t], [otok["c_k_s"]], xt.t)
        wv0, wv0v = wload(w_in[:, 2048:2560].rearrange("(k p) g -> p k g", p=128))
        wv1, wv1v = wload(w_in[:, 2560:3072].rearrange("(k p) g -> p k g", p=128))
        for (b, tt0, toff, nt) in tiles:
            xt = xin[C["vi"] % 2]
            C["vi"] += 1
            for hf, (w, wv) in enumerate(((wv0, wv0v), (wv1, wv1v))):
                p_ = ps()
                for c in range(KC):
                    mm(p_[0:nt, :], xn[:, c, toff:toff + nt], wv[:, c, :], c == 0, c == KC - 1, [xn.t, w.t], [p_.t])
                evac(xt[0:nt, hf * 512:(hf + 1) * 512], p_[0:nt, :], [p_.t], [xt.t])
            if b < 4:
                dma("sp", O["c_v_p"][tt0:tt0 + nt, :], xt[0:nt, 0:D], [xt.t], [C["vtok"][tt0 // 128]], xt.t)
            else:
                dma("sp", O["c_v_s"][:, :], xt[0:nt, 0:D], [xt.t], [C["vtok"][16]], xt.t)
        wf, wfv = wload(w_in[:, 4096:4112].rearrange("(k p) g -> p k g", p=128))
        for (b, tt0, toff, nt) in tiles:
            ti = tt0 // 128
            p_ = ps()
            for c in range(KC):
                mm(p_[0:nt, 0:16], xn[:, c, toff:toff + nt], wfv[:, c, :], c == 0, c == KC - 1, [xn.t, wf.t], [p_.t])
            tt("dve", lf[0:nt, ti, :], p_[0:nt, 0:16], fb[0:nt, :], ALU.add, [p_.t, fb.t], [lf.t])
            act(lf[0:nt, ti, :], lf[0:nt, ti, :], AF.Sigmoid, [lf.t], [lf.t])
            act(lf[0:nt, ti, :], lf[0:nt, ti, :], AF.Ln, [lf.t], [lf.t])
            if b < 4:
                dma("sp", O["c_lf_p"][tt0:tt0 + nt, :], lf[0:nt, ti, :], [lf.t], [otok["c_lf_p"]], lf.t)
                lfh, lfl, lft = C["lfh"], C["lfl"], C["lft"]
                cp("dve", lfh[:, :], lf[:, ti, :], [lf.t], [lfh.t])
                cp("dve", lft[:, :], lfh[:, :], [lfh.t], [lft.t])
                tt("dve", lft[:, :], lf[:, ti, :], lft[:, :], ALU.subtract, [lf.t, lft.t], [lft.t])
                cp("dve", lfl[:, :], lft[:, :], [lft.t], [lfl.t])
                pF = ps()
                mm(pF[:, 0:16], triu_b[:, :], lfh[:, :], True, False, [triu_b.t, lfh.t], [pF.t])
                mm(pF[:, 0:16], triu_b[:, :], lfl[:, :], False, True, [triu_b.t, lfl.t], [pF.t])
                tt("dve", Ft[:, ti, :], pF[:, 0:16], Pre[:, ti, :], ALU.add, [pF.t, Pre.t], [Ft.t])
                pT = ps()
                mm(pT[:, 0:16], ones_b[:, :], lfh[:, :], True, False, [ones_b.t, lfh.t], [pT.t])
                mm(pT[:, 0:16], ones_b[:, :], lfl[:, :], False, True, [ones_b.t, lfl.t], [pT.t])
                tt("dve", Pre[:, ti + 1, :], pT[:, 0:16], Pre[:, ti, :], ALU.add, [pT.t, Pre.t], [Pre.t])
            else:
                dma("sp", O["c_lf_s"][:, :], lf[0:nt, ti, :], [lf.t], [otok["c_lf_s"]], lf.t)
        for (b, tt0, toff, nt) in tiles:
            if b == 4:
                continue
            i = tt0 // 128
            for j in range(i + 1):
                Va = C["Va"][(C["vi"]) % 2]
                C["vi"] += 1
                dma("pool", Va[:, :, 0:64], O["c_v_p"][j * 128:(j + 1) * 128, :].rearrange("t (h d) -> t h d", h=16),
                    [C["vtok"][j]], [Va.t], Va.t)
                tt("dve", bias[:, :], Pre[:, i, :], Ft[:, j, :], ALU.subtract, [Pre.t, Ft.t], [bias.t])
                Aset = psets[npair[0] % 2]
                npair[0] += 1
                scq = {}

                def issue_score(h2, j=j, toff=toff):
                    hp2, lo2 = h2 // 2, (h2 % 2) * 64
                    s_ = ps()
                    mm(s_[:, 0:128], KT[lo2:lo2 + 64, hp2, j * 128:(j + 1) * 128], qTn[lo2:lo2 + 64, hp2, toff:toff + 128], True, True,
                       [KT.t, qTn.t], [s_.t])
                    scq[h2] = s_

                issue_score(0)
                for hh in range(16):
                    if hh + 1 < 16:
                        issue_score(hh + 1)
                    p_ = scq.pop(hh)
                    pt_ = C["pt"][hh % 3]
                    act(pt_[:, :], p_[:, 0:128], AF.Exp, [p_.t, bias.t], [pt_.t], bias=bias[:, hh:hh + 1])
                    if j == i:
                        tt("dve", pt_[:, :], pt_[:, :], triu_f[:, :], ALU.mult, [pt_.t, triu_f.t], [pt_.t])
                    a_ = Aset[hh // 7]
                    col = (hh % 7) * 65
                    mm(a_[:, col:col + 65], pt_[:, :], Va[:, hh, :], True, True, [pt_.t, Va.t], [a_.t])
                for bk in range(3):
                    ncol = 455 if bk < 2 else 130
                    if j == 0:
                        cp("dve", accs[:, bk * 455:bk * 455 + ncol], Aset[bk][:, 0:ncol], [Aset[bk].t], [accs.t])
                    else:
                        tt("dve", accs[:, bk * 455:bk * 455 + ncol], accs[:, bk * 455:bk * 455 + ncol], Aset[bk][:, 0:ncol], ALU.add,
                           [Aset[bk].t, accs.t], [accs.t])
            a3 = accs[:, 0:1040].rearrange("p (h d) -> p h d", h=16)
            P.add("dve", lambda e, o=rec[:, :].unsqueeze(2), i_=a3[:, :, 64:65]: e.reciprocal(o, i_), [accs.t], [rec.t])
            tt("dve", ot[:, 0:1024].rearrange("p (h d) -> p h d", h=16), a3[:, :, 0:64], rec[:, :].unsqueeze(2).to_broadcast([128, 16, 64]), ALU.mult,
               [accs.t, rec.t], [ot.t])
            if DEBUG and i == 1:
                dma("sp", O["dbg2"][:, :], ot[:, :], [ot.t], [dbgtok], ot.t)
                dma("sp", O["dbg1"][:, 0:272], Pre[:, :, :].rearrange("p a b -> p (a b)"), [Pre.t], [dbgtok], Pre.t)
                dma("sp", O["dbg1"][:, 272:528], Ft[:, :, :].rearrange("p a b -> p (a b)"), [Ft.t], [dbgtok], Ft.t)
            for half in range(2):
                p_ = ps()
                for jj in range(4):
                    hp = half * 4 + jj
                    tr(p_[:, jj * 128:(jj + 1) * 128], ot[:, hp * 128:(hp + 1) * 128], ident[:, :], [ot.t, ident.t], [p_.t])
                tt("dve", moT[:, half * 4:half * 4 + 4, toff:toff + 128], p_[:, :].rearrange("p (j n) -> p j n", j=4),
                   ogs[:, half * 4:half * 4 + 4, toff:toff + 128], ALU.mult, [p_.t, ogs.t], [moT.t])
        PSROT[0] = 7
        if sbk[-1] == 4 and not globals().get("SKIP_C_SAMPLE", False):
            mixer_c_sample(sbk)

    def mixer_c_sample(sbk):
        C = MC
        (b, t0, n, off) = seg_list(sbk)[-1]
        qTn, ogs, kTs, qTs, ogss, lf = C["qTn"], C["ogs"], C["kTs"], C["qTs"], C["ogss"], C["lf"]
        cp("dve", qTs[:, :, :], qTn[:, :, off:off + 32], [qTn.t], [qTs.t])
        cp("dve", ogss[:, :, :], ogs[:, :, off:off + 32], [ogs.t], [ogss.t])
        ast["prev"] = ast["prev"] + list(ast["cur"])
        save_off = ast["off"]
        ast["off"] = C["off_q"]
        lfp = aalloc("lfp", [128, 64, 16])
        lph = aalloc("lph", [128, 1024], BF16)
        lpl = aalloc("lpl", [128, 1024], BF16)
        totA = aalloc("totA", [128, 64, 16])
        kst = [aalloc("kst%d" % i, [128, 1024]) for i in range(2)]
        totB = T(P, kst[0].h[:, :].rearrange("p (a b) -> p a b", a=64), "totB")
        totB.t = kst[0].t
        vst = [aalloc("vst%d" % i, [128, 1024]) for i in range(2)]
        KTp = [aalloc("KTp%d" % i, [128, 8, 128], BF16) for i in range(2)]
        Vap = [aalloc("Vap%d" % i, [128, 16, 65], BF16) for i in range(2)]
        ptp = [aalloc("ptp%d" % i, [128, 16, 32], BF16) for i in range(2)]
        stmp = aalloc("stmp", [128, 16, 32])
        smask = aalloc("smask", [128, 4, 32], BF16)
        bd32 = aalloc("bd32", [32, 32])
        ptb = aalloc("ptb", [128, 64], I32)
        ptf = aalloc("ptf", [128, 64])
        idx = aalloc("idx", [128, 64], I32)
        vnew = aalloc("vnew", [32, 16, 65], BF16)
        nfn = aalloc("nfn", [32, 16])
        ptn = aalloc("ptn", [32, 16, 32], BF16)
        rec8 = aalloc("rec8", [32, 16])
        ot8 = aalloc("ot8", [32, 1024])
        lf32b = aalloc("lf32b", [32, 16], BF16)
        qzs = aalloc("qzs", [128, 16, 32], BF16)
        assert ast["off"] <= C["off_kts"], ("sample arena overflow", ast["off"], C["off_kts"])
        for v_ in Vap:
            memset("dve", v_[:, :, :], 1.0, [v_.t])
        memset("dve", vnew[:, :, :], 1.0, [vnew.t])
        cp("dve", lf32b[:, :], lf[0:32, 16, :], [lf.t], [lf32b.t])
        memset("dve", qzs[:, :, :], 0.0, [qzs.t])
        qz4 = qzs[:, :, :].rearrange("p (a two) q -> p a two q", two=2)
        cp("dve", qz4[0:64, :, 0, :], qTs[0:64, :, :], [qTs.t], [qzs.t])
        cp("dve", qz4[64:128, :, 1, :], qTs[64:128, :, :], [qTs.t], [qzs.t])
        dma("sp", stmp[:, 0:4, :], I["c_smask"].partition_broadcast(128), (), [stmp.t], stmp.t, slow=True)
        cp("dve", smask[:, :, :], stmp[:, 0:4, :], [stmp.t], [smask.t])
        dma("sp", bd32[:, :], I["c_bd32"][:, :], (), [bd32.t], bd32.t)
        PSROT[0] = 2
        psets = [[PS[2], PS[3], PS[4]], [PS[5], PS[6], PS[7]]]
        acc8 = xin[1]
        cnt = [0]
        first = [True]

        def accumulate(Aset):
            for bk in range(3):
                ncol = 455 if bk < 2 else 130
                if first[0]:
                    cp("dve", acc8[0:32, bk * 455:bk * 455 + ncol], Aset[bk][0:32, 0:ncol], [Aset[bk].t], [acc8.t])
                else:
                    tt("dve", acc8[0:32, bk * 455:bk * 455 + ncol], acc8[0:32, bk * 455:bk * 455 + ncol], Aset[bk][0:32, 0:ncol], ALU.add,
                       [Aset[bk].t, acc8.t], [acc8.t])
            first[0] = False

        for s_ in range(4):
            dma("sp", ptb[:, :], I["pt"][s_].partition_broadcast(128), (), [ptb.t], ptb.t, slow=True)
            cp("dve", ptf[:, :], ptb[:, :], [ptb.t], [ptf.t])
            ts("dve", ptf[:, :], ptf[:, :], 128.0, iota_f[:, 0:1], ALU.mult, ALU.add, [ptf.t, iota_f.t], [ptf.t])
            cp("dve", idx[:, :], ptf[:, :], [ptf.t], [idx.t])
            for pg in range(64):
                P.add("pool", lambda e, o=lfp[:, :, :].rearrange("p a b -> p (a b)")[:, pg * 16:(pg + 1) * 16], ix=idx[:, pg:pg + 1]: e.indirect_dma_start(
                    out=o, out_offset=None, in_=I["clf"][:, :], in_offset=bass.IndirectOffsetOnAxis(ap=ix, axis=0)),
                    [idx.t], [lfp.t], dma_tok=lfp.t)
            lfp2 = lfp[:, :, :].rearrange("p a b -> p (a b)")
            tA2 = totA[:, :, :].rearrange("p a b -> p (a b)")
            cp("dve", lph[:, :], lfp2, [lfp.t], [lph.t])
            cp("dve", tA2, lph[:, :], [lph.t], [totA.t])
            tt("dve", tA2, lfp2, tA2, ALU.subtract, [lfp.t, totA.t], [totA.t])
            cp("dve", lpl[:, :], tA2, [totA.t], [lpl.t])
            for hf in range(2):
                p_ = ps()
                mm(p_[:, :], ones_b[:, :], lph[:, hf * 512:(hf + 1) * 512], True, False, [ones_b.t, lph.t], [p_.t])
                mm(p_[:, :], ones_b[:, :], lpl[:, hf * 512:(hf + 1) * 512], False, True, [ones_b.t, lpl.t], [p_.t])
                evac(totA[:, hf * 32:(hf + 1) * 32, :], p_[:, :].rearrange("p (a b) -> p a b", a=32), [p_.t], [totA.t])
            src, dst = totA, totB
            k_ = 1
            while k_ < 64:
                cp("act", dst[:, 64 - k_:64, :], src[:, 64 - k_:64, :], [src.t], [dst.t])
                tt("dve", dst[:, 0:64 - k_, :], src[:, 0:64 - k_, :], src[:, k_:64, :], ALU.add, [src.t], [dst.t])
                src, dst = dst, src
                k_ *= 2
            incl = src
            assert incl is totA
            for hf in range(2):
                p_ = ps()
                mm(p_[:, :], stril_b[:, :], lph[:, hf * 512:(hf + 1) * 512], True, False, [stril_b.t, lph.t], [p_.t])
                mm(p_[:, :], stril_b[:, :], lpl[:, hf * 512:(hf + 1) * 512], False, True, [stril_b.t, lpl.t], [p_.t])
                p3 = p_[:, :].rearrange("p (a b) -> p a b", a=32)
                if hf == 0:
                    tt("dve", lfp[:, 0:32, :], p3, incl[:, 1:33, :], ALU.add, [p_.t, incl.t], [lfp.t])
                else:
                    tt("dve", lfp[:, 32:63, :], p3[:, 0:31, :], incl[:, 33:64, :], ALU.add, [p_.t, incl.t], [lfp.t])
                    cp("dve", lfp[:, 63:64, :], p3[:, 31:32, :], [p_.t], [lfp.t])
            rest = lfp
            SLV = globals().get("SLEVEL", "Z")
            for pg in range(64 if SLV != "P1" else 0):
                ci = cnt[0]
                cnt[0] += 1
                ks, vs, kt, va, pp = kst[ci % 2], vst[ci % 2], KTp[ci % 2], Vap[ci % 2], ptp[ci % 2]
                P.add("pool", lambda e, o=ks[:, :], ix=idx[:, pg:pg + 1]: e.indirect_dma_start(
                    out=o, out_offset=None, in_=I["ck"][:, :], in_offset=bass.IndirectOffsetOnAxis(ap=ix, axis=0)),
                    [idx.t], [ks.t], dma_tok=ks.t)
                P.add("pool", lambda e, o=vs[:, :], ix=idx[:, pg:pg + 1]: e.indirect_dma_start(
                    out=o, out_offset=None, in_=I["cv"][:, :], in_offset=bass.IndirectOffsetOnAxis(ap=ix, axis=0)),
                    [idx.t], [vs.t], dma_tok=vs.t)
                if SLV == "P11":
                    continue
                for half in range(2):
                    p_ = ps()
                    for jj in range(4):
                        hp = half * 4 + jj
                        tr(p_[:, jj * 128:(jj + 1) * 128], ks[:, hp * 128:(hp + 1) * 128], ident[:, :], [ks.t, ident.t], [p_.t])
                    evac(kt[:, half * 4:half * 4 + 4, :], p_[:, :].rearrange("p (j n) -> p j n", j=4), [p_.t], [kt.t])
                if SLV == "P12":
                    continue
                cp("dve" if pg % 2 else "act", va[:, :, 0:64], vs[:, :].rearrange("p (h d) -> p h d", h=16), [vs.t], [va.t])
                if SLV == "P13":
                    continue
                pS = ps()
                for hh in range(16):
                    hp, lo = hh // 2, (hh % 2) * 64
                    mm(pS[:, hh * 32:(hh + 1) * 32], kt[:, hp, :], qzs[:, hh, :], True, True, [kt.t, qzs.t], [pS.t])
                if SLV == "P15":
                    continue
                tt("dve", stmp[:, :, :], pS[:, :].rearrange("p (h q) -> p h q", h=16),
                   rest[:, pg, :].unsqueeze(2).to_broadcast([128, 16, 32]), ALU.add, [pS.t, rest.t], [stmp.t])
                if SLV == "P17":
                    continue
                act(pp[:, :, :], stmp[:, :, :], AF.Exp, [stmp.t], [pp.t])
                if SLV == "P2":
                    continue
                tt("dve", pp[:, :, :], pp[:, :, :], smask[:, s_, :].unsqueeze(1).to_broadcast([128, 16, 32]), ALU.mult, [pp.t, smask.t], [pp.t])
                if SLV == "P25":
                    continue
                Aset = psets[ci % 2]
                for hh in range(16):
                    a_ = Aset[hh // 7]
                    col = (hh % 7) * 65
                    mm(a_[0:32, col:col + 65], pp[:, hh, :], va[:, hh, :], True, True, [pp.t, va.t], [a_.t])
                if SLV == "P3":
                    continue
                accumulate(Aset)
        if SLV in ("P1", "P11", "P12", "P13", "P15", "P17", "P2", "P25", "P3", "P4"):
            PSROT[0] = 7
            ast["off"] = save_off
            return
        dma("pool", vnew[:, :, 0:64], O["c_v_s"][:, :].rearrange("t (h d) -> t h d", h=16), [C["vtok"][16]], [vnew.t], vnew.t)
        if globals().get("SLEVEL", "Z") == "A":
            PSROT[0] = 7
            ast["off"] = save_off
            return
        pf_ = ps()
        mm(pf_[0:32, 0:16], msel[:, :], lf32b[:, :], True, True, [msel.t, lf32b.t], [pf_.t])
        ts("dve", nfn[:, :], pf_[0:32, 0:16], -1.0, None, ALU.mult, None, [pf_.t], [nfn.t])
        if globals().get("SLEVEL", "Z") == "B":
            PSROT[0] = 7
            ast["off"] = save_off
            return
        pS = ps()
        for hh in range(16):
            hp, lo = hh // 2, (hh % 2) * 64
            mm(pS[0:32, hh * 32:(hh + 1) * 32], kTs[:, hp, :], qzs[:, hh, :], True, True, [kTs.t, qzs.t], [pS.t])
        if globals().get("SLEVEL", "Z") == "C":
            PSROT[0] = 7
            ast["off"] = save_off
            return
        tt("dve", stmp[0:32, :, :], pS[0:32, :].rearrange("p (h q) -> p h q", h=16), nfn[:, :].unsqueeze(2).to_broadcast([32, 16, 32]), ALU.add,
           [pS.t, nfn.t], [stmp.t])
        act(stmp[0:32, :, :], stmp[0:32, :, :], AF.Exp, [stmp.t], [stmp.t])
        tt("dve", ptn[:, :, :], stmp[0:32, :, :], bd32[:, :].unsqueeze(1).to_broadcast([32, 16, 32]), ALU.mult, [stmp.t, bd32.t], [ptn.t])
        if globals().get("SLEVEL", "Z") == "D":
            PSROT[0] = 7
            ast["off"] = save_off
            return
        Aset = psets[0]
        for hh in range(16):
            a_ = Aset[hh // 7]
            col = (hh % 7) * 65
            mm(a_[0:32, col:col + 65], ptn[:, hh, :], vnew[:, hh, :], True, True, [ptn.t, vnew.t], [a_.t])
        accumulate(Aset)
        if globals().get("SLEVEL", "Z") == "E":
            PSROT[0] = 7
            ast["off"] = save_off
            return
        a3 = acc8[0:32, 0:1040].rearrange("p (h d) -> p h d", h=16)
        P.add("dve", lambda e, o=rec8[:, :].unsqueeze(2), i_=a3[:, :, 64:65]: e.reciprocal(o, i_), [acc8.t], [rec8.t])
        tt("dve", ot8[:, :].rearrange("p (h d) -> p h d", h=16), a3[:, :, 0:64], rec8[:, :].unsqueeze(2).to_broadcast([32, 16, 64]), ALU.mult,
           [acc8.t, rec8.t], [ot8.t])
        p_ = ps()
        for hp in range(8):
            tr(p_[:, hp * 32:(hp + 1) * 32], ot8[:, hp * 128:(hp + 1) * 128], ident[0:32, 0:32], [ot8.t, ident.t], [p_.t])
        tt("dve", moT[:, 0:8, off:off + 32], p_[:, 0:256].rearrange("p (j n) -> p j n", j=8), ogss[:, :, :], ALU.mult, [p_.t, ogss.t], [moT.t])
        PSROT[0] = 7
        ast["off"] = save_off

    for l in layers:
        if l % 4 == 2 and 2 in mixers:
            setup_c()
        if l % 4 == 1 and 1 in mixers:
            setup_b(l)
        if l % 4 == 3 and 3 in mixers:
            setup_d()
        if l % 4 == 0 and 0 in mixers:
            setup_a()
        prep_sample_mem(l)
        for sbk in SUPER:
            for (b, t0, n, off) in seg_list(sbk):
                rmsnorm_fm(hT, t0, n, gv["norm_mix"][:, l * 8:(l + 1) * 8], xn, [hTt[b]], [xn.t], doff=off)
            xq_and_attend(l, sbk)
            if l % 4 == 3 and 3 in mixers:
                mixer_d(sbk)
            if l % 4 == 0 and 0 in mixers:
                mixer_a(sbk)
            if l % 4 == 1 and 1 in mixers:
                mixer_b(sbk)
            if l % 4 == 2 and 2 in mixers:
                mixer_c(sbk)
            out_proj(l, sbk)
            mlp(l, sbk)
        if l % 4 == 0 and 0 in mixers:
            finish_a()
        if l % 4 == 1 and 1 in mixers:
            finish_b()

    yst = xin

    def store_tokens(src, t0, nrows, dst_ap, rd_tok, otk, i):
        y = yst[i % 2]
        for half in range(2):
            p_ = ps()
            for j in range(4):
                c = half * 4 + j
                tr(p_[0:nrows, j * 128:(j + 1) * 128], src[:, c, t0:t0 + nrows], ident[:, :], [rd_tok, ident.t], [p_.t])
            evac(y[0:nrows, half * 512:(half + 1) * 512], p_[0:nrows, :], [p_.t], [y.t])
        dma("sp", dst_ap, y[0:nrows, 0:D], [y.t], [otk], y.t)

    for i in range(16):
        store_tokens(hT, i * 128, 128, O["y_p"][i * 128:(i + 1) * 128, :], hTt[i // 4], otok["y_p"], i)
    store_tokens(hT, TP, 32, O["y_s"][:, :], hTt[4], otok["y_s"], 16)

    return nc, P, es, locals()


def host_consts():
    i = np.arange(128)
    c = {}
    c["c_ident"] = np.eye(128, dtype=np.float32)
    c["c_ones"] = np.ones((128, 128), np.float32)
    c["c_blk64"] = ((i[:, None] // 64) == (i[None, :] // 64)).astype(np.float32)
    c["c_triu"] = (i[:, None] <= i[None, :]).astype(np.float32)
    c["c_striu"] = (i[:, None] < i[None, :]).astype(np.float32)
    sel = np.zeros((8, 8, 128), np.float32)
    for h in range(8):
        sel[h, h, :] = 1.0
    c["c_sel"] = sel.reshape(8, 1024)
    sl64 = np.zeros((64, 64), np.float32); sl64[63, :] = 1.0
    sl8 = np.zeros((8, 8), np.float32); sl8[7, :] = 1.0
    c["c_sl64"] = sl64
    c["c_sl8"] = sl8
    c["c_stril"] = (i[:, None] > i[None, :]).astype(np.float32)
    msel = np.zeros((32, 4, 8), np.float32)
    for s_ in range(4):
        for t_ in range(8):
            msel[8 * s_:8 * s_ + t_ + 1, s_, t_] = 1.0
    c["c_msel"] = msel.reshape(32, 32)
    i32_ = np.arange(32)
    c["c_bd32"] = (((i32_[:, None] // 8) == (i32_[None, :] // 8)) & (i32_[:, None] <= i32_[None, :])).astype(np.float32)
    c["c_smask"] = ((i32_[None, :] // 8) == np.arange(4)[:, None]).astype(np.float32)
    i32 = np.arange(128, dtype=np.int32).reshape(128, 1)
    c["c_iota"] = i32
    return c


def make_in_maps(inp, npool=2560):
    f = lambda a: np.ascontiguousarray(np.asarray(a))
    shared = {
        "ck": f(inp["cache_c_k"][0][:npool]).reshape(npool * 128, 1024),
        "cv": f(inp["cache_c_v"][0][:npool]).reshape(npool * 128, 1024),
        "clf": f(inp["cache_c_logf"][0][:npool]).reshape(npool * 128, 16),
        "norm_mix": f(inp["norm_mix"]), "w_out": f(inp["w_out"]), "norm_mlp": f(inp["norm_mlp"]),
        "w_up": f(inp["w_up"]), "w_down": f(inp["w_down"]), "mem_norm": f(inp["mem_norm"]),
        "w_mem_kv": f(inp["w_mem_kv"]), "xa_qnorm": f(inp["xa_qnorm"]), "xa_knorm": f(inp["xa_knorm"]),
        "w_in_a": f(inp["w_in_a"][0]), "a_conv_w": f(inp["a_conv_w"][0]), "a_log": f(inp["a_log"]),
        "a_dt_bias": f(inp["a_dt_bias"]), "a_norm_w": f(inp["a_norm_w"]), "w_in_b": f(inp["w_in_b"][0]),
        "hg_lb": f(inp["hg_lb"]), "b_norm_w": f(inp["b_norm_w"]), "w_in_c": f(inp["w_in_c"][0]),
        "c_fbias": f(inp["c_fbias"]), "c_qnorm": f(inp["c_qnorm"]), "c_knorm": f(inp["c_knorm"]),
        "w_in_d": f(inp["w_in_d"][0]), "d_ln_g": f(inp["d_ln_g"]), "d_ln_b": f(inp["d_ln_b"]),
        "d_ws": f(inp["d_ws"][0]), "d_bs": f(inp["d_bs"][0]),
    }
    shared.update(host_consts())
    maps = []
    for c in range(NCORES):
        s = slice(4 * c, 4 * c + 4)
        m = dict(shared)
        m["xp"] = f(inp["x_prompt"][c])
        m["xs"] = f(inp["x_sample"][s]).reshape(TS, D)
        m["mem"] = f(inp["mem_prompt"][c])
        m["a_conv"] = f(inp["state_a_conv"][0, s])
        m["a_ssm"] = f(inp["state_a_ssm"][0, s])
        m["b_ssm"] = f(inp["state_b_ssm"][0, s])
        m["cmk"] = f(inp["cache_mem_k"][:, s]).reshape(4, 4, 256, 256)
        m["cmv"] = f(inp["cache_mem_v"][:, s]).reshape(4, 4, 256, 256)
        m["pt"] = f(inp["page_table"][s]).astype(np.int32)
        maps.append(m)
    return maps


def assemble(results):
    g = lambda nm: [np.asarray(r[nm]) for r in results]
    y_p = np.stack(g("y_p"))
    y_s = np.concatenate(g("y_s")).reshape(32, 8, D)
    a_conv_p = np.stack(g("a_conv_p"))[None]
    a_conv_s = np.concatenate(g("a_conv_s"))[None]
    a_ssm_p = np.stack(g("a_ssm_p"))[None]
    a_ssm_s = np.concatenate(g("a_ssm_s"))[None]
    b_ssm_p = np.stack(g("b_ssm_p"))[None]
    b_ssm_s = np.concatenate(g("b_ssm_s"))[None]
    c_k_p = np.stack(g("c_k_p")).reshape(1, 8, TP, 16, 64)
    c_v_p = np.stack(g("c_v_p")).reshape(1, 8, TP, 16, 64)
    c_lf_p = np.stack(g("c_lf_p")).reshape(1, 8, TP, 16)
    c_k_s = np.concatenate(g("c_k_s")).reshape(1, 32, 8, 16, 64)
    c_v_s = np.concatenate(g("c_v_s")).reshape(1, 32, 8, 16, 64)
    c_lf_s = np.concatenate(g("c_lf_s")).reshape(1, 32, 8, 16)
    d_v_s = np.concatenate(g("d_v_s")).reshape(1, 32, 8, D)
    mem_k_p = np.stack(g("mem_k_p"), axis=1).reshape(4, 8, 256, 4, 64)
    mem_v_p = np.stack(g("mem_v_p"), axis=1).reshape(4, 8, 256, 4, 64)
    outs = (y_p, y_s, a_conv_p, a_conv_s, a_ssm_p, a_ssm_s, b_ssm_p, b_ssm_s, c_k_p, c_v_p, c_lf_p,
            c_k_s, c_v_s, c_lf_s, d_v_s, mem_k_p, mem_v_p)
    return tuple(np.ascontiguousarray(o, dtype=np.float32) for o in outs)


_CACHE = {}


def get_program(npool=2560):
    if npool not in _CACHE:
        nc, P, es, _ = build(npool=npool)
        P.finalize()
        P.emit()
        _CACHE[npool] = nc
    return _CACHE[npool]


def kernel(**inputs):
    nc = get_program()
    in_maps = make_in_maps(inputs)
    res = run_bass_kernel_spmd(nc, in_maps, core_ids=list(range(NCORES)))
    return assemble(res.results)
```
